# Optimizing a Trainium2 kernel written in Bass

```python
import math
import jax, jax.numpy as jnp
from jax import lax
import numpy as np

D_MODEL = 1024
BATCH = 8
SEQ = 2048
DEPTH = 4

N_HEADS = 8
QK_NOPE = 128
QK_ROPE = 64
V_HEAD = 128
Q_LORA = 384
KV_LORA = 256
ATTN_WIDTH = N_HEADS * V_HEAD
ROPE_THETA = 10000.0
Q_BLOCK = 128
CONV_WIDTH = D_MODEL
CONV_K = 31
RMS_EPS = 1e-6
LN_EPS = 1e-5
N_MIXERS = 2
N_MLA = (DEPTH + 1) // 2
N_CONV = DEPTH // 2
MLA_IN = Q_LORA + KV_LORA + QK_ROPE + ATTN_WIDTH

kernel_name = "hybrid_mla_conformer_conv_gated"


def rmsnorm(x, g):
    xf = x.astype(jnp.float32)
    y = xf * lax.rsqrt(jnp.mean(xf * xf, axis=-1, keepdims=True) + RMS_EPS)
    return (y * g.astype(jnp.float32)).astype(x.dtype)


def layernorm(x, g, b):
    xf = x.astype(jnp.float32)
    mu = jnp.mean(xf, axis=-1, keepdims=True)
    var = jnp.mean(jnp.square(xf - mu), axis=-1, keepdims=True)
    y = (xf - mu) * lax.rsqrt(var + LN_EPS)
    return (y * g.astype(jnp.float32) + b.astype(jnp.float32)).astype(x.dtype)


def rope_tables(positions):
    inv_freq = ROPE_THETA ** (-jnp.arange(0, QK_ROPE, 2, dtype=jnp.float32) / QK_ROPE)
    ang = positions.astype(jnp.float32)[..., None] * inv_freq
    return jnp.cos(ang), jnp.sin(ang)


def apply_rope(x, cos, sin):
    xf = x.astype(jnp.float32)
    x1, x2 = jnp.split(xf, 2, axis=-1)
    out = jnp.concatenate([x1 * cos - x2 * sin, x2 * cos + x1 * sin], axis=-1)
    return out.astype(x.dtype)


def causal_block_attention(q, k, v):
    B, S, H, D = q.shape
    nb = S // Q_BLOCK
    scale = 1.0 / math.sqrt(D)
    kf = k.astype(jnp.float32)
    kpos = jnp.arange(S)
    qb = q.reshape(B, nb, Q_BLOCK, H, D).transpose(1, 0, 2, 3, 4)

    def one_block(args):
        qi, i = args
        qpos = i * Q_BLOCK + jnp.arange(Q_BLOCK)
        s = jnp.einsum('bqhd,bkhd->bhqk', qi.astype(jnp.float32), kf) * scale
        mask = kpos[None, :] <= qpos[:, None]
        s = jnp.where(mask[None, None], s, -jnp.inf)
        p = jax.nn.softmax(s, axis=-1).astype(v.dtype)
        return jnp.einsum('bhqk,bkhd->bqhd', p, v)

    out = lax.map(one_block, (qb, jnp.arange(nb)))
    return out.transpose(1, 0, 2, 3, 4).reshape(B, S, H, v.shape[-1])


def mla_mixer(h, cos, sin, w_in, q_norm_g, w_qb, kv_norm_g, w_kvb, w_out):
    B, S, _ = h.shape
    z = h @ w_in
    q_lat, kv_lat, k_pe, gate = jnp.split(
        z, [Q_LORA, Q_LORA + KV_LORA, Q_LORA + KV_LORA + QK_ROPE], axis=-1)
    q = (rmsnorm(q_lat, q_norm_g) @ w_qb).reshape(B, S, N_HEADS, QK_NOPE + QK_ROPE)
    q_nope, q_pe = jnp.split(q, [QK_NOPE], axis=-1)
    kv = (rmsnorm(kv_lat, kv_norm_g) @ w_kvb).reshape(B, S, N_HEADS, QK_NOPE + V_HEAD)
    k_nope, v = jnp.split(kv, [QK_NOPE], axis=-1)
    q_pe = apply_rope(q_pe, cos[:, :, None, :], sin[:, :, None, :])
    k_pe = apply_rope(k_pe, cos, sin)
    q = jnp.concatenate([q_nope, q_pe], axis=-1)
    k = jnp.concatenate(
        [k_nope, jnp.broadcast_to(k_pe[:, :, None, :], (B, S, N_HEADS, QK_ROPE))], axis=-1)
    o = causal_block_attention(q, k, v).reshape(B, S, ATTN_WIDTH)
    return (o * jax.nn.silu(gate)) @ w_out


def conv_mixer(h, w_in, b_in, dw_w, dw_b, ln_g, ln_b, w_out, b_out):
    z = h @ w_in + b_in
    a, b, gate = jnp.split(z, 3, axis=-1)
    u = a * jax.nn.sigmoid(b)
    u = lax.conv_general_dilated(
        u, dw_w[:, None, :], window_strides=(1,), padding=[(CONV_K - 1, 0)],
        dimension_numbers=('NWC', 'WIO', 'NWC'),
        feature_group_count=CONV_WIDTH) + dw_b
    u = jax.nn.silu(layernorm(u, ln_g, ln_b))
    return (u * jax.nn.silu(gate)) @ w_out + b_out


def setup_inputs(seed: int = 0) -> dict:
    key = jax.random.key(seed)
    ks = jax.random.split(key, 24)
    f32 = jnp.float32

    def w(k, shape, fan_in):
        return jax.random.normal(k, shape, f32) * (fan_in ** -0.5)

    def gain(k, shape):
        return 1.0 + 0.02 * jax.random.normal(k, shape, f32)

    def bias(k, shape):
        return 0.02 * jax.random.normal(k, shape, f32)

    x = jax.random.normal(ks[0], (BATCH, SEQ, D_MODEL), f32)
    offset = jax.random.randint(ks[1], (BATCH, 1), 0, 1024, dtype=jnp.int32)
    positions = offset + jnp.arange(SEQ, dtype=jnp.int32)[None, :]
    return {
        "x": x,
        "positions": positions,
        "final_norm_g": gain(ks[2], (D_MODEL,)),
        "mla_norm_g": gain(ks[3], (N_MLA, D_MODEL)),
        "mla_w_in": w(ks[4], (N_MLA, D_MODEL, MLA_IN), D_MODEL),
        "mla_q_norm_g": gain(ks[5], (N_MLA, Q_LORA)),
        "mla_w_qb": w(ks[6], (N_MLA, Q_LORA, N_HEADS * (QK_NOPE + QK_ROPE)), Q_LORA),
        "mla_kv_norm_g": gain(ks[7], (N_MLA, KV_LORA)),
        "mla_w_kvb": w(ks[8], (N_MLA, KV_LORA, N_HEADS * (QK_NOPE + V_HEAD)), KV_LORA),
        "mla_w_out": w(ks[9], (N_MLA, ATTN_WIDTH, D_MODEL), ATTN_WIDTH),
        "conv_norm_g": gain(ks[10], (N_CONV, D_MODEL)),
        "conv_w_in": w(ks[11], (N_CONV, D_MODEL, 3 * CONV_WIDTH), D_MODEL),
        "conv_b_in": bias(ks[12], (N_CONV, 3 * CONV_WIDTH)),
        "conv_dw_w": w(ks[13], (N_CONV, CONV_K, CONV_WIDTH), CONV_K),
        "conv_dw_b": bias(ks[14], (N_CONV, CONV_WIDTH)),
        "conv_ln_g": gain(ks[15], (N_CONV, CONV_WIDTH)),
        "conv_ln_b": bias(ks[16], (N_CONV, CONV_WIDTH)),
        "conv_w_out": w(ks[17], (N_CONV, CONV_WIDTH, D_MODEL), CONV_WIDTH),
        "conv_b_out": bias(ks[18], (N_CONV, D_MODEL)),
    }


def reference(x, positions, final_norm_g,
              mla_norm_g, mla_w_in, mla_q_norm_g, mla_w_qb, mla_kv_norm_g, mla_w_kvb, mla_w_out,
              conv_norm_g, conv_w_in, conv_b_in, conv_dw_w, conv_dw_b, conv_ln_g, conv_ln_b,
              conv_w_out, conv_b_out):
    cos, sin = rope_tables(positions)
    for i in range(DEPTH):
        j = i // N_MIXERS
        if i % N_MIXERS == 0:
            h = rmsnorm(x, mla_norm_g[j])
            x = x + mla_mixer(h, cos, sin, mla_w_in[j], mla_q_norm_g[j], mla_w_qb[j],
                              mla_kv_norm_g[j], mla_w_kvb[j], mla_w_out[j])
        else:
            h = rmsnorm(x, conv_norm_g[j])
            x = x + conv_mixer(h, conv_w_in[j], conv_b_in[j], conv_dw_w[j], conv_dw_b[j],
                               conv_ln_g[j], conv_ln_b[j], conv_w_out[j], conv_b_out[j])
    return rmsnorm(x, final_norm_g)
```

```python
import math
from contextlib import ExitStack

import numpy as np
import concourse.bass as bass
import concourse.mybir as mybir
from concourse.bass_utils import run_bass_kernel_spmd

F32 = mybir.dt.float32
BF16 = mybir.dt.bfloat16
I32 = mybir.dt.int32
AF = mybir.ActivationFunctionType
ALU = mybir.AluOpType

S = 2048
D = 1024
NT = 4
TW = 512
NH = 8
SCALE = 1.0 / math.sqrt(192.0)
RMS_EPS = 1e-6
LN_EPS = 1e-5
NEG = -30000.0
TWO_PI = 2.0 * math.pi
C1 = 6.28125
C2 = TWO_PI - C1
PI_LO = 3.1415925

V_FINAL = 0
V_INVF = 8
V_PHASE = 9
V_MLA = 10
MLA_VW = 13
V_CONV = V_MLA + 2 * MLA_VW
CONV_VW = 8 + 24 + 8 + 8 + 8 + 8 + 256
NV = V_CONV + 2 * CONV_VW


class T:
    __slots__ = ("name", "w", "r")

    def __init__(self, name):
        self.name = name
        self.w = None
        self.r = {}


class Sched:
    ENGS = ("pe", "act", "dve", "pool", "sp")

    def __init__(self, nc, st):
        self.nc, self.st = nc, st
        self.prog = {e: [] for e in self.ENGS}
        self.sem = {}
        self.cnt = {}
        self.seen = {e: {} for e in self.ENGS}
        self.uid = 0

    def getsem(self, key):
        if key not in self.sem:
            self.sem[key] = self.st.enter_context(self.nc.semaphore("s_" + key))
            self.cnt[key] = 0
        return self.sem[key]

    def _deps(self, eng, reads, writes, is_dma):
        deps = {}

        def add(d, keep_same):
            if d is None:
                return
            k, v = d
            if k == eng and not keep_same:
                return
            if deps.get(k, 0) < v:
                deps[k] = v

        for t in reads:
            add(t.w, is_dma or eng != "pe")
        for t in writes:
            add(t.w, is_dma)
            for k, v in t.r.items():
                add((k, v), is_dma)
        for k, v in deps.items():
            if self.seen[eng].get(k, 0) < v:
                self.prog[eng].append(("wait", k, v))
                self.seen[eng][k] = v

    def _mark(self, d, reads, writes):
        k, v = d
        for t in reads:
            if t.r.get(k, 0) < v:
                t.r[k] = v
        for t in writes:
            t.w = d
            t.r = {}

    def op(self, eng, fn, reads=(), writes=(), sig=True):
        self._deps(eng, reads, writes, False)
        self.getsem(eng)
        val = self.cnt[eng] + 1
        if sig:
            self.cnt[eng] = val
        self.prog[eng].append(("op", fn, eng if sig else None, 1))
        self._mark((eng, val), reads, writes)

    def dma(self, q, out_ap, in_ap, reads=(), writes=(), key=None):
        if key is None:
            key = "d%d" % self.uid
            self.uid += 1
        self._deps(q, reads, writes, True)
        self.getsem(key)
        self.cnt[key] += 16
        self.prog[q].append(("op", lambda e: e.dma_start(out=out_ap, in_=in_ap), key, 16))
        d = (key, self.cnt[key])
        self._mark(d, reads, writes)
        return d

    def wait_all(self, eng, key):
        v = self.cnt[key]
        if self.seen[eng].get(key, 0) < v:
            self.prog[eng].append(("wait", key, v))
            self.seen[eng][key] = v

    def emit(self):
        blk = self.st.enter_context(self.nc.Block())

        def body(name):
            def run(e):
                for it in self.prog[name]:
                    if it[0] == "wait":
                        e.wait_ge(self.sem[it[1]], it[2])
                    else:
                        ins = it[1](e)
                        if it[2] is not None:
                            ins.then_inc(self.sem[it[2]], it[3])
            return run

        blk.tensor(body("pe"))
        blk.scalar(body("act"))
        blk.vector(body("dve"))
        blk.gpsimd(body("pool"))
        blk.sync(body("sp"))


def handoff(old, new):
    u = {}
    for t in old:
        if t.w is not None:
            k, v = t.w
            if u.get(k, 0) < v:
                u[k] = v
        for k, v in t.r.items():
            if u.get(k, 0) < v:
                u[k] = v
    for t in new:
        t.w = None
        t.r = dict(u)


def build_program(layer_kinds, final_norm=True):
    nc = bass.Bass("TRN2", target_bir_lowering=False)
    n_layers = len(layer_kinds)
    dram = {}
    dram["xT"] = nc.dram_tensor("xT", [D, S], F32, kind="ExternalInput").ap()
    dram["pos"] = nc.dram_tensor("pos", [1, S], I32, kind="ExternalInput").ap()
    dram["vecs"] = nc.dram_tensor("vecs", [128, NV], F32, kind="ExternalInput").ap()
    dram["ident"] = nc.dram_tensor("ident", [128, 128], F32, kind="ExternalInput").ap()
    dram["maskT"] = nc.dram_tensor("maskT", [128, 128], F32, kind="ExternalInput").ap()
    dram["i4"] = nc.dram_tensor("i4", [128, 32], F32, kind="ExternalInput").ap()
    n_mla = sum(1 for k in layer_kinds if k == "mla")
    n_conv = n_layers - n_mla
    for j in range(n_mla):
        dram["mwin%d" % j] = nc.dram_tensor("mwin%d" % j, [D, 1792], F32, kind="ExternalInput").ap()
        dram["mwqb%d" % j] = nc.dram_tensor("mwqb%d" % j, [384, 2048], F32, kind="ExternalInput").ap()
        dram["mwkvb%d" % j] = nc.dram_tensor("mwkvb%d" % j, [256, 2048], F32, kind="ExternalInput").ap()
        dram["mwout%d" % j] = nc.dram_tensor("mwout%d" % j, [D, D], F32, kind="ExternalInput").ap()
    for j in range(n_conv):
        dram["cwin%d" % j] = nc.dram_tensor("cwin%d" % j, [D, 3072], F32, kind="ExternalInput").ap()
        dram["cwout%d" % j] = nc.dram_tensor("cwout%d" % j, [D, D], F32, kind="ExternalInput").ap()
    yT = nc.dram_tensor("yT", [D, S], F32, kind="ExternalOutput").ap()

    st = ExitStack()
    with st:
        def sb(name, shape, dt):
            return st.enter_context(nc.sbuf_tensor(name, shape, dt))

        X = sb("X", [128, 8, S], F32)
        R1 = sb("R1", [128, 16384], BF16)
        R2 = sb("R2", [128, 16384], BF16)
        R3 = sb("R3", [128, 16640], BF16)
        WH = sb("WH", [128, 2, 1280], BF16)
        WR = sb("WR", [128, 2, 8, 512], BF16)
        NF, NB = 6, 6
        Fp = [sb("F%d" % i, [128, TW], F32) for i in range(NF)]
        Bp = [sb("B%d" % i, [128, TW], BF16) for i in range(NB)]
        U4B = sb("U4B", [128, 2176], BF16)
        i4 = sb("i4b", [128, 32], BF16)
        ident = sb("identb", [128, 128], BF16)
        maskT = sb("maskb", [128, 128], BF16)
        ones = sb("onesb", [128, 128], BF16)
        vecs = sb("vecs_sb", [128, NV], F32)
        epsr = sb("epsr", [128, 1], F32)
        epsl = sb("epsl", [128, 1], F32)
        PSb = [st.enter_context(nc.psum_tensor("ps%d" % i, [128, TW], F32)) for i in range(8)]

        sch = Sched(nc, st)

        Xt = [[T("X%d_%d" % (c, t)) for t in range(NT)] for c in range(8)]
        PSt = [T("ps%d" % i) for i in range(8)]
        Ft = [T("F%d" % i) for i in range(NF)]
        Bt = [T("B%d" % i) for i in range(NB)]
        WRt = [T("WR0"), T("WR1")]
        WHt = [T("WH0"), T("WH1")]
        constT = T("const")
        vecT = T("vecs")
        region = {"R1": [], "R2": [], "R3": []}

        def take(rname, tiles):
            handoff(region[rname], tiles)
            region[rname] = list(tiles)

        fi = [0]
        bi = [0]

        fpool = [list(range(NF))]

        def nextF():
            pool = fpool[0]
            i = pool[fi[0] % len(pool)]
            fi[0] += 1
            return Fp[i], Ft[i]

        def nextB():
            i = bi[0] % NB
            bi[0] += 1
            return Bp[i], Bt[i]

        def mm(out, lhsT, rhs, start, stop, reads, writes, sig):
            sch.op("pe", lambda e: e.matmul(out, lhsT=lhsT, rhs=rhs, start=start, stop=stop),
                   reads, writes, sig)

        def act(func, out, in_, reads, writes, bias=None, scale=None):
            kw = {}
            if bias is not None:
                kw["bias"] = bias
            if scale is not None:
                kw["scale"] = scale
            sch.op("act", lambda e: e.activation(out=out, in_=in_, func=func, **kw), reads, writes)

        def tt(eng, out, in0, in1, op, reads, writes):
            sch.op(eng, lambda e: e.tensor_tensor(out=out, in0=in0, in1=in1, op=op), reads, writes)

        def ts2(eng, out, in0, s1, s2, op0, op1, reads, writes):
            if s2 is None:
                sch.op(eng, lambda e: e.tensor_scalar(out=out, in0=in0, scalar1=s1, scalar2=None, op0=op0),
                       reads, writes)
            else:
                sch.op(eng, lambda e: e.tensor_scalar(out=out, in0=in0, scalar1=s1, scalar2=s2, op0=op0, op1=op1),
                       reads, writes)

        def stt(eng, out, in0, scalar, in1, op0, op1, reads, writes):
            sch.op(eng, lambda e: e.scalar_tensor_tensor(out=out, in0=in0, scalar=scalar, in1=in1, op0=op0, op1=op1),
                   reads, writes)

        def cp(eng, out, in_, reads, writes):
            sch.op(eng, lambda e: e.tensor_copy(out=out, in_=in_), reads, writes)

        def recip(out, in_, reads, writes):
            sch.op("dve", lambda e: e.reciprocal(out=out, in_=in_), reads, writes)

        def vcol(c):
            return vecs[:, c:c + 1]

        def act_rsqrt(out, in_, scale, eps_ap, reads, wt):
            act(AF.Ln, out, in_, list(reads) + [epsT], [wt], bias=eps_ap, scale=scale)
            act(AF.Exp, out, out, [wt], [wt], scale=-0.5)

        def act_recip(out, in_, reads, wt):
            act(AF.Ln, out, in_, list(reads), [wt])
            act(AF.Exp, out, out, [wt], [wt], scale=-1.0)

        witems = []
        mi = ci = 0
        for kind in layer_kinds:
            if kind == "mla":
                w = dram["mwin%d" % mi]
                witems += [(w[:, 0:512], 512), (w[:, 512:1024], 512), (w[:, 1024:1536], 512), (w[:, 1536:1792], 256)]
                w = dram["mwout%d" % mi]
                witems += [(w[:, 0:512], 512), (w[:, 512:1024], 512)]
                mi += 1
            else:
                w = dram["cwin%d" % ci]
                witems += [(w[:, i * 512:(i + 1) * 512], 512) for i in range(6)]
                w = dram["cwout%d" % ci]
                witems += [(w[:, 0:512], 512), (w[:, 512:1024], 512)]
                ci += 1
        wstate = {"loaded": 0, "acq": 0, "rel": 0}

        def w_fill():
            while wstate["loaded"] < min(len(witems), wstate["rel"] + 2):
                i = wstate["loaded"]
                ap, ncols = witems[i]
                slot = i % 2
                sch.dma("pool", WR[:, slot, :, 0:ncols], ap.rearrange("(c p) n -> p c n", p=128),
                        writes=[WRt[slot]], key="wr%d" % slot)
                wstate["loaded"] += 1

        def w_acquire():
            i = wstate["acq"]
            assert i < wstate["loaded"], "weight item not loaded"
            wstate["acq"] += 1
            return i % 2

        def w_release():
            wstate["rel"] += 1
            w_fill()

        sch.dma("sp", vecs[:], dram["vecs"][:, :], writes=[vecT])
        first_pos = [True]
        posi0 = R1[:, 0:4096].bitcast(I32)
        pos0T = T("posi")
        sch.dma("sp", posi0, dram["pos"].partition_broadcast(128), writes=[pos0T])
        for t in range(NT):
            sch.dma("sp", X[:, :, t * TW:(t + 1) * TW],
                    dram["xT"][:, t * TW:(t + 1) * TW].rearrange("(c p) n -> p c n", p=128),
                    writes=[Xt[c][t] for c in range(8)])
        sch.dma("pool", ident[:], dram["ident"][:, :], writes=[constT], key="cst")
        sch.dma("pool", maskT[:], dram["maskT"][:, :], writes=[constT], key="cst")
        sch.dma("pool", i4[:], dram["i4"][:, :], writes=[constT], key="cst")
        onesT = T("ones")
        sch.op("dve", lambda e: e.memset(ones[:], 1.0), writes=[onesT])
        epsT = T("eps")
        sch.op("dve", lambda e: e.memset(epsr[:], RMS_EPS), writes=[epsT])
        sch.op("dve", lambda e: e.memset(epsl[:], LN_EPS), writes=[epsT])
        w_fill()

        def rmsnorm(gcol, dst_ap_fn, dst_t_fn, tiles=None, bank_=None):
            for t in (range(NT) if tiles is None else tiles):
                tsl = slice(t * TW, (t + 1) * TW)
                bank = (6 + (t % 2)) if bank_ is None else bank_
                for c in range(8):
                    b_ap, b_t = nextB()
                    act(AF.Square, b_ap[:], X[:, c, tsl], [Xt[c][t]], [b_t])
                    mm(PSb[bank][:], ones[:], b_ap[:], c == 0, c == 7, [b_t, onesT], [PSt[bank]], True)
                f_ap, f_t = nextF()
                act_rsqrt(f_ap[:], PSb[bank][:], 1.0 / D, epsr[:], [PSt[bank]], f_t)
                for c in range(8):
                    stt("dve", dst_ap_fn(c, tsl), X[:, c, tsl], vcol(gcol + c), f_ap[:], ALU.mult, ALU.mult,
                        [Xt[c][t], f_t, vecT], [dst_t_fn(c, t)])

        H = R1[:, :].rearrange("p (c n) -> p c n", c=8)
        G = R2[:, :].rearrange("p (c n) -> p c n", c=8)

        def mla_layer(j, conv_next):
            vb = V_MLA + j * MLA_VW
            QN = R3[:, 0:6144].rearrange("p (c n) -> p c n", c=3)
            KVN = R3[:, 6144:10240].rearrange("p (c n) -> p c n", c=2)
            KPE = R3[:, 10240:12288]
            ROPE = R3[:, 12288:16384].bitcast(F32)
            LAT = [QN[:, 0], QN[:, 1], QN[:, 2], KVN[:, 0], KVN[:, 1]]
            posi = R1[:, 0:4096].bitcast(I32)
            t1 = R1[:, 4096:8192].bitcast(F32)
            ki = R1[:, 8192:12288].bitcast(I32)
            kf = R1[:, 12288:16384].bitcast(F32)
            tmpT = [T("posi"), T("t1"), T("ki"), T("kf")]
            if first_pos[0]:
                tmpT[0] = pos0T
                region["R1"] = [pos0T]
            take("R1", tmpT[1:] if first_pos[0] else tmpT)
            if first_pos[0]:
                region["R1"] = list(tmpT)
            ropeT = T("rope")
            LATt = [[T("lat%d_%d" % (f, t)) for t in range(NT)] for f in range(5)]
            KPEt = [T("kpe%d" % t) for t in range(NT)]
            take("R3", [ropeT] + [x for r in LATt for x in r] + KPEt)
            if not first_pos[0]:
                sch.dma("sp", posi, dram["pos"].partition_broadcast(128), writes=[tmpT[0]])
            first_pos[0] = False
            cp("dve", t1, posi, [tmpT[0]], [tmpT[1]])
            ts2("dve", t1, t1, vcol(V_INVF), vcol(V_PHASE), ALU.mult, ALU.add, [tmpT[1], vecT], [tmpT[1]])
            ts2("dve", kf, t1, 1.0 / TWO_PI, None, ALU.mult, None, [tmpT[1]], [tmpT[3]])
            cp("dve", ki, kf, [tmpT[3]], [tmpT[2]])
            cp("dve", kf, ki, [tmpT[2]], [tmpT[3]])
            stt("dve", t1, kf, -C1, t1, ALU.mult, ALU.add, [tmpT[3], tmpT[1]], [tmpT[1]])
            stt("dve", t1, kf, -C2, t1, ALU.mult, ALU.add, [tmpT[3], tmpT[1]], [tmpT[1]])
            ts2("dve", t1, t1, PI_LO, -PI_LO, ALU.min, ALU.max, [tmpT[1]], [tmpT[1]])
            act(AF.Sin, ROPE, t1, [tmpT[1]], [ropeT])

            Ht = [[T("h%d_%d" % (c, t)) for t in range(NT)] for c in range(8)]
            take("R1", [x for r in Ht for x in r])
            rmsnorm(vb, lambda c, tsl: H[:, c, tsl], lambda c, t: Ht[c][t])

            RAWv = R2[:, 0:10240].bitcast(F32)
            RAWt = [[T("raw%d_%d" % (s_, f)) for f in range(5)] for s_ in range(2)]
            take("R2", [x for r in RAWt for x in r])
            s0 = w_acquire()
            s1 = w_acquire()
            wsl = [(s0, 0), (s0, 128), (s0, 256), (s0, 384), (s1, 0), (s1, 128)]
            pcnt = [0]

            def proj_bank():
                b = pcnt[0] % 4
                pcnt[0] += 1
                return b

            def rope_apply(ps_ap, ps_t, out_ap, out_t, tsl):
                a_ap, a_t = nextF()
                tt("dve", a_ap[:], ps_ap, ROPE[:, tsl], ALU.mult, [ps_t, ropeT], [a_t])
                b_ap, b_t = nextF()
                cp("dve", b_ap[0:64, :], a_ap[64:128, :], [a_t], [b_t])
                tt("dve", out_ap, a_ap[0:64, :], b_ap[0:64, :], ALU.add, [a_t, b_t], [out_t])

            for t in range(NT):
                tsl = slice(t * TW, (t + 1) * TW)
                rs = t % 2
                for f in range(6):
                    slot, off = wsl[f]
                    b = proj_bank()
                    for kc in range(8):
                        mm(PSb[b][:], WR[:, slot, kc, off:off + 128], H[:, kc, tsl], kc == 0, kc == 7,
                           [WRt[slot], Ht[kc][t]], [PSt[b]], kc == 7)
                    if f < 5:
                        raw = RAWv[:, (rs * 5 + f) * TW:(rs * 5 + f + 1) * TW]
                        act(AF.Copy, raw, PSb[b][:], [PSt[b]], [RAWt[rs][f]])
                        q_ap, q_t = nextB()
                        act(AF.Square, q_ap[:], PSb[b][:], [PSt[b]], [q_t])
                        sbank = 6 if f < 3 else 7
                        first = f in (0, 3)
                        last = f in (2, 4)
                        mm(PSb[sbank][:], ones[:], q_ap[:], first, last, [q_t, onesT], [PSt[sbank]], True)
                    else:
                        rope_apply(PSb[b][:], PSt[b], KPE[0:64, tsl], KPEt[t], tsl)
                rq_ap, rq_t = nextF()
                act_rsqrt(rq_ap[:], PSb[6][:], 1.0 / 384, epsr[:], [PSt[6]], rq_t)
                rk_ap, rk_t = nextF()
                act_rsqrt(rk_ap[:], PSb[7][:], 1.0 / 256, epsr[:], [PSt[7]], rk_t)
                for f in range(5):
                    raw = RAWv[:, (rs * 5 + f) * TW:(rs * 5 + f + 1) * TW]
                    r_ap, r_t = (rq_ap, rq_t) if f < 3 else (rk_ap, rk_t)
                    stt("dve", LAT[f][:, tsl], raw, vcol(vb + 8 + f), r_ap[:], ALU.mult, ALU.mult,
                        [RAWt[rs][f], r_t, vecT], [LATt[f][t]])

            Gt = [[T("g%d_%d" % (c, t)) for t in range(NT)] for c in range(8)]
            take("R2", [x for r in Gt for x in r])
            cur = s1
            held = [s0, s1]
            for gi in range(8):
                if gi == 0:
                    slot, off = s1, 256
                elif gi == 1:
                    slot, off = s1, 384
                elif gi < 6:
                    if gi == 2:
                        w_release()
                        s2 = w_acquire()
                    slot, off = s2, (gi - 2) * 128
                else:
                    if gi == 6:
                        w_release()
                        s3 = w_acquire()
                    slot, off = s3, (gi - 6) * 128
                for t in range(NT):
                    tsl = slice(t * TW, (t + 1) * TW)
                    b = proj_bank()
                    for kc in range(8):
                        mm(PSb[b][:], WR[:, slot, kc, off:off + 128], H[:, kc, tsl], kc == 0, kc == 7,
                           [WRt[slot], Ht[kc][t]], [PSt[b]], kc == 7)
                    act(AF.Silu, G[:, gi, tsl], PSb[b][:], [PSt[b]], [Gt[gi][t]])
                if gi == 5:
                    w_release()
            w_release()

            HB = []
            for s_ in range(2):
                base = s_ * 8192
                HB.append(dict(
                    qn=R1[:, base:base + 2048], qp=R1[:, base + 2048:base + 4096],
                    kh=R1[:, base + 4096:base + 6144],
                    vh=R1[:, base + 6144:base + 8192].rearrange("p (b n) -> p b n", n=128),
                    qnt=[T("qn%d_%d" % (s_, t)) for t in range(NT)],
                    qpt=[T("qp%d_%d" % (s_, t)) for t in range(NT)],
                    kht=[T("kh%d_%d" % (s_, t)) for t in range(NT)],
                    vht=[T("vh%d_%d" % (s_, t)) for t in range(NT)],
                ))
            take("R1", [x for hb in HB for key in ("qnt", "qpt", "kht", "vht") for x in hb[key]])
            wq_d = dram["mwqb%d" % j]
            wkv_d = dram["mwkvb%d" % j]

            def load_head_w(h):
                s_ = h % 2
                sch.dma("pool", WH[:, s_, 0:768].rearrange("p (c n) -> p c n", c=3),
                        wq_d[:, h * 256:(h + 1) * 256].rearrange("(c p) n -> p c n", p=128),
                        writes=[WHt[s_]], key="wh%d" % s_)
                sch.dma("pool", WH[:, s_, 768:1280].rearrange("p (c n) -> p c n", c=2),
                        wkv_d[:, h * 256:(h + 1) * 256].rearrange("(c p) n -> p c n", p=128),
                        writes=[WHt[s_]], key="wh%d" % s_)

            jb = [0]

            def jit_bank():
                return 7

            def proj_pieces(h):
                s_ = h % 2
                hb = HB[s_]
                wq = WH[:, s_, 0:768].rearrange("p (c n) -> p c n", c=3)
                wkv = WH[:, s_, 768:1280].rearrange("p (c n) -> p c n", c=2)
                pieces = []
                for t in range(NT):
                    tsl = slice(t * TW, (t + 1) * TW)

                    def g_qn(t=t, tsl=tsl):
                        b = jit_bank()
                        for kc in range(3):
                            mm(PSb[b][:], wq[:, kc, 0:128], QN[:, kc, tsl], kc == 0, kc == 2,
                               [WHt[s_], LATt[kc][t]], [PSt[b]], kc == 2)
                        cp("dve", hb["qn"][:, tsl], PSb[b][:], [PSt[b]], [hb["qnt"][t]])

                    def g_qp(t=t, tsl=tsl):
                        b = jit_bank()
                        for kc in range(3):
                            mm(PSb[b][:], wq[:, kc, 128:256], QN[:, kc, tsl], kc == 0, kc == 2,
                               [WHt[s_], LATt[kc][t]], [PSt[b]], kc == 2)
                        rope_apply(PSb[b][:], PSt[b], hb["qp"][0:64, tsl], hb["qpt"][t], tsl)

                    def g_k(t=t, tsl=tsl):
                        b = jit_bank()
                        for kc in range(2):
                            mm(PSb[b][:], wkv[:, kc, 0:128], KVN[:, kc, tsl], kc == 0, kc == 1,
                               [WHt[s_], LATt[3 + kc][t]], [PSt[b]], kc == 1)
                        cp("dve", hb["kh"][:, tsl], PSb[b][:], [PSt[b]], [hb["kht"][t]])

                    def g_v(t=t, tsl=tsl):
                        b = jit_bank()
                        for bb in range(4):
                            tb = 4 * t + bb
                            for kc in range(2):
                                mm(PSb[b][:, bb * 128:(bb + 1) * 128], KVN[:, kc, tb * 128:(tb + 1) * 128],
                                   wkv[:, kc, 128:256], kc == 0, kc == 1,
                                   [WHt[s_], LATt[3 + kc][t]], [PSt[b]], (bb == 3 and kc == 1))
                        cp("dve", hb["vh"][:, 4 * t:4 * t + 4, :],
                           PSb[b][:].rearrange("p (b n) -> p b n", n=128), [PSt[b]], [hb["vht"][t]])
                    pieces.append([g_qn, g_k, g_qp, g_v])
                return pieces

            def attention(h, extra, groups):
                s_ = h % 2
                hb = HB[s_]
                n_groups = max(len(groups), 1)
                blk_done = [0]
                for i in range(NT):
                    ob, sbk = 3 + 2 * (i % 2), 4 + 2 * (i % 2)
                    nblk = 4 * i + 4
                    tsl = slice(i * TW, (i + 1) * TW)

                    def c0_of(jk):
                        return max(jk - 4 * i, 0) * 128

                    def qk(jk):
                        bank = jk % 3
                        c0 = c0_of(jk)
                        diag = jk >= 4 * i
                        tq0 = i * TW + c0
                        tk = slice(jk * 128, (jk + 1) * 128)
                        mm(PSb[bank][:, c0:TW], hb["kh"][:, tk], hb["qn"][:, tq0:(i + 1) * TW], True, False,
                           [hb["kht"][jk // 4], hb["qnt"][i]], [PSt[bank]], False)
                        mm(PSb[bank][:, c0:TW], KPE[0:64, tk], hb["qp"][0:64, tq0:(i + 1) * TW], False, not diag,
                           [KPEt[jk // 4], hb["qpt"][i]], [PSt[bank]], not diag)
                        if diag:
                            mm(PSb[bank][:, c0:c0 + 128], ident[:], maskT[:], False, True,
                               [constT], [PSt[bank]], True)

                    def pv(jk):
                        bank = jk % 3
                        c0 = c0_of(jk)
                        p_ap, p_t = nextB()
                        act(AF.Exp, p_ap[:, c0:TW], PSb[bank][:, c0:TW], [PSt[bank]], [p_t], scale=SCALE)
                        mm(PSb[ob][:, c0:TW], hb["vh"][:, jk, :], p_ap[:, c0:TW], jk == 0, jk == nblk - 1,
                           [hb["vht"][jk // 4], p_t], [PSt[ob]], False)
                        mm(PSb[sbk][:, c0:TW], ones[:], p_ap[:, c0:TW], jk == 0, jk == nblk - 1,
                           [onesT, p_t], [PSt[sbk]], True)

                    qk(0)
                    if nblk > 1:
                        qk(1)
                    for jk in range(nblk):
                        if jk + 2 < nblk:
                            qk(jk + 2)
                        pv(jk)
                        blk_done[0] += 1
                        while groups and blk_done[0] * n_groups >= (n_groups - len(groups) + 1) * 40:
                            groups.pop(0)()
                    r_ap, r_t = nextF()
                    act_recip(r_ap[:], PSb[sbk][:], [PSt[sbk]], r_t)
                    u_ap, u_t = nextF()
                    tt("dve", u_ap[:], PSb[ob][:], r_ap[:], ALU.mult, [PSt[ob], r_t], [u_t])
                    tt("pool", G[:, h, tsl], u_ap[:], G[:, h, tsl], ALU.mult, [u_t, Gt[h][i]], [Gt[h][i]])
                    for fn in extra[i]:
                        fn()
                while groups:
                    groups.pop(0)()

            load_head_w(0)
            load_head_w(1)
            for p in proj_pieces(0):
                for g_ in p:
                    g_()
            for h in range(NH):
                extra = [[] for _ in range(NT)]
                groups = []
                if h + 1 < NH:
                    groups = [g_ for p in proj_pieces(h + 1) for g_ in p]
                attention(h, extra, groups)
                if h + 2 < NH:
                    load_head_w(h + 2)

            for half in range(2):
                slot = w_acquire()
                for mm_ in range(4):
                    m = half * 4 + mm_
                    for t in range(NT):
                        tsl = slice(t * TW, (t + 1) * TW)
                        b = proj_bank()
                        for kc in range(8):
                            mm(PSb[b][:], WR[:, slot, kc, mm_ * 128:(mm_ + 1) * 128], G[:, kc, tsl], kc == 0, kc == 7,
                               [WRt[slot], Gt[kc][t]], [PSt[b]], kc == 7)
                        tt("dve", X[:, m, tsl], PSb[b][:], X[:, m, tsl], ALU.add, [PSt[b], Xt[m][t]], [Xt[m][t]])
                w_release()

        def conv_layer(j, after_tile=None):
            vb = V_CONV + j * CONV_VW
            c_bin, c_dwb, c_lng, c_lnb, c_bout = vb + 8, vb + 32, vb + 40, vb + 48, vb + 56
            Ht = [[T("h%d_%d" % (c, t)) for t in range(NT)] for c in range(8)]
            take("R1", [x for r in Ht for x in r])
            rmsnorm(vb, lambda c, tsl: H[:, c, tsl], lambda c, t: Ht[c][t])
            U = R3[:, :].rearrange("p (c n) -> p c n", c=8)
            Upad = T("upad")
            Ut = [[T("u%d_%d" % (c, t)) for t in range(NT)] for c in range(8)]
            take("R3", [Upad] + [x for r in Ut for x in r])
            sch.op("dve", lambda e: e.memset(U[:, :, 0:32], 0.0), writes=[Upad])
            Gt = [[T("g%d_%d" % (c, t)) for t in range(NT)] for c in range(8)]
            take("R2", [x for r in Gt for x in r])
            pc = [0]
            for p in range(8):
                if p % 2 == 0:
                    slot = w_acquire()
                off = (p % 2) * 256
                for t in range(NT):
                    tsl = slice(t * TW, (t + 1) * TW)
                    ba = (2 * pc[0]) % 4
                    bb_ = ba + 1
                    pc[0] += 1
                    for kc in range(8):
                        mm(PSb[ba][:], WR[:, slot, kc, off:off + 128], H[:, kc, tsl], kc == 0, kc == 7,
                           [WRt[slot], Ht[kc][t]], [PSt[ba]], kc == 7)
                    for kc in range(8):
                        mm(PSb[bb_][:], WR[:, slot, kc, off + 128:off + 256], H[:, kc, tsl], kc == 0, kc == 7,
                           [WRt[slot], Ht[kc][t]], [PSt[bb_]], kc == 7)
                    s_ap, s_t = nextF()
                    act(AF.Sigmoid, s_ap[:], PSb[bb_][:], [PSt[bb_], vecT], [s_t], bias=vcol(c_bin + 2 * p + 1))
                    stt("dve", U[:, p, 32 + t * TW:32 + (t + 1) * TW], PSb[ba][:], vcol(c_bin + 2 * p), s_ap[:],
                        ALU.add, ALU.mult, [PSt[ba], s_t, vecT], [Ut[p][t]])
                if p % 2 == 1:
                    w_release()
            gc = [0]
            for gi in range(8):
                if gi % 4 == 0:
                    slot = w_acquire()
                off = (gi % 4) * 128
                for t in range(NT):
                    tsl = slice(t * TW, (t + 1) * TW)
                    b = 4 + (gc[0] % 2)
                    gc[0] += 1
                    for kc in range(8):
                        mm(PSb[b][:], WR[:, slot, kc, off:off + 128], H[:, kc, tsl], kc == 0, kc == 7,
                           [WRt[slot], Ht[kc][t]], [PSt[b]], kc == 7)
                    act(AF.Silu, G[:, gi, tsl], PSb[b][:], [PSt[b], vecT], [Gt[gi][t]], bias=vcol(c_bin + 16 + gi))
                if gi % 4 == 3:
                    w_release()
            Cv = R1[:, 0:8192].bitcast(F32)
            L = R1[:, 8192:16384].rearrange("p (g m c) -> p g m c", g=32, m=8)
            Ct = [T("c%d" % p) for p in range(8)]
            Lt = [T("L%d" % p) for p in range(8)]
            take("R1", Ct + Lt)
            w4c = vb + 64
            for p in range(8):
                sch.op("dve" if p % 2 == 0 else "pool",
                       lambda e, p=p: e.tensor_tensor(
                           out=L[:, 4 * p:4 * p + 4, :, :].rearrange("p g m c -> p (g m) c"),
                           in0=i4[:].unsqueeze(1).to_broadcast([128, 32, 32]),
                           in1=vecs[:, w4c + 32 * p:w4c + 32 * p + 32].unsqueeze(2).to_broadcast([128, 32, 32]),
                           op=ALU.mult),
                       [constT, vecT], [Lt[p]])
            UB = [WH[:, :, :].rearrange("p a b -> p (a b)"), U4B[:, :]]
            UBt = [T("ub0"), T("ub1")]
            handoff(WHt, [UBt[0]])
            for s_ in range(2):
                sch.op("dve", lambda e, s_=s_: e.memset(UB[s_][:, 0:2176], 0.0), writes=[UBt[s_]])
            ws0 = w_acquire()
            ws1 = w_acquire()
            wso = [ws0, ws1]
            dcnt = 0
            oc = [0]
            yc = [0]
            mu_ap, mu_t = Fp[0], Ft[0]
            m2_ap, m2_t = Fp[1], Ft[1]
            pend = {}

            def conv_mm(n):
                t, p = divmod(n, 8)
                s_ = n % 2
                cb = n % 2
                for g_ in range(4):
                    for jj in range(4):
                        wdt = 540 if jj < 3 else 539
                        c0 = t * TW + 2 + jj
                        sch.dma("sp", UB[s_][32 * jj:32 * jj + 32, g_ * 544:g_ * 544 + wdt],
                                U[32 * g_:32 * g_ + 32, p, c0:c0 + wdt],
                                reads=[Ut[p][t], Ut[p][max(t - 1, 0)], Upad], writes=[UBt[s_]], key="ub%d" % s_)
                for m in range(8):
                    for g_ in range(4):
                        last = (m == 7 and g_ == 3)
                        out_ap = PSb[cb][32 * g_:32 * g_ + 32, :]
                        lhs = L[:, 4 * p + g_, m, :]
                        rhs = UB[s_][:, g_ * 544 + 4 * m:g_ * 544 + 4 * m + TW]
                        sch.op("pe", lambda e, out_ap=out_ap, lhs=lhs, rhs=rhs, m=m, g_=g_: e.matmul(
                            out_ap, lhsT=lhs, rhs=rhs, start=(m == 0), stop=(m == 7), tile_position=(0, 32 * g_)),
                            [Lt[p], UBt[s_]], [PSt[cb]], last)

            def evac(n):
                t, p = divmod(n, 8)
                cb = n % 2
                Cp = Cv[:, p * TW:(p + 1) * TW]
                act(AF.Identity, Cp, PSb[cb][:], [PSt[cb], vecT], [Ct[p]], bias=vcol(c_dwb + p))
                q_ap, q_t = nextB()
                act(AF.Square, q_ap[:], PSb[cb][:], [PSt[cb], vecT], [q_t], bias=vcol(c_dwb + p))
                l_ap, l_t = nextB()
                cp("dve", l_ap[:], Cp, [Ct[p]], [l_t])
                pend[n] = (l_ap, l_t, q_ap, q_t)

            def stats_mm(n):
                t, p = divmod(n, 8)
                l_ap, l_t, q_ap, q_t = pend.pop(n)
                mm(PSb[6][:], ones[:], l_ap[:], p == 0, p == 7, [onesT, l_t], [PSt[6]], True)
                mm(PSb[7][:], ones[:], q_ap[:], p == 0, p == 7, [onesT, q_t], [PSt[7]], True)

            def ln_stats():
                ts2("dve", mu_ap[:], PSb[6][:], 1.0 / D, None, ALU.mult, None, [PSt[6]], [mu_t])
                tt("dve", m2_ap[:], mu_ap[:], mu_ap[:], ALU.mult, [mu_t], [m2_t])
                stt("dve", m2_ap[:], PSb[7][:], 1.0 / D, m2_ap[:], ALU.mult, ALU.subtract, [PSt[7], m2_t], [m2_t])
                act_rsqrt(m2_ap[:], m2_ap[:], 1.0, epsl[:], [m2_t], m2_t)

            def normalize(n):
                t, p = divmod(n, 8)
                tsl = slice(t * TW, (t + 1) * TW)
                Cp = Cv[:, p * TW:(p + 1) * TW]
                y_ap, y_t = Fp[2 + yc[0] % 3], Ft[2 + yc[0] % 3]
                yc[0] += 1
                tt("dve", y_ap[:], Cp, mu_ap[:], ALU.subtract, [Ct[p], mu_t], [y_t])
                tt("dve", y_ap[:], y_ap[:], m2_ap[:], ALU.mult, [y_t, m2_t], [y_t])
                act(AF.Silu, y_ap[:], y_ap[:], [y_t, vecT], [y_t], bias=vcol(c_lnb + p), scale=vcol(c_lng + p))
                tt("pool", G[:, p, tsl], y_ap[:], G[:, p, tsl], ALU.mult, [y_t, Gt[p][t]], [Gt[p][t]])

            def outproj(t):
                tsl = slice(t * TW, (t + 1) * TW)
                for m in range(8):
                    b = 2 + (oc[0] % 4)
                    oc[0] += 1
                    slot = wso[m // 4]
                    for kc in range(8):
                        mm(PSb[b][:], WR[:, slot, kc, (m % 4) * 128:(m % 4 + 1) * 128], G[:, kc, tsl], kc == 0, kc == 7,
                           [WRt[slot], Gt[kc][t]], [PSt[b]], kc == 7)
                    stt("dve", X[:, m, tsl], PSb[b][:], vcol(c_bout + m), X[:, m, tsl], ALU.add, ALU.add,
                        [PSt[b], Xt[m][t], vecT], [Xt[m][t]])

            NCH = NT * 8
            fpool[0] = [5]
            for n in range(NCH):
                t, p = divmod(n, 8)
                conv_mm(n)
                if n >= 1:
                    stats_mm(n - 1)
                if p == 0 and t > 0:
                    ln_stats()
                if n >= 8:
                    normalize(n - 8)
                if p == 7 and t > 0:
                    outproj(t - 1)
                    if after_tile is not None:
                        after_tile(t - 1, 2 + (oc[0] % 4))
                        oc[0] += 1
                evac(n)
            stats_mm(NCH - 1)
            ln_stats()
            for n in range(NCH - 8, NCH):
                normalize(n)
            outproj(NT - 1)
            if after_tile is not None:
                after_tile(NT - 1, 2 + (oc[0] % 4))
            fpool[0] = list(range(NF))
            w_release()
            w_release()

        def finish_tile(t, bank_=None):
            if final_norm:
                rmsnorm(V_FINAL, lambda c, tsl: X[:, c, tsl], lambda c, t_: Xt[c][t_], tiles=[t], bank_=bank_)
            sch.dma("sp", yT[:, t * TW:(t + 1) * TW].rearrange("(c p) n -> p c n", p=128),
                    X[:, :, t * TW:(t + 1) * TW], reads=[Xt[c][t] for c in range(8)], key="out")

        mi = ci = 0
        for li, kind in enumerate(layer_kinds):
            if kind == "mla":
                conv_next = ci if (li + 1 < n_layers and layer_kinds[li + 1] == "conv") else None
                mla_layer(mi, conv_next)
                mi += 1
            else:
                conv_layer(ci, finish_tile if li == n_layers - 1 else None)
                ci += 1

        if layer_kinds[-1] != "conv":
            for t in range(NT):
                finish_tile(t)
        sch.wait_all("sp", "out")
        sch.emit()
    return nc


def _col8(v):
    v = np.asarray(v, np.float32)
    n = v.shape[0] // 128
    return np.ascontiguousarray(v.reshape(n, 128).T)


def _perm_half(w):
    return np.concatenate([w[..., 32:64], w[..., 0:32]], axis=-1)


def prep_shared(inp, layer_kinds):
    f32 = np.float32
    vecs = np.zeros((128, NV), f32)
    vecs[:, V_FINAL:V_FINAL + 8] = _col8(inp["final_norm_g"])
    inv = (np.float32(10000.0) ** (-(np.arange(0, 64, 2, dtype=np.float32)) / np.float32(64))).astype(f32)
    vecs[:, V_INVF] = np.tile(inv, 4)
    ph = np.zeros(128, f32)
    ph[0:64] = np.float32(math.pi / 2)
    ph[64:96] = np.float32(math.pi)
    vecs[:, V_PHASE] = ph
    out = {}
    mi = ci = 0
    for kind in layer_kinds:
        if kind == "mla":
            vb = V_MLA + mi * MLA_VW
            vecs[:, vb:vb + 8] = _col8(inp["mla_norm_g"][mi])
            vecs[:, vb + 8:vb + 11] = _col8(inp["mla_q_norm_g"][mi])
            vecs[:, vb + 11:vb + 13] = _col8(inp["mla_kv_norm_g"][mi])
            w = np.asarray(inp["mla_w_in"][mi], f32)
            kpe = w[:, 640:704]
            out["mwin%d" % mi] = np.ascontiguousarray(
                np.concatenate([w[:, 0:640], kpe, _perm_half(kpe), w[:, 704:1728]], axis=1))
            wq = np.asarray(inp["mla_w_qb"][mi], f32).reshape(384, 8, 192)
            out["mwqb%d" % mi] = np.ascontiguousarray(
                np.concatenate([wq, _perm_half(wq[:, :, 128:192])], axis=2).reshape(384, 2048))
            out["mwkvb%d" % mi] = np.ascontiguousarray(np.asarray(inp["mla_w_kvb"][mi], f32))
            out["mwout%d" % mi] = np.ascontiguousarray(np.asarray(inp["mla_w_out"][mi], f32))
            mi += 1
        else:
            vb = V_CONV + ci * CONV_VW
            vecs[:, vb:vb + 8] = _col8(inp["conv_norm_g"][ci])
            bin_ = np.asarray(inp["conv_b_in"][ci], f32)
            ba, bb, bg = _col8(bin_[0:1024]), _col8(bin_[1024:2048]), _col8(bin_[2048:3072])
            for p in range(8):
                vecs[:, vb + 8 + 2 * p] = ba[:, p]
                vecs[:, vb + 8 + 2 * p + 1] = bb[:, p]
            vecs[:, vb + 24:vb + 32] = bg
            vecs[:, vb + 32:vb + 40] = _col8(inp["conv_dw_b"][ci])
            vecs[:, vb + 40:vb + 48] = _col8(inp["conv_ln_g"][ci])
            vecs[:, vb + 48:vb + 56] = _col8(inp["conv_ln_b"][ci])
            vecs[:, vb + 56:vb + 64] = _col8(inp["conv_b_out"][ci])
            dw = np.asarray(inp["conv_dw_w"][ci], f32)
            dwp = np.concatenate([dw, np.zeros((1, 1024), f32)], axis=0)
            w4 = dwp.reshape(8, 4, 8, 4, 32).transpose(1, 4, 2, 3, 0).reshape(128, 256)
            vecs[:, vb + 64:vb + 64 + 256] = w4
            w = np.asarray(inp["conv_w_in"][ci], f32)
            cols = []
            for p in range(8):
                cols.append(w[:, p * 128:(p + 1) * 128])
                cols.append(w[:, 1024 + p * 128:1024 + (p + 1) * 128])
            cols.append(w[:, 2048:3072])
            out["cwin%d" % ci] = np.ascontiguousarray(np.concatenate(cols, axis=1))
            out["cwout%d" % ci] = np.ascontiguousarray(np.asarray(inp["conv_w_out"][ci], f32))
            ci += 1
    out["vecs"] = vecs
    out["ident"] = np.eye(128, dtype=f32)
    out["i4"] = np.tile(np.eye(32, dtype=f32), (4, 1))
    tk = np.arange(128)[:, None]
    tq = np.arange(128)[None, :]
    out["maskT"] = np.where(tk <= tq, 0.0, NEG).astype(f32)
    return out


LAYERS = ["mla", "conv", "mla", "conv"]
_CACHE = {}


def kernel(**inputs):
    x = np.asarray(inputs["x"], np.float32)
    pos = np.asarray(inputs["positions"], np.int32)
    B = x.shape[0]
    shared = prep_shared(inputs, LAYERS)
    if "nc" not in _CACHE:
        _CACHE["nc"] = build_program(LAYERS, final_norm=True)
    nc = _CACHE["nc"]
    in_maps = []
    for b in range(B):
        m = dict(shared)
        m["xT"] = np.ascontiguousarray(x[b].T)
        m["pos"] = np.ascontiguousarray(pos[b][None, :])
        in_maps.append(m)
    res = run_bass_kernel_spmd(nc, in_maps, core_ids=list(range(B)))
    out = np.stack([np.ascontiguousarray(res.results[b]["yT"].T) for b in range(B)], axis=0)
    return out.astype(np.float32)
```

```python
import math
from contextlib import ExitStack

import numpy as np
import concourse.bass as bass
import concourse.mybir as mybir
from concourse.bass_utils import run_bass_kernel_spmd

F32 = mybir.dt.float32
BF16 = mybir.dt.bfloat16
I32 = mybir.dt.int32
AF = mybir.ActivationFunctionType
ALU = mybir.AluOpType

S = 2048
D = 1024
NT = 4
TW = 512
NH = 8
SCALE = 1.0 / math.sqrt(192.0)
RMS_EPS = 1e-6
LN_EPS = 1e-5
NEG = -30000.0
TWO_PI = 2.0 * math.pi
C1 = 6.28125
C2 = TWO_PI - C1
PI_LO = 3.1415925

V_FINAL = 0
V_INVF = 8
V_PHASE = 9
V_MLA = 10
MLA_VW = 13
V_CONV = V_MLA + 2 * MLA_VW
CONV_VW = 8 + 24 + 8 + 8 + 8 + 8 + 256
NV = V_CONV + 2 * CONV_VW


class T:
    __slots__ = ("name", "w", "r")

    def __init__(self, name):
        self.name = name
        self.w = None
        self.r = {}


class Sched:
    ENGS = ("pe", "act", "dve", "pool", "sp")

    def __init__(self, nc, st):
        self.nc, self.st = nc, st
        self.prog = {e: [] for e in self.ENGS}
        self.sem = {}
        self.cnt = {}
        self.seen = {e: {} for e in self.ENGS}
        self.uid = 0

    def getsem(self, key):
        if key not in self.sem:
            self.sem[key] = self.st.enter_context(self.nc.semaphore("s_" + key))
            self.cnt[key] = 0
        return self.sem[key]

    def _deps(self, eng, reads, writes, is_dma):
        deps = {}

        def add(d, keep_same):
            if d is None:
                return
            k, v = d
            if k == eng and not keep_same:
                return
            if deps.get(k, 0) < v:
                deps[k] = v

        for t in reads:
            add(t.w, is_dma or eng != "pe")
        for t in writes:
            add(t.w, is_dma)
            for k, v in t.r.items():
                add((k, v), is_dma)
        for k, v in deps.items():
            if self.seen[eng].get(k, 0) < v:
                self.prog[eng].append(("wait", k, v))
                self.seen[eng][k] = v

    def _mark(self, d, reads, writes):
        k, v = d
        for t in reads:
            if t.r.get(k, 0) < v:
                t.r[k] = v
        for t in writes:
            t.w = d
            t.r = {}

    def op(self, eng, fn, reads=(), writes=(), sig=True):
        self._deps(eng, reads, writes, False)
        self.getsem(eng)
        val = self.cnt[eng] + 1
        if sig:
            self.cnt[eng] = val
        self.prog[eng].append(("op", fn, eng if sig else None, 1))
        self._mark((eng, val), reads, writes)

    def dma(self, q, out_ap, in_ap, reads=(), writes=(), key=None):
        if key is None:
            key = "d%d" % self.uid
            self.uid += 1
        self._deps(q, reads, writes, True)
        self.getsem(key)
        self.cnt[key] += 16
        self.prog[q].append(("op", lambda e: e.dma_start(out=out_ap, in_=in_ap), key, 16))
        d = (key, self.cnt[key])
        self._mark(d, reads, writes)
        return d

    def wait_all(self, eng, key):
        v = self.cnt[key]
        if self.seen[eng].get(key, 0) < v:
            self.prog[eng].append(("wait", key, v))
            self.seen[eng][key] = v

    def emit(self):
        blk = self.st.enter_context(self.nc.Block())

        def body(name):
            def run(e):
                for it in self.prog[name]:
                    if it[0] == "wait":
                        e.wait_ge(self.sem[it[1]], it[2])
                    else:
                        ins = it[1](e)
                        if it[2] is not None:
                            ins.then_inc(self.sem[it[2]], it[3])
            return run

        blk.tensor(body("pe"))
        blk.scalar(body("act"))
        blk.vector(body("dve"))
        blk.gpsimd(body("pool"))
        blk.sync(body("sp"))


def handoff(old, new):
    u = {}
    for t in old:
        if t.w is not None:
            k, v = t.w
            if u.get(k, 0) < v:
                u[k] = v
        for k, v in t.r.items():
            if u.get(k, 0) < v:
                u[k] = v
    for t in new:
        t.w = None
        t.r = dict(u)


def build_program(layer_kinds, final_norm=True):
    nc = bass.Bass("TRN2", target_bir_lowering=False)
    n_layers = len(layer_kinds)
    dram = {}
    dram["xT"] = nc.dram_tensor("xT", [D, S], F32, kind="ExternalInput").ap()
    dram["pos"] = nc.dram_tensor("pos", [1, S], I32, kind="ExternalInput").ap()
    dram["vecs"] = nc.dram_tensor("vecs", [128, NV], F32, kind="ExternalInput").ap()
    dram["ident"] = nc.dram_tensor("ident", [128, 128], F32, kind="ExternalInput").ap()
    dram["maskT"] = nc.dram_tensor("maskT", [128, 128], F32, kind="ExternalInput").ap()
    dram["i4"] = nc.dram_tensor("i4", [128, 32], F32, kind="ExternalInput").ap()
    n_mla = sum(1 for k in layer_kinds if k == "mla")
    n_conv = n_layers - n_mla
    for j in range(n_mla):
        dram["mwin%d" % j] = nc.dram_tensor("mwin%d" % j, [D, 1792], F32, kind="ExternalInput").ap()
        dram["mwqb%d" % j] = nc.dram_tensor("mwqb%d" % j, [384, 2048], F32, kind="ExternalInput").ap()
        dram["mwkvb%d" % j] = nc.dram_tensor("mwkvb%d" % j, [256, 2048], F32, kind="ExternalInput").ap()
        dram["mwout%d" % j] = nc.dram_tensor("mwout%d" % j, [D, D], F32, kind="ExternalInput").ap()
    for j in range(n_conv):
        dram["cwin%d" % j] = nc.dram_tensor("cwin%d" % j, [D, 3072], F32, kind="ExternalInput").ap()
        dram["cwout%d" % j] = nc.dram_tensor("cwout%d" % j, [D, D], F32, kind="ExternalInput").ap()
    yT = nc.dram_tensor("yT", [D, S], F32, kind="ExternalOutput").ap()

    ud = nc.dram_tensor("ud", [max(n_conv, 1), D, 2080], BF16, kind="Internal").ap()

    st = ExitStack()
    with st:
        def sb(name, shape, dt):
            return st.enter_context(nc.sbuf_tensor(name, shape, dt))

        X = sb("X", [128, 8, S], F32)
        R1 = sb("R1", [128, 16384], BF16)
        R2 = sb("R2", [128, 16384], BF16)
        R3 = sb("R3", [128, 16640], BF16)
        WH = sb("WH", [128, 2, 1280], BF16)
        WR = sb("WR", [128, 2, 8, 512], BF16)
        NF, NB = 6, 6
        Fp = [sb("F%d" % i, [128, TW], F32) for i in range(NF)]
        Bp = [sb("B%d" % i, [128, TW], BF16) for i in range(NB)]
        i4 = sb("i4b", [128, 32], BF16)
        ident = sb("identb", [128, 128], BF16)
        maskT = sb("maskb", [128, 128], BF16)
        ones = sb("onesb", [128, 128], BF16)
        vecs = sb("vecs_sb", [128, NV], F32)
        epsr = sb("epsr", [128, 1], F32)
        epsl = sb("epsl", [128, 1], F32)
        PSb = [st.enter_context(nc.psum_tensor("ps%d" % i, [128, TW], F32)) for i in range(8)]

        sch = Sched(nc, st)

        Xt = [[T("X%d_%d" % (c, t)) for t in range(NT)] for c in range(8)]
        PSt = [T("ps%d" % i) for i in range(8)]
        Ft = [T("F%d" % i) for i in range(NF)]
        Bt = [T("B%d" % i) for i in range(NB)]
        WRt = [T("WR0"), T("WR1")]
        WHt = [T("WH0"), T("WH1")]
        constT = T("const")
        vecT = T("vecs")
        region = {"R1": [], "R2": [], "R3": []}

        def take(rname, tiles):
            handoff(region[rname], tiles)
            region[rname] = list(tiles)

        fi = [0]
        bi = [0]

        fpool = [list(range(NF))]

        def nextF():
            pool = fpool[0]
            i = pool[fi[0] % len(pool)]
            fi[0] += 1
            return Fp[i], Ft[i]

        def nextB():
            i = bi[0] % NB
            bi[0] += 1
            return Bp[i], Bt[i]

        def mm(out, lhsT, rhs, start, stop, reads, writes, sig):
            sch.op("pe", lambda e: e.matmul(out, lhsT=lhsT, rhs=rhs, start=start, stop=stop),
                   reads, writes, sig)

        def act(func, out, in_, reads, writes, bias=None, scale=None):
            kw = {}
            if bias is not None:
                kw["bias"] = bias
            if scale is not None:
                kw["scale"] = scale
            sch.op("act", lambda e: e.activation(out=out, in_=in_, func=func, **kw), reads, writes)

        def tt(eng, out, in0, in1, op, reads, writes):
            sch.op(eng, lambda e: e.tensor_tensor(out=out, in0=in0, in1=in1, op=op), reads, writes)

        def ts2(eng, out, in0, s1, s2, op0, op1, reads, writes):
            if s2 is None:
                sch.op(eng, lambda e: e.tensor_scalar(out=out, in0=in0, scalar1=s1, scalar2=None, op0=op0),
                       reads, writes)
            else:
                sch.op(eng, lambda e: e.tensor_scalar(out=out, in0=in0, scalar1=s1, scalar2=s2, op0=op0, op1=op1),
                       reads, writes)

        def stt(eng, out, in0, scalar, in1, op0, op1, reads, writes):
            sch.op(eng, lambda e: e.scalar_tensor_tensor(out=out, in0=in0, scalar=scalar, in1=in1, op0=op0, op1=op1),
                   reads, writes)

        def cp(eng, out, in_, reads, writes):
            sch.op(eng, lambda e: e.tensor_copy(out=out, in_=in_), reads, writes)

        def recip(out, in_, reads, writes):
            sch.op("dve", lambda e: e.reciprocal(out=out, in_=in_), reads, writes)

        def vcol(c):
            return vecs[:, c:c + 1]

        def act_rsqrt(out, in_, scale, eps_ap, reads, wt):
            act(AF.Ln, out, in_, list(reads) + [epsT], [wt], bias=eps_ap, scale=scale)
            act(AF.Exp, out, out, [wt], [wt], scale=-0.5)

        def act_recip(out, in_, reads, wt):
            act(AF.Ln, out, in_, list(reads), [wt])
            act(AF.Exp, out, out, [wt], [wt], scale=-1.0)

        witems = []
        mi = ci = 0
        for kind in layer_kinds:
            if kind == "mla":
                w = dram["mwin%d" % mi]
                witems += [(w[:, 0:512], 512), (w[:, 512:1024], 512), (w[:, 1024:1536], 512), (w[:, 1536:1792], 256)]
                w = dram["mwout%d" % mi]
                witems += [(w[:, 0:512], 512), (w[:, 512:1024], 512)]
                mi += 1
            else:
                w = dram["cwin%d" % ci]
                witems += [(w[:, i * 512:(i + 1) * 512], 512) for i in range(6)]
                w = dram["cwout%d" % ci]
                witems += [(w[:, 0:512], 512), (w[:, 512:1024], 512)]
                ci += 1
        wstate = {"loaded": 0, "acq": 0, "rel": 0}

        def w_fill():
            while wstate["loaded"] < min(len(witems), wstate["rel"] + 2):
                i = wstate["loaded"]
                ap, ncols = witems[i]
                slot = i % 2
                sch.dma("pool", WR[:, slot, :, 0:ncols], ap.rearrange("(c p) n -> p c n", p=128),
                        writes=[WRt[slot]], key="wr%d" % slot)
                wstate["loaded"] += 1

        def w_acquire():
            i = wstate["acq"]
            assert i < wstate["loaded"], "weight item not loaded"
            wstate["acq"] += 1
            return i % 2

        def w_release():
            wstate["rel"] += 1
            w_fill()

        sch.dma("sp", vecs[:], dram["vecs"][:, :], writes=[vecT])
        first_pos = [True]
        posi0 = R1[:, 0:4096].bitcast(I32)
        pos0T = T("posi")
        sch.dma("sp", posi0, dram["pos"].partition_broadcast(128), writes=[pos0T])
        for t in range(NT):
            sch.dma("sp", X[:, :, t * TW:(t + 1) * TW],
                    dram["xT"][:, t * TW:(t + 1) * TW].rearrange("(c p) n -> p c n", p=128),
                    writes=[Xt[c][t] for c in range(8)])
        sch.dma("pool", ident[:], dram["ident"][:, :], writes=[constT], key="cst")
        sch.dma("pool", maskT[:], dram["maskT"][:, :], writes=[constT], key="cst")
        sch.dma("pool", i4[:], dram["i4"][:, :], writes=[constT], key="cst")
        onesT = T("ones")
        sch.op("dve", lambda e: e.memset(ones[:], 1.0), writes=[onesT])
        epsT = T("eps")
        sch.op("dve", lambda e: e.memset(epsr[:], RMS_EPS), writes=[epsT])
        sch.op("dve", lambda e: e.memset(epsl[:], LN_EPS), writes=[epsT])
        w_fill()

        def rmsnorm(gcol, dst_ap_fn, dst_t_fn, tiles=None, bank_=None):
            for t in (range(NT) if tiles is None else tiles):
                tsl = slice(t * TW, (t + 1) * TW)
                bank = (6 + (t % 2)) if bank_ is None else bank_
                for c in range(8):
                    b_ap, b_t = nextB()
                    act(AF.Square, b_ap[:], X[:, c, tsl], [Xt[c][t]], [b_t])
                    mm(PSb[bank][:], ones[:], b_ap[:], c == 0, c == 7, [b_t, onesT], [PSt[bank]], True)
                f_ap, f_t = nextF()
                act_rsqrt(f_ap[:], PSb[bank][:], 1.0 / D, epsr[:], [PSt[bank]], f_t)
                for c in range(8):
                    stt("dve", dst_ap_fn(c, tsl), X[:, c, tsl], vcol(gcol + c), f_ap[:], ALU.mult, ALU.mult,
                        [Xt[c][t], f_t, vecT], [dst_t_fn(c, t)])

        H = R1[:, :].rearrange("p (c n) -> p c n", c=8)
        G = R2[:, :].rearrange("p (c n) -> p c n", c=8)

        def mla_layer(j, conv_next):
            vb = V_MLA + j * MLA_VW
            QN = R3[:, 0:6144].rearrange("p (c n) -> p c n", c=3)
            KVN = R3[:, 6144:10240].rearrange("p (c n) -> p c n", c=2)
            KPE = R3[:, 10240:12288]
            ROPE = R3[:, 12288:16384].bitcast(F32)
            LAT = [QN[:, 0], QN[:, 1], QN[:, 2], KVN[:, 0], KVN[:, 1]]
            posi = R1[:, 0:4096].bitcast(I32)
            t1 = R1[:, 4096:8192].bitcast(F32)
            ki = R1[:, 8192:12288].bitcast(I32)
            kf = R1[:, 12288:16384].bitcast(F32)
            tmpT = [T("posi"), T("t1"), T("ki"), T("kf")]
            if first_pos[0]:
                tmpT[0] = pos0T
                region["R1"] = [pos0T]
            take("R1", tmpT[1:] if first_pos[0] else tmpT)
            if first_pos[0]:
                region["R1"] = list(tmpT)
            ropeT = T("rope")
            LATt = [[T("lat%d_%d" % (f, t)) for t in range(NT)] for f in range(5)]
            KPEt = [T("kpe%d" % t) for t in range(NT)]
            take("R3", [ropeT] + [x for r in LATt for x in r] + KPEt)
            if not first_pos[0]:
                sch.dma("sp", posi, dram["pos"].partition_broadcast(128), writes=[tmpT[0]])
            first_pos[0] = False
            cp("dve", t1, posi, [tmpT[0]], [tmpT[1]])
            ts2("dve", t1, t1, vcol(V_INVF), vcol(V_PHASE), ALU.mult, ALU.add, [tmpT[1], vecT], [tmpT[1]])
            ts2("dve", kf, t1, 1.0 / TWO_PI, None, ALU.mult, None, [tmpT[1]], [tmpT[3]])
            cp("dve", ki, kf, [tmpT[3]], [tmpT[2]])
            cp("dve", kf, ki, [tmpT[2]], [tmpT[3]])
            stt("dve", t1, kf, -C1, t1, ALU.mult, ALU.add, [tmpT[3], tmpT[1]], [tmpT[1]])
            stt("dve", t1, kf, -C2, t1, ALU.mult, ALU.add, [tmpT[3], tmpT[1]], [tmpT[1]])
            ts2("dve", t1, t1, PI_LO, -PI_LO, ALU.min, ALU.max, [tmpT[1]], [tmpT[1]])
            act(AF.Sin, ROPE, t1, [tmpT[1]], [ropeT])

            Ht = [[T("h%d_%d" % (c, t)) for t in range(NT)] for c in range(8)]
            take("R1", [x for r in Ht for x in r])
            rmsnorm(vb, lambda c, tsl: H[:, c, tsl], lambda c, t: Ht[c][t])

            RAWv = R2[:, 0:10240].bitcast(F32)
            RAWt = [[T("raw%d_%d" % (s_, f)) for f in range(5)] for s_ in range(2)]
            take("R2", [x for r in RAWt for x in r])
            s0 = w_acquire()
            s1 = w_acquire()
            wsl = [(s0, 0), (s0, 128), (s0, 256), (s0, 384), (s1, 0), (s1, 128)]
            pcnt = [0]

            def proj_bank():
                b = pcnt[0] % 4
                pcnt[0] += 1
                return b

            def rope_apply(ps_ap, ps_t, out_ap, out_t, tsl):
                a_ap, a_t = nextF()
                tt("dve", a_ap[:], ps_ap, ROPE[:, tsl], ALU.mult, [ps_t, ropeT], [a_t])
                b_ap, b_t = nextF()
                cp("dve", b_ap[0:64, :], a_ap[64:128, :], [a_t], [b_t])
                tt("dve", out_ap, a_ap[0:64, :], b_ap[0:64, :], ALU.add, [a_t, b_t], [out_t])

            for t in range(NT):
                tsl = slice(t * TW, (t + 1) * TW)
                rs = t % 2
                for f in range(6):
                    slot, off = wsl[f]
                    b = proj_bank()
                    for kc in range(8):
                        mm(PSb[b][:], WR[:, slot, kc, off:off + 128], H[:, kc, tsl], kc == 0, kc == 7,
                           [WRt[slot], Ht[kc][t]], [PSt[b]], kc == 7)
                    if f < 5:
                        raw = RAWv[:, (rs * 5 + f) * TW:(rs * 5 + f + 1) * TW]
                        act(AF.Copy, raw, PSb[b][:], [PSt[b]], [RAWt[rs][f]])
                        q_ap, q_t = nextB()
                        act(AF.Square, q_ap[:], PSb[b][:], [PSt[b]], [q_t])
                        sbank = 6 if f < 3 else 7
                        first = f in (0, 3)
                        last = f in (2, 4)
                        mm(PSb[sbank][:], ones[:], q_ap[:], first, last, [q_t, onesT], [PSt[sbank]], True)
                    else:
                        rope_apply(PSb[b][:], PSt[b], KPE[0:64, tsl], KPEt[t], tsl)
                rq_ap, rq_t = nextF()
                act_rsqrt(rq_ap[:], PSb[6][:], 1.0 / 384, epsr[:], [PSt[6]], rq_t)
                rk_ap, rk_t = nextF()
                act_rsqrt(rk_ap[:], PSb[7][:], 1.0 / 256, epsr[:], [PSt[7]], rk_t)
                for f in range(5):
                    raw = RAWv[:, (rs * 5 + f) * TW:(rs * 5 + f + 1) * TW]
                    r_ap, r_t = (rq_ap, rq_t) if f < 3 else (rk_ap, rk_t)
                    stt("dve", LAT[f][:, tsl], raw, vcol(vb + 8 + f), r_ap[:], ALU.mult, ALU.mult,
                        [RAWt[rs][f], r_t, vecT], [LATt[f][t]])

            Gt = [[T("g%d_%d" % (c, t)) for t in range(NT)] for c in range(8)]
            take("R2", [x for r in Gt for x in r])
            cur = s1
            held = [s0, s1]
            for gi in range(8):
                if gi == 0:
                    slot, off = s1, 256
                elif gi == 1:
                    slot, off = s1, 384
                elif gi < 6:
                    if gi == 2:
                        w_release()
                        s2 = w_acquire()
                    slot, off = s2, (gi - 2) * 128
                else:
                    if gi == 6:
                        w_release()
                        s3 = w_acquire()
                    slot, off = s3, (gi - 6) * 128
                for t in range(NT):
                    tsl = slice(t * TW, (t + 1) * TW)
                    b = proj_bank()
                    for kc in range(8):
                        mm(PSb[b][:], WR[:, slot, kc, off:off + 128], H[:, kc, tsl], kc == 0, kc == 7,
                           [WRt[slot], Ht[kc][t]], [PSt[b]], kc == 7)
                    act(AF.Silu, G[:, gi, tsl], PSb[b][:], [PSt[b]], [Gt[gi][t]])
                if gi == 5:
                    w_release()
            w_release()

            HB = []
            for s_ in range(2):
                base = s_ * 8192
                HB.append(dict(
                    qn=R1[:, base:base + 2048], qp=R1[:, base + 2048:base + 4096],
                    kh=R1[:, base + 4096:base + 6144],
                    vh=R1[:, base + 6144:base + 8192].rearrange("p (b n) -> p b n", n=128),
                    qnt=[T("qn%d_%d" % (s_, t)) for t in range(NT)],
                    qpt=[T("qp%d_%d" % (s_, t)) for t in range(NT)],
                    kht=[T("kh%d_%d" % (s_, t)) for t in range(NT)],
                    vht=[T("vh%d_%d" % (s_, t)) for t in range(NT)],
                ))
            take("R1", [x for hb in HB for key in ("qnt", "qpt", "kht", "vht") for x in hb[key]])
            wq_d = dram["mwqb%d" % j]
            wkv_d = dram["mwkvb%d" % j]

            def load_head_w(h):
                s_ = h % 2
                sch.dma("pool", WH[:, s_, 0:768].rearrange("p (c n) -> p c n", c=3),
                        wq_d[:, h * 256:(h + 1) * 256].rearrange("(c p) n -> p c n", p=128),
                        writes=[WHt[s_]], key="wh%d" % s_)
                sch.dma("pool", WH[:, s_, 768:1280].rearrange("p (c n) -> p c n", c=2),
                        wkv_d[:, h * 256:(h + 1) * 256].rearrange("(c p) n -> p c n", p=128),
                        writes=[WHt[s_]], key="wh%d" % s_)

            jb = [0]

            def jit_bank():
                return 7

            def proj_pieces(h):
                s_ = h % 2
                hb = HB[s_]
                wq = WH[:, s_, 0:768].rearrange("p (c n) -> p c n", c=3)
                wkv = WH[:, s_, 768:1280].rearrange("p (c n) -> p c n", c=2)
                pieces = []
                for t in range(NT):
                    tsl = slice(t * TW, (t + 1) * TW)

                    def g_qn(t=t, tsl=tsl):
                        b = jit_bank()
                        for kc in range(3):
                            mm(PSb[b][:], wq[:, kc, 0:128], QN[:, kc, tsl], kc == 0, kc == 2,
                               [WHt[s_], LATt[kc][t]], [PSt[b]], kc == 2)
                        cp("dve", hb["qn"][:, tsl], PSb[b][:], [PSt[b]], [hb["qnt"][t]])

                    def g_qp(t=t, tsl=tsl):
                        b = jit_bank()
                        for kc in range(3):
                            mm(PSb[b][:], wq[:, kc, 128:256], QN[:, kc, tsl], kc == 0, kc == 2,
                               [WHt[s_], LATt[kc][t]], [PSt[b]], kc == 2)
                        rope_apply(PSb[b][:], PSt[b], hb["qp"][0:64, tsl], hb["qpt"][t], tsl)

                    def g_k(t=t, tsl=tsl):
                        b = jit_bank()
                        for kc in range(2):
                            mm(PSb[b][:], wkv[:, kc, 0:128], KVN[:, kc, tsl], kc == 0, kc == 1,
                               [WHt[s_], LATt[3 + kc][t]], [PSt[b]], kc == 1)
                        cp("dve", hb["kh"][:, tsl], PSb[b][:], [PSt[b]], [hb["kht"][t]])

                    def g_v(t=t, tsl=tsl):
                        b = jit_bank()
                        for bb in range(4):
                            tb = 4 * t + bb
                            for kc in range(2):
                                mm(PSb[b][:, bb * 128:(bb + 1) * 128], KVN[:, kc, tb * 128:(tb + 1) * 128],
                                   wkv[:, kc, 128:256], kc == 0, kc == 1,
                                   [WHt[s_], LATt[3 + kc][t]], [PSt[b]], (bb == 3 and kc == 1))
                        cp("dve", hb["vh"][:, 4 * t:4 * t + 4, :],
                           PSb[b][:].rearrange("p (b n) -> p b n", n=128), [PSt[b]], [hb["vht"][t]])
                    pieces.append([g_qn, g_k, g_qp, g_v])
                return pieces

            def attention(h, extra, groups):
                s_ = h % 2
                hb = HB[s_]
                n_groups = max(len(groups), 1)
                blk_done = [0]
                for i in range(NT):
                    ob, sbk = 3 + 2 * (i % 2), 4 + 2 * (i % 2)
                    nblk = 4 * i + 4
                    tsl = slice(i * TW, (i + 1) * TW)

                    def c0_of(jk):
                        return max(jk - 4 * i, 0) * 128

                    def qk(jk):
                        bank = jk % 3
                        c0 = c0_of(jk)
                        diag = jk >= 4 * i
                        tq0 = i * TW + c0
                        tk = slice(jk * 128, (jk + 1) * 128)
                        mm(PSb[bank][:, c0:TW], hb["kh"][:, tk], hb["qn"][:, tq0:(i + 1) * TW], True, False,
                           [hb["kht"][jk // 4], hb["qnt"][i]], [PSt[bank]], False)
                        mm(PSb[bank][:, c0:TW], KPE[0:64, tk], hb["qp"][0:64, tq0:(i + 1) * TW], False, not diag,
                           [KPEt[jk // 4], hb["qpt"][i]], [PSt[bank]], not diag)
                        if diag:
                            mm(PSb[bank][:, c0:c0 + 128], ident[:], maskT[:], False, True,
                               [constT], [PSt[bank]], True)

                    def pv(jk):
                        bank = jk % 3
                        c0 = c0_of(jk)
                        p_ap, p_t = nextB()
                        act(AF.Exp, p_ap[:, c0:TW], PSb[bank][:, c0:TW], [PSt[bank]], [p_t], scale=SCALE)
                        mm(PSb[ob][:, c0:TW], hb["vh"][:, jk, :], p_ap[:, c0:TW], jk == 0, jk == nblk - 1,
                           [hb["vht"][jk // 4], p_t], [PSt[ob]], False)
                        mm(PSb[sbk][:, c0:TW], ones[:], p_ap[:, c0:TW], jk == 0, jk == nblk - 1,
                           [onesT, p_t], [PSt[sbk]], True)

                    qk(0)
                    if nblk > 1:
                        qk(1)
                    for jk in range(nblk):
                        if jk + 2 < nblk:
                            qk(jk + 2)
                        pv(jk)
                        blk_done[0] += 1
                        while groups and blk_done[0] * n_groups >= (n_groups - len(groups) + 1) * 40:
                            groups.pop(0)()
                    r_ap, r_t = nextF()
                    act_recip(r_ap[:], PSb[sbk][:], [PSt[sbk]], r_t)
                    u_ap, u_t = nextF()
                    tt("dve", u_ap[:], PSb[ob][:], r_ap[:], ALU.mult, [PSt[ob], r_t], [u_t])
                    tt("pool", G[:, h, tsl], u_ap[:], G[:, h, tsl], ALU.mult, [u_t, Gt[h][i]], [Gt[h][i]])
                    for fn in extra[i]:
                        fn()
                while groups:
                    groups.pop(0)()

            load_head_w(0)
            load_head_w(1)
            for p in proj_pieces(0):
                for g_ in p:
                    g_()
            for h in range(NH):
                extra = [[] for _ in range(NT)]
                groups = []
                if h + 1 < NH:
                    groups = [g_ for p in proj_pieces(h + 1) for g_ in p]
                attention(h, extra, groups)
                if h + 2 < NH:
                    load_head_w(h + 2)

            for half in range(2):
                slot = w_acquire()
                for mm_ in range(4):
                    m = half * 4 + mm_
                    for t in range(NT):
                        tsl = slice(t * TW, (t + 1) * TW)
                        b = proj_bank()
                        for kc in range(8):
                            mm(PSb[b][:], WR[:, slot, kc, mm_ * 128:(mm_ + 1) * 128], G[:, kc, tsl], kc == 0, kc == 7,
                               [WRt[slot], Gt[kc][t]], [PSt[b]], kc == 7)
                        tt("dve", X[:, m, tsl], PSb[b][:], X[:, m, tsl], ALU.add, [PSt[b], Xt[m][t]], [Xt[m][t]])
                w_release()

        def conv_layer(j, after_tile=None):
            vb = V_CONV + j * CONV_VW
            c_bin, c_dwb, c_lng, c_lnb, c_bout = vb + 8, vb + 32, vb + 40, vb + 48, vb + 56
            Ht = [[T("h%d_%d" % (c, t)) for t in range(NT)] for c in range(8)]
            take("R1", [x for r in Ht for x in r])
            rmsnorm(vb, lambda c, tsl: H[:, c, tsl], lambda c, t: Ht[c][t])
            U = R3[:, :].rearrange("p (c n) -> p c n", c=8)
            Upad = T("upad")
            Ut = [[T("u%d_%d" % (c, t)) for t in range(NT)] for c in range(8)]
            take("R3", [Upad] + [x for r in Ut for x in r])
            sch.op("dve", lambda e: e.memset(U[:, :, 0:32], 0.0), writes=[Upad])
            Gt = [[T("g%d_%d" % (c, t)) for t in range(NT)] for c in range(8)]
            take("R2", [x for r in Gt for x in r])
            udT = [T("ud%d" % p) for p in range(8)]
            pc = [0]
            for p in range(8):
                if p % 2 == 0:
                    slot = w_acquire()
                off = (p % 2) * 256
                for t in range(NT):
                    tsl = slice(t * TW, (t + 1) * TW)
                    ba = (2 * pc[0]) % 4
                    bb_ = ba + 1
                    pc[0] += 1
                    for kc in range(8):
                        mm(PSb[ba][:], WR[:, slot, kc, off:off + 128], H[:, kc, tsl], kc == 0, kc == 7,
                           [WRt[slot], Ht[kc][t]], [PSt[ba]], kc == 7)
                    for kc in range(8):
                        mm(PSb[bb_][:], WR[:, slot, kc, off + 128:off + 256], H[:, kc, tsl], kc == 0, kc == 7,
                           [WRt[slot], Ht[kc][t]], [PSt[bb_]], kc == 7)
                    s_ap, s_t = nextF()
                    act(AF.Sigmoid, s_ap[:], PSb[bb_][:], [PSt[bb_], vecT], [s_t], bias=vcol(c_bin + 2 * p + 1))
                    stt("dve", U[:, p, 32 + t * TW:32 + (t + 1) * TW], PSb[ba][:], vcol(c_bin + 2 * p), s_ap[:],
                        ALU.add, ALU.mult, [PSt[ba], s_t, vecT], [Ut[p][t]])
                sch.dma("sp", ud[j, p * 128:(p + 1) * 128, :], U[:, p, :],
                        reads=[Upad] + Ut[p], writes=[udT[p]], key="ud%d" % p)
                if p % 2 == 1:
                    w_release()
            gc = [0]
            for gi in range(8):
                if gi % 4 == 0:
                    slot = w_acquire()
                off = (gi % 4) * 128
                for t in range(NT):
                    tsl = slice(t * TW, (t + 1) * TW)
                    b = 4 + (gc[0] % 2)
                    gc[0] += 1
                    for kc in range(8):
                        mm(PSb[b][:], WR[:, slot, kc, off:off + 128], H[:, kc, tsl], kc == 0, kc == 7,
                           [WRt[slot], Ht[kc][t]], [PSt[b]], kc == 7)
                    act(AF.Silu, G[:, gi, tsl], PSb[b][:], [PSt[b], vecT], [Gt[gi][t]], bias=vcol(c_bin + 16 + gi))
                if gi % 4 == 3:
                    w_release()
            Cv = R1[:, 0:8192].bitcast(F32)
            L = R1[:, 8192:16384].rearrange("p (g m c) -> p g m c", g=32, m=8)
            Ct = [T("c%d" % p) for p in range(8)]
            Lt = [T("L%d" % p) for p in range(8)]
            take("R1", Ct + Lt)
            w4c = vb + 64
            for p in range(8):
                sch.op("dve" if p % 2 == 0 else "pool",
                       lambda e, p=p: e.tensor_tensor(
                           out=L[:, 4 * p:4 * p + 4, :, :].rearrange("p g m c -> p (g m) c"),
                           in0=i4[:].unsqueeze(1).to_broadcast([128, 32, 32]),
                           in1=vecs[:, w4c + 32 * p:w4c + 32 * p + 32].unsqueeze(2).to_broadcast([128, 32, 32]),
                           op=ALU.mult),
                       [constT, vecT], [Lt[p]])
            NUB = 4
            UB = [R3[:, s_ * 2176:(s_ + 1) * 2176] for s_ in range(NUB)]
            UBt = [T("ub%d" % s_) for s_ in range(NUB)]
            take("R3", UBt)
            for s_ in range(NUB):
                sch.op("dve" if s_ % 2 == 0 else "pool", lambda e, s_=s_: e.memset(UB[s_][:, 0:2176], 0.0),
                       writes=[UBt[s_]])
            ws0 = w_acquire()
            ws1 = w_acquire()
            wso = [ws0, ws1]
            dcnt = 0
            oc = [0]
            yc = [0]
            mu_ap, mu_t = Fp[0], Ft[0]
            m2_ap, m2_t = Fp[1], Ft[1]
            pend = {}

            def conv_mm(n):
                t, p = divmod(n, 8)
                s_ = n % NUB
                cb = n % 2
                for jj in range(4):
                    wdt = 540 if jj < 3 else 539
                    c0 = t * TW + 2 + jj
                    sch.dma("sp",
                            UB[s_][32 * jj:32 * jj + 32, 0:2176].rearrange("p (g x) -> p g x", g=4)[:, :, 0:wdt],
                            ud[j, p * 128:(p + 1) * 128, c0:c0 + wdt].rearrange("(g c) x -> c g x", g=4),
                            reads=[udT[p]], writes=[UBt[s_]], key="ub%d" % s_)
                for m in range(8):
                    for g_ in range(4):
                        last = (m == 7 and g_ == 3)
                        out_ap = PSb[cb][32 * g_:32 * g_ + 32, :]
                        lhs = L[:, 4 * p + g_, m, :]
                        rhs = UB[s_][:, g_ * 544 + 4 * m:g_ * 544 + 4 * m + TW]
                        sch.op("pe", lambda e, out_ap=out_ap, lhs=lhs, rhs=rhs, m=m, g_=g_: e.matmul(
                            out_ap, lhsT=lhs, rhs=rhs, start=(m == 0), stop=(m == 7), tile_position=(0, 32 * g_)),
                            [Lt[p], UBt[s_]], [PSt[cb]], last)

            def evac(n):
                t, p = divmod(n, 8)
                cb = n % 2
                Cp = Cv[:, p * TW:(p + 1) * TW]
                act(AF.Identity, Cp, PSb[cb][:], [PSt[cb], vecT], [Ct[p]], bias=vcol(c_dwb + p))
                q_ap, q_t = nextB()
                act(AF.Square, q_ap[:], PSb[cb][:], [PSt[cb], vecT], [q_t], bias=vcol(c_dwb + p))
                l_ap, l_t = nextB()
                cp("dve", l_ap[:], Cp, [Ct[p]], [l_t])
                pend[n] = (l_ap, l_t, q_ap, q_t)

            def stats_mm(n):
                t, p = divmod(n, 8)
                l_ap, l_t, q_ap, q_t = pend.pop(n)
                mm(PSb[6][:], ones[:], l_ap[:], p == 0, p == 7, [onesT, l_t], [PSt[6]], True)
                mm(PSb[7][:], ones[:], q_ap[:], p == 0, p == 7, [onesT, q_t], [PSt[7]], True)

            def ln_stats():
                ts2("dve", mu_ap[:], PSb[6][:], 1.0 / D, None, ALU.mult, None, [PSt[6]], [mu_t])
                tt("dve", m2_ap[:], mu_ap[:], mu_ap[:], ALU.mult, [mu_t], [m2_t])
                stt("dve", m2_ap[:], PSb[7][:], 1.0 / D, m2_ap[:], ALU.mult, ALU.subtract, [PSt[7], m2_t], [m2_t])
                act_rsqrt(m2_ap[:], m2_ap[:], 1.0, epsl[:], [m2_t], m2_t)

            def normalize(n):
                t, p = divmod(n, 8)
                tsl = slice(t * TW, (t + 1) * TW)
                Cp = Cv[:, p * TW:(p + 1) * TW]
                y_ap, y_t = Fp[2 + yc[0] % 3], Ft[2 + yc[0] % 3]
                yc[0] += 1
                tt("dve", y_ap[:], Cp, mu_ap[:], ALU.subtract, [Ct[p], mu_t], [y_t])
                tt("dve", y_ap[:], y_ap[:], m2_ap[:], ALU.mult, [y_t, m2_t], [y_t])
                act(AF.Silu, y_ap[:], y_ap[:], [y_t, vecT], [y_t], bias=vcol(c_lnb + p), scale=vcol(c_lng + p))
                tt("pool", G[:, p, tsl], y_ap[:], G[:, p, tsl], ALU.mult, [y_t, Gt[p][t]], [Gt[p][t]])

            def outproj(t):
                tsl = slice(t * TW, (t + 1) * TW)
                for m in range(8):
                    b = 2 + (oc[0] % 4)
                    oc[0] += 1
                    slot = wso[m // 4]
                    for kc in range(8):
                        mm(PSb[b][:], WR[:, slot, kc, (m % 4) * 128:(m % 4 + 1) * 128], G[:, kc, tsl], kc == 0, kc == 7,
                           [WRt[slot], Gt[kc][t]], [PSt[b]], kc == 7)
                    stt("dve", X[:, m, tsl], PSb[b][:], vcol(c_bout + m), X[:, m, tsl], ALU.add, ALU.add,
                        [PSt[b], Xt[m][t], vecT], [Xt[m][t]])

            NCH = NT * 8
            fpool[0] = [5]
            for n in range(NCH):
                t, p = divmod(n, 8)
                conv_mm(n)
                if n >= 1:
                    stats_mm(n - 1)
                if p == 0 and t > 0:
                    ln_stats()
                if n >= 8:
                    normalize(n - 8)
                if p == 7 and t > 0:
                    outproj(t - 1)
                    if after_tile is not None:
                        after_tile(t - 1, 2 + (oc[0] % 4))
                        oc[0] += 1
                evac(n)
            stats_mm(NCH - 1)
            ln_stats()
            for n in range(NCH - 8, NCH):
                normalize(n)
            outproj(NT - 1)
            if after_tile is not None:
                after_tile(NT - 1, 2 + (oc[0] % 4))
            fpool[0] = list(range(NF))
            w_release()
            w_release()

        def finish_tile(t, bank_=None):
            if final_norm:
                rmsnorm(V_FINAL, lambda c, tsl: X[:, c, tsl], lambda c, t_: Xt[c][t_], tiles=[t], bank_=bank_)
            sch.dma("sp", yT[:, t * TW:(t + 1) * TW].rearrange("(c p) n -> p c n", p=128),
                    X[:, :, t * TW:(t + 1) * TW], reads=[Xt[c][t] for c in range(8)], key="out")

        mi = ci = 0
        for li, kind in enumerate(layer_kinds):
            if kind == "mla":
                conv_next = ci if (li + 1 < n_layers and layer_kinds[li + 1] == "conv") else None
                mla_layer(mi, conv_next)
                mi += 1
            else:
                conv_layer(ci, finish_tile if li == n_layers - 1 else None)
                ci += 1

        if layer_kinds[-1] != "conv":
            for t in range(NT):
                finish_tile(t)
        sch.wait_all("sp", "out")
        sch.emit()
    return nc


def _col8(v):
    v = np.asarray(v, np.float32)
    n = v.shape[0] // 128
    return np.ascontiguousarray(v.reshape(n, 128).T)


def _perm_half(w):
    return np.concatenate([w[..., 32:64], w[..., 0:32]], axis=-1)


def prep_shared(inp, layer_kinds):
    f32 = np.float32
    vecs = np.zeros((128, NV), f32)
    vecs[:, V_FINAL:V_FINAL + 8] = _col8(inp["final_norm_g"])
    inv = (np.float32(10000.0) ** (-(np.arange(0, 64, 2, dtype=np.float32)) / np.float32(64))).astype(f32)
    vecs[:, V_INVF] = np.tile(inv, 4)
    ph = np.zeros(128, f32)
    ph[0:64] = np.float32(math.pi / 2)
    ph[64:96] = np.float32(math.pi)
    vecs[:, V_PHASE] = ph
    out = {}
    mi = ci = 0
    for kind in layer_kinds:
        if kind == "mla":
            vb = V_MLA + mi * MLA_VW
            vecs[:, vb:vb + 8] = _col8(inp["mla_norm_g"][mi])
            vecs[:, vb + 8:vb + 11] = _col8(inp["mla_q_norm_g"][mi])
            vecs[:, vb + 11:vb + 13] = _col8(inp["mla_kv_norm_g"][mi])
            w = np.asarray(inp["mla_w_in"][mi], f32)
            kpe = w[:, 640:704]
            out["mwin%d" % mi] = np.ascontiguousarray(
                np.concatenate([w[:, 0:640], kpe, _perm_half(kpe), w[:, 704:1728]], axis=1))
            wq = np.asarray(inp["mla_w_qb"][mi], f32).reshape(384, 8, 192)
            out["mwqb%d" % mi] = np.ascontiguousarray(
                np.concatenate([wq, _perm_half(wq[:, :, 128:192])], axis=2).reshape(384, 2048))
            out["mwkvb%d" % mi] = np.ascontiguousarray(np.asarray(inp["mla_w_kvb"][mi], f32))
            out["mwout%d" % mi] = np.ascontiguousarray(np.asarray(inp["mla_w_out"][mi], f32))
            mi += 1
        else:
            vb = V_CONV + ci * CONV_VW
            vecs[:, vb:vb + 8] = _col8(inp["conv_norm_g"][ci])
            bin_ = np.asarray(inp["conv_b_in"][ci], f32)
            ba, bb, bg = _col8(bin_[0:1024]), _col8(bin_[1024:2048]), _col8(bin_[2048:3072])
            for p in range(8):
                vecs[:, vb + 8 + 2 * p] = ba[:, p]
                vecs[:, vb + 8 + 2 * p + 1] = bb[:, p]
            vecs[:, vb + 24:vb + 32] = bg
            vecs[:, vb + 32:vb + 40] = _col8(inp["conv_dw_b"][ci])
            vecs[:, vb + 40:vb + 48] = _col8(inp["conv_ln_g"][ci])
            vecs[:, vb + 48:vb + 56] = _col8(inp["conv_ln_b"][ci])
            vecs[:, vb + 56:vb + 64] = _col8(inp["conv_b_out"][ci])
            dw = np.asarray(inp["conv_dw_w"][ci], f32)
            dwp = np.concatenate([dw, np.zeros((1, 1024), f32)], axis=0)
            w4 = dwp.reshape(8, 4, 8, 4, 32).transpose(1, 4, 2, 3, 0).reshape(128, 256)
            vecs[:, vb + 64:vb + 64 + 256] = w4
            w = np.asarray(inp["conv_w_in"][ci], f32)
            cols = []
            for p in range(8):
                cols.append(w[:, p * 128:(p + 1) * 128])
                cols.append(w[:, 1024 + p * 128:1024 + (p + 1) * 128])
            cols.append(w[:, 2048:3072])
            out["cwin%d" % ci] = np.ascontiguousarray(np.concatenate(cols, axis=1))
            out["cwout%d" % ci] = np.ascontiguousarray(np.asarray(inp["conv_w_out"][ci], f32))
            ci += 1
    out["vecs"] = vecs
    out["ident"] = np.eye(128, dtype=f32)
    out["i4"] = np.tile(np.eye(32, dtype=f32), (4, 1))
    tk = np.arange(128)[:, None]
    tq = np.arange(128)[None, :]
    out["maskT"] = np.where(tk <= tq, 0.0, NEG).astype(f32)
    return out


LAYERS = ["mla", "conv", "mla", "conv"]
_CACHE = {}


def kernel(**inputs):
    x = np.asarray(inputs["x"], np.float32)
    pos = np.asarray(inputs["positions"], np.int32)
    B = x.shape[0]
    shared = prep_shared(inputs, LAYERS)
    if "nc" not in _CACHE:
        _CACHE["nc"] = build_program(LAYERS, final_norm=True)
    nc = _CACHE["nc"]
    in_maps = []
    for b in range(B):
        m = dict(shared)
        m["xT"] = np.ascontiguousarray(x[b].T)
        m["pos"] = np.ascontiguousarray(pos[b][None, :])
        in_maps.append(m)
    res = run_bass_kernel_spmd(nc, in_maps, core_ids=list(range(B)))
    out = np.stack([np.ascontiguousarray(res.results[b]["yT"].T) for b in range(B)], axis=0)
    return out.astype(np.float32)
```

```python
import math
from contextlib import ExitStack

import numpy as np
import concourse.bass as bass
import concourse.mybir as mybir
from concourse.bass_utils import run_bass_kernel_spmd

F32 = mybir.dt.float32
BF16 = mybir.dt.bfloat16
I32 = mybir.dt.int32
AF = mybir.ActivationFunctionType
ALU = mybir.AluOpType

S = 2048
D = 1024
NT = 4
TW = 512
NH = 8
SCALE = 1.0 / math.sqrt(192.0)
RMS_EPS = 1e-6
LN_EPS = 1e-5
NEG = -30000.0
TWO_PI = 2.0 * math.pi
C1 = 6.28125
C2 = TWO_PI - C1
PI_LO = 3.1415925

V_FINAL = 0
V_INVF = 8
V_PHASE = 9
V_MLA = 10
MLA_VW = 13
V_CONV = V_MLA + 2 * MLA_VW
CONV_VW = 8 + 24 + 8 + 8 + 8 + 8 + 256
NV = V_CONV + 2 * CONV_VW


class T:
    __slots__ = ("name", "w", "r")

    def __init__(self, name):
        self.name = name
        self.w = None
        self.r = {}


class Sched:
    ENGS = ("pe", "act", "dve", "pool", "sp")

    def __init__(self, nc, st):
        self.nc, self.st = nc, st
        self.prog = {e: [] for e in self.ENGS}
        self.sem = {}
        self.cnt = {}
        self.seen = {e: {} for e in self.ENGS}
        self.uid = 0

    def getsem(self, key):
        if key not in self.sem:
            self.sem[key] = self.st.enter_context(self.nc.semaphore("s_" + key))
            self.cnt[key] = 0
        return self.sem[key]

    def _deps(self, eng, reads, writes, is_dma):
        deps = {}

        def add(d, keep_same):
            if d is None:
                return
            k, v = d
            if k == eng and not keep_same:
                return
            if deps.get(k, 0) < v:
                deps[k] = v

        for t in reads:
            add(t.w, is_dma or eng != "pe")
        for t in writes:
            add(t.w, is_dma)
            for k, v in t.r.items():
                add((k, v), is_dma)
        for k, v in deps.items():
            if self.seen[eng].get(k, 0) < v:
                self.prog[eng].append(("wait", k, v))
                self.seen[eng][k] = v

    def _mark(self, d, reads, writes):
        k, v = d
        for t in reads:
            if t.r.get(k, 0) < v:
                t.r[k] = v
        for t in writes:
            t.w = d
            t.r = {}

    def op(self, eng, fn, reads=(), writes=(), sig=True):
        self._deps(eng, reads, writes, False)
        self.getsem(eng)
        val = self.cnt[eng] + 1
        if sig:
            self.cnt[eng] = val
        self.prog[eng].append(("op", fn, eng if sig else None, 1))
        self._mark((eng, val), reads, writes)

    def dma(self, q, out_ap, in_ap, reads=(), writes=(), key=None):
        if key is None:
            key = "d%d" % self.uid
            self.uid += 1
        self._deps(q, reads, writes, True)
        self.getsem(key)
        self.cnt[key] += 16
        self.prog[q].append(("op", lambda e: e.dma_start(out=out_ap, in_=in_ap), key, 16))
        d = (key, self.cnt[key])
        self._mark(d, reads, writes)
        return d

    def wait_all(self, eng, key):
        v = self.cnt[key]
        if self.seen[eng].get(key, 0) < v:
            self.prog[eng].append(("wait", key, v))
            self.seen[eng][key] = v

    def emit(self):
        blk = self.st.enter_context(self.nc.Block())

        def body(name):
            def run(e):
                for it in self.prog[name]:
                    if it[0] == "wait":
                        e.wait_ge(self.sem[it[1]], it[2])
                    else:
                        ins = it[1](e)
                        if it[2] is not None:
                            ins.then_inc(self.sem[it[2]], it[3])
            return run

        blk.tensor(body("pe"))
        blk.scalar(body("act"))
        blk.vector(body("dve"))
        blk.gpsimd(body("pool"))
        blk.sync(body("sp"))


def handoff(old, new):
    u = {}
    for t in old:
        if t.w is not None:
            k, v = t.w
            if u.get(k, 0) < v:
                u[k] = v
        for k, v in t.r.items():
            if u.get(k, 0) < v:
                u[k] = v
    for t in new:
        t.w = None
        t.r = dict(u)


def build_program(layer_kinds, final_norm=True):
    nc = bass.Bass("TRN2", target_bir_lowering=False)
    n_layers = len(layer_kinds)
    dram = {}
    dram["xT"] = nc.dram_tensor("xT", [D, S], F32, kind="ExternalInput").ap()
    dram["pos"] = nc.dram_tensor("pos", [1, S], I32, kind="ExternalInput").ap()
    dram["vecs"] = nc.dram_tensor("vecs", [128, NV], F32, kind="ExternalInput").ap()
    dram["ident"] = nc.dram_tensor("ident", [128, 128], F32, kind="ExternalInput").ap()
    dram["maskT"] = nc.dram_tensor("maskT", [128, 128], F32, kind="ExternalInput").ap()
    dram["i4"] = nc.dram_tensor("i4", [128, 32], F32, kind="ExternalInput").ap()
    n_mla = sum(1 for k in layer_kinds if k == "mla")
    n_conv = n_layers - n_mla
    for j in range(n_mla):
        dram["mwin%d" % j] = nc.dram_tensor("mwin%d" % j, [D, 1792], F32, kind="ExternalInput").ap()
        dram["mwqb%d" % j] = nc.dram_tensor("mwqb%d" % j, [384, 2048], F32, kind="ExternalInput").ap()
        dram["mwkvb%d" % j] = nc.dram_tensor("mwkvb%d" % j, [256, 2048], F32, kind="ExternalInput").ap()
        dram["mwout%d" % j] = nc.dram_tensor("mwout%d" % j, [D, D], F32, kind="ExternalInput").ap()
    for j in range(n_conv):
        dram["cwin%d" % j] = nc.dram_tensor("cwin%d" % j, [D, 3072], F32, kind="ExternalInput").ap()
        dram["cwout%d" % j] = nc.dram_tensor("cwout%d" % j, [D, D], F32, kind="ExternalInput").ap()
    yT = nc.dram_tensor("yT", [D, S], F32, kind="ExternalOutput").ap()

    ud = nc.dram_tensor("ud", [max(n_conv, 1), D, 2080], BF16, kind="Internal").ap()

    st = ExitStack()
    with st:
        def sb(name, shape, dt):
            return st.enter_context(nc.sbuf_tensor(name, shape, dt))

        X = sb("X", [128, 8, S], F32)
        R1 = sb("R1", [128, 16384], BF16)
        R2 = sb("R2", [128, 16384], BF16)
        R3 = sb("R3", [128, 16640], BF16)
        WH = sb("WH", [128, 2, 1280], BF16)
        WR = sb("WR", [128, 2, 8, 512], BF16)
        NF, NB = 6, 6
        Fp = [sb("F%d" % i, [128, TW], F32) for i in range(NF)]
        Bp = [sb("B%d" % i, [128, TW], BF16) for i in range(NB)]
        i4 = sb("i4b", [128, 32], BF16)
        ident = sb("identb", [128, 128], BF16)
        maskT = sb("maskb", [128, 128], BF16)
        ones = sb("onesb", [128, 128], BF16)
        vecs = sb("vecs_sb", [128, NV], F32)
        epsr = sb("epsr", [128, 1], F32)
        epsl = sb("epsl", [128, 1], F32)
        PSb = [st.enter_context(nc.psum_tensor("ps%d" % i, [128, TW], F32)) for i in range(8)]

        sch = Sched(nc, st)

        Xt = [[T("X%d_%d" % (c, t)) for t in range(NT)] for c in range(8)]
        PSt = [T("ps%d" % i) for i in range(8)]
        Ft = [T("F%d" % i) for i in range(NF)]
        Bt = [T("B%d" % i) for i in range(NB)]
        WRt = [T("WR0"), T("WR1")]
        WHt = [T("WH0"), T("WH1")]
        constT = T("const")
        vecT = T("vecs")
        region = {"R1": [], "R2": [], "R3": []}

        def take(rname, tiles):
            handoff(region[rname], tiles)
            region[rname] = list(tiles)

        fi = [0]
        bi = [0]

        fpool = [list(range(NF))]

        def nextF():
            pool = fpool[0]
            i = pool[fi[0] % len(pool)]
            fi[0] += 1
            return Fp[i], Ft[i]

        def nextB():
            i = bi[0] % NB
            bi[0] += 1
            return Bp[i], Bt[i]

        def mm(out, lhsT, rhs, start, stop, reads, writes, sig):
            sch.op("pe", lambda e: e.matmul(out, lhsT=lhsT, rhs=rhs, start=start, stop=stop),
                   reads, writes, sig)

        def act(func, out, in_, reads, writes, bias=None, scale=None):
            kw = {}
            if bias is not None:
                kw["bias"] = bias
            if scale is not None:
                kw["scale"] = scale
            sch.op("act", lambda e: e.activation(out=out, in_=in_, func=func, **kw), reads, writes)

        def tt(eng, out, in0, in1, op, reads, writes):
            sch.op(eng, lambda e: e.tensor_tensor(out=out, in0=in0, in1=in1, op=op), reads, writes)

        def ts2(eng, out, in0, s1, s2, op0, op1, reads, writes):
            if s2 is None:
                sch.op(eng, lambda e: e.tensor_scalar(out=out, in0=in0, scalar1=s1, scalar2=None, op0=op0),
                       reads, writes)
            else:
                sch.op(eng, lambda e: e.tensor_scalar(out=out, in0=in0, scalar1=s1, scalar2=s2, op0=op0, op1=op1),
                       reads, writes)

        def stt(eng, out, in0, scalar, in1, op0, op1, reads, writes):
            sch.op(eng, lambda e: e.scalar_tensor_tensor(out=out, in0=in0, scalar=scalar, in1=in1, op0=op0, op1=op1),
                   reads, writes)

        def cp(eng, out, in_, reads, writes):
            sch.op(eng, lambda e: e.tensor_copy(out=out, in_=in_), reads, writes)

        def recip(out, in_, reads, writes):
            sch.op("dve", lambda e: e.reciprocal(out=out, in_=in_), reads, writes)

        def vcol(c):
            return vecs[:, c:c + 1]

        def act_rsqrt(out, in_, scale, eps_ap, reads, wt):
            act(AF.Ln, out, in_, list(reads) + [epsT], [wt], bias=eps_ap, scale=scale)
            act(AF.Exp, out, out, [wt], [wt], scale=-0.5)

        def act_recip(out, in_, reads, wt):
            act(AF.Ln, out, in_, list(reads), [wt])
            act(AF.Exp, out, out, [wt], [wt], scale=-1.0)

        witems = []
        mi = ci = 0
        for kind in layer_kinds:
            if kind == "mla":
                w = dram["mwin%d" % mi]
                witems += [(w[:, 0:512], 512), (w[:, 512:1024], 512), (w[:, 1024:1536], 512), (w[:, 1536:1792], 256)]
                w = dram["mwout%d" % mi]
                witems += [(w[:, 0:512], 512), (w[:, 512:1024], 512)]
                mi += 1
            else:
                w = dram["cwin%d" % ci]
                witems += [(w[:, i * 512:(i + 1) * 512], 512) for i in range(6)]
                w = dram["cwout%d" % ci]
                witems += [(w[:, 0:512], 512), (w[:, 512:1024], 512)]
                ci += 1
        wstate = {"loaded": 0, "acq": 0, "rel": 0}

        def w_fill():
            while wstate["loaded"] < min(len(witems), wstate["rel"] + 2):
                i = wstate["loaded"]
                ap, ncols = witems[i]
                slot = i % 2
                sch.dma("pool", WR[:, slot, :, 0:ncols], ap.rearrange("(c p) n -> p c n", p=128),
                        writes=[WRt[slot]], key="wr%d" % slot)
                wstate["loaded"] += 1

        def w_acquire():
            i = wstate["acq"]
            assert i < wstate["loaded"], "weight item not loaded"
            wstate["acq"] += 1
            return i % 2

        def w_release():
            wstate["rel"] += 1
            w_fill()

        sch.dma("sp", vecs[:], dram["vecs"][:, :], writes=[vecT])
        first_pos = [True]
        posi0 = R1[:, 0:4096].bitcast(I32)
        pos0T = T("posi")
        sch.dma("sp", posi0, dram["pos"].partition_broadcast(128), writes=[pos0T])
        for t in range(NT):
            sch.dma("sp", X[:, :, t * TW:(t + 1) * TW],
                    dram["xT"][:, t * TW:(t + 1) * TW].rearrange("(c p) n -> p c n", p=128),
                    writes=[Xt[c][t] for c in range(8)])
        sch.dma("pool", ident[:], dram["ident"][:, :], writes=[constT], key="cst")
        sch.dma("pool", maskT[:], dram["maskT"][:, :], writes=[constT], key="cst")
        sch.dma("pool", i4[:], dram["i4"][:, :], writes=[constT], key="cst")
        onesT = T("ones")
        sch.op("dve", lambda e: e.memset(ones[:], 1.0), writes=[onesT])
        epsT = T("eps")
        sch.op("dve", lambda e: e.memset(epsr[:], RMS_EPS), writes=[epsT])
        sch.op("dve", lambda e: e.memset(epsl[:], LN_EPS), writes=[epsT])
        w_fill()

        def rmsnorm(gcol, dst_ap_fn, dst_t_fn, tiles=None, bank_=None):
            for t in (range(NT) if tiles is None else tiles):
                tsl = slice(t * TW, (t + 1) * TW)
                bank = (6 + (t % 2)) if bank_ is None else bank_
                for c in range(8):
                    b_ap, b_t = nextB()
                    act(AF.Square, b_ap[:], X[:, c, tsl], [Xt[c][t]], [b_t])
                    mm(PSb[bank][:], ones[:], b_ap[:], c == 0, c == 7, [b_t, onesT], [PSt[bank]], True)
                f_ap, f_t = nextF()
                act_rsqrt(f_ap[:], PSb[bank][:], 1.0 / D, epsr[:], [PSt[bank]], f_t)
                for c in range(8):
                    stt("dve", dst_ap_fn(c, tsl), X[:, c, tsl], vcol(gcol + c), f_ap[:], ALU.mult, ALU.mult,
                        [Xt[c][t], f_t, vecT], [dst_t_fn(c, t)])

        H = R1[:, :].rearrange("p (c n) -> p c n", c=8)
        G = R2[:, :].rearrange("p (c n) -> p c n", c=8)

        def mla_layer(j, conv_next):
            vb = V_MLA + j * MLA_VW
            QN = R3[:, 0:6144].rearrange("p (c n) -> p c n", c=3)
            KVN = R3[:, 6144:10240].rearrange("p (c n) -> p c n", c=2)
            KPE = R3[:, 10240:12288]
            ROPE = R3[:, 12288:16384].bitcast(F32)
            LAT = [QN[:, 0], QN[:, 1], QN[:, 2], KVN[:, 0], KVN[:, 1]]
            posi = R1[:, 0:4096].bitcast(I32)
            t1 = R1[:, 4096:8192].bitcast(F32)
            ki = R1[:, 8192:12288].bitcast(I32)
            kf = R1[:, 12288:16384].bitcast(F32)
            tmpT = [T("posi"), T("t1"), T("ki"), T("kf")]
            if first_pos[0]:
                tmpT[0] = pos0T
                region["R1"] = [pos0T]
            take("R1", tmpT[1:] if first_pos[0] else tmpT)
            if first_pos[0]:
                region["R1"] = list(tmpT)
            ropeT = T("rope")
            LATt = [[T("lat%d_%d" % (f, t)) for t in range(NT)] for f in range(5)]
            KPEt = [T("kpe%d" % t) for t in range(NT)]
            take("R3", [ropeT] + [x for r in LATt for x in r] + KPEt)
            if not first_pos[0]:
                sch.dma("sp", posi, dram["pos"].partition_broadcast(128), writes=[tmpT[0]])
            first_pos[0] = False
            cp("dve", t1, posi, [tmpT[0]], [tmpT[1]])
            ts2("dve", t1, t1, vcol(V_INVF), vcol(V_PHASE), ALU.mult, ALU.add, [tmpT[1], vecT], [tmpT[1]])
            ts2("dve", kf, t1, 1.0 / TWO_PI, None, ALU.mult, None, [tmpT[1]], [tmpT[3]])
            cp("dve", ki, kf, [tmpT[3]], [tmpT[2]])
            cp("dve", kf, ki, [tmpT[2]], [tmpT[3]])
            stt("dve", t1, kf, -C1, t1, ALU.mult, ALU.add, [tmpT[3], tmpT[1]], [tmpT[1]])
            stt("dve", t1, kf, -C2, t1, ALU.mult, ALU.add, [tmpT[3], tmpT[1]], [tmpT[1]])
            ts2("dve", t1, t1, PI_LO, -PI_LO, ALU.min, ALU.max, [tmpT[1]], [tmpT[1]])
            act(AF.Sin, ROPE, t1, [tmpT[1]], [ropeT])

            Ht = [[T("h%d_%d" % (c, t)) for t in range(NT)] for c in range(8)]
            take("R1", [x for r in Ht for x in r])
            rmsnorm(vb, lambda c, tsl: H[:, c, tsl], lambda c, t: Ht[c][t])

            RAWv = R2[:, 0:10240].bitcast(F32)
            RAWt = [[T("raw%d_%d" % (s_, f)) for f in range(5)] for s_ in range(2)]
            take("R2", [x for r in RAWt for x in r])
            s0 = w_acquire()
            s1 = w_acquire()
            wsl = [(s0, 0), (s0, 128), (s0, 256), (s0, 384), (s1, 0), (s1, 128)]
            pcnt = [0]

            def proj_bank():
                b = pcnt[0] % 4
                pcnt[0] += 1
                return b

            def rope_apply(ps_ap, ps_t, out_ap, out_t, tsl):
                a_ap, a_t = nextF()
                tt("dve", a_ap[:], ps_ap, ROPE[:, tsl], ALU.mult, [ps_t, ropeT], [a_t])
                b_ap, b_t = nextF()
                cp("dve", b_ap[0:64, :], a_ap[64:128, :], [a_t], [b_t])
                tt("dve", out_ap, a_ap[0:64, :], b_ap[0:64, :], ALU.add, [a_t, b_t], [out_t])

            for t in range(NT):
                tsl = slice(t * TW, (t + 1) * TW)
                rs = t % 2
                for f in range(6):
                    slot, off = wsl[f]
                    b = proj_bank()
                    for kc in range(8):
                        mm(PSb[b][:], WR[:, slot, kc, off:off + 128], H[:, kc, tsl], kc == 0, kc == 7,
                           [WRt[slot], Ht[kc][t]], [PSt[b]], kc == 7)
                    if f < 5:
                        raw = RAWv[:, (rs * 5 + f) * TW:(rs * 5 + f + 1) * TW]
                        act(AF.Copy, raw, PSb[b][:], [PSt[b]], [RAWt[rs][f]])
                        q_ap, q_t = nextB()
                        act(AF.Square, q_ap[:], PSb[b][:], [PSt[b]], [q_t])
                        sbank = 6 if f < 3 else 7
                        first = f in (0, 3)
                        last = f in (2, 4)
                        mm(PSb[sbank][:], ones[:], q_ap[:], first, last, [q_t, onesT], [PSt[sbank]], True)
                    else:
                        rope_apply(PSb[b][:], PSt[b], KPE[0:64, tsl], KPEt[t], tsl)
                rq_ap, rq_t = nextF()
                act_rsqrt(rq_ap[:], PSb[6][:], 1.0 / 384, epsr[:], [PSt[6]], rq_t)
                rk_ap, rk_t = nextF()
                act_rsqrt(rk_ap[:], PSb[7][:], 1.0 / 256, epsr[:], [PSt[7]], rk_t)
                for f in range(5):
                    raw = RAWv[:, (rs * 5 + f) * TW:(rs * 5 + f + 1) * TW]
                    r_ap, r_t = (rq_ap, rq_t) if f < 3 else (rk_ap, rk_t)
                    stt("dve", LAT[f][:, tsl], raw, vcol(vb + 8 + f), r_ap[:], ALU.mult, ALU.mult,
                        [RAWt[rs][f], r_t, vecT], [LATt[f][t]])

            Gt = [[T("g%d_%d" % (c, t)) for t in range(NT)] for c in range(8)]
            take("R2", [x for r in Gt for x in r])
            cur = s1
            held = [s0, s1]
            for gi in range(8):
                if gi == 0:
                    slot, off = s1, 256
                elif gi == 1:
                    slot, off = s1, 384
                elif gi < 6:
                    if gi == 2:
                        w_release()
                        s2 = w_acquire()
                    slot, off = s2, (gi - 2) * 128
                else:
                    if gi == 6:
                        w_release()
                        s3 = w_acquire()
                    slot, off = s3, (gi - 6) * 128
                for t in range(NT):
                    tsl = slice(t * TW, (t + 1) * TW)
                    b = proj_bank()
                    for kc in range(8):
                        mm(PSb[b][:], WR[:, slot, kc, off:off + 128], H[:, kc, tsl], kc == 0, kc == 7,
                           [WRt[slot], Ht[kc][t]], [PSt[b]], kc == 7)
                    act(AF.Silu, G[:, gi, tsl], PSb[b][:], [PSt[b]], [Gt[gi][t]])
                if gi == 5:
                    w_release()
            w_release()

            HB = []
            for s_ in range(2):
                base = s_ * 8192
                HB.append(dict(
                    qn=R1[:, base:base + 2048], qp=R1[:, base + 2048:base + 4096],
                    kh=R1[:, base + 4096:base + 6144],
                    vh=R1[:, base + 6144:base + 8192].rearrange("p (b n) -> p b n", n=128),
                    qnt=[T("qn%d_%d" % (s_, t)) for t in range(NT)],
                    qpt=[T("qp%d_%d" % (s_, t)) for t in range(NT)],
                    kht=[T("kh%d_%d" % (s_, t)) for t in range(NT)],
                    vht=[T("vh%d_%d" % (s_, t)) for t in range(NT)],
                ))
            take("R1", [x for hb in HB for key in ("qnt", "qpt", "kht", "vht") for x in hb[key]])
            wq_d = dram["mwqb%d" % j]
            wkv_d = dram["mwkvb%d" % j]

            def load_head_w(h):
                s_ = h % 2
                sch.dma("pool", WH[:, s_, 0:768].rearrange("p (c n) -> p c n", c=3),
                        wq_d[:, h * 256:(h + 1) * 256].rearrange("(c p) n -> p c n", p=128),
                        writes=[WHt[s_]], key="wh%d" % s_)
                sch.dma("pool", WH[:, s_, 768:1280].rearrange("p (c n) -> p c n", c=2),
                        wkv_d[:, h * 256:(h + 1) * 256].rearrange("(c p) n -> p c n", p=128),
                        writes=[WHt[s_]], key="wh%d" % s_)

            jb = [0]

            def jit_bank():
                return 7

            def proj_pieces(h):
                s_ = h % 2
                hb = HB[s_]
                wq = WH[:, s_, 0:768].rearrange("p (c n) -> p c n", c=3)
                wkv = WH[:, s_, 768:1280].rearrange("p (c n) -> p c n", c=2)
                pieces = []
                for t in range(NT):
                    tsl = slice(t * TW, (t + 1) * TW)

                    def g_qn(t=t, tsl=tsl):
                        b = jit_bank()
                        for kc in range(3):
                            mm(PSb[b][:], wq[:, kc, 0:128], QN[:, kc, tsl], kc == 0, kc == 2,
                               [WHt[s_], LATt[kc][t]], [PSt[b]], kc == 2)
                        cp("dve", hb["qn"][:, tsl], PSb[b][:], [PSt[b]], [hb["qnt"][t]])

                    def g_qp(t=t, tsl=tsl):
                        b = jit_bank()
                        for kc in range(3):
                            mm(PSb[b][:], wq[:, kc, 128:256], QN[:, kc, tsl], kc == 0, kc == 2,
                               [WHt[s_], LATt[kc][t]], [PSt[b]], kc == 2)
                        rope_apply(PSb[b][:], PSt[b], hb["qp"][0:64, tsl], hb["qpt"][t], tsl)

                    def g_k(t=t, tsl=tsl):
                        b = jit_bank()
                        for kc in range(2):
                            mm(PSb[b][:], wkv[:, kc, 0:128], KVN[:, kc, tsl], kc == 0, kc == 1,
                               [WHt[s_], LATt[3 + kc][t]], [PSt[b]], kc == 1)
                        cp("dve", hb["kh"][:, tsl], PSb[b][:], [PSt[b]], [hb["kht"][t]])

                    def g_v(t=t, tsl=tsl):
                        b = jit_bank()
                        for bb in range(4):
                            tb = 4 * t + bb
                            for kc in range(2):
                                mm(PSb[b][:, bb * 128:(bb + 1) * 128], KVN[:, kc, tb * 128:(tb + 1) * 128],
                                   wkv[:, kc, 128:256], kc == 0, kc == 1,
                                   [WHt[s_], LATt[3 + kc][t]], [PSt[b]], (bb == 3 and kc == 1))
                        cp("dve", hb["vh"][:, 4 * t:4 * t + 4, :],
                           PSb[b][:].rearrange("p (b n) -> p b n", n=128), [PSt[b]], [hb["vht"][t]])
                    pieces.append([g_qn, g_k, g_qp, g_v])
                return pieces

            def attention(h, extra, groups):
                s_ = h % 2
                hb = HB[s_]
                n_groups = max(len(groups), 1)
                blk_done = [0]
                for i in range(NT):
                    ob, sbk = 3 + 2 * (i % 2), 4 + 2 * (i % 2)
                    nblk = 4 * i + 4
                    tsl = slice(i * TW, (i + 1) * TW)

                    def c0_of(jk):
                        return max(jk - 4 * i, 0) * 128

                    def qk(jk):
                        bank = jk % 3
                        c0 = c0_of(jk)
                        diag = jk >= 4 * i
                        tq0 = i * TW + c0
                        tk = slice(jk * 128, (jk + 1) * 128)
                        mm(PSb[bank][:, c0:TW], hb["kh"][:, tk], hb["qn"][:, tq0:(i + 1) * TW], True, False,
                           [hb["kht"][jk // 4], hb["qnt"][i]], [PSt[bank]], False)
                        mm(PSb[bank][:, c0:TW], KPE[0:64, tk], hb["qp"][0:64, tq0:(i + 1) * TW], False, not diag,
                           [KPEt[jk // 4], hb["qpt"][i]], [PSt[bank]], not diag)
                        if diag:
                            mm(PSb[bank][:, c0:c0 + 128], ident[:], maskT[:], False, True,
                               [constT], [PSt[bank]], True)

                    def pv(jk):
                        bank = jk % 3
                        c0 = c0_of(jk)
                        p_ap, p_t = nextB()
                        act(AF.Exp, p_ap[:, c0:TW], PSb[bank][:, c0:TW], [PSt[bank]], [p_t], scale=SCALE)
                        mm(PSb[ob][:, c0:TW], hb["vh"][:, jk, :], p_ap[:, c0:TW], jk == 0, jk == nblk - 1,
                           [hb["vht"][jk // 4], p_t], [PSt[ob]], False)
                        mm(PSb[sbk][:, c0:TW], ones[:], p_ap[:, c0:TW], jk == 0, jk == nblk - 1,
                           [onesT, p_t], [PSt[sbk]], True)

                    qk(0)
                    if nblk > 1:
                        qk(1)
                    for jk in range(nblk):
                        if jk + 2 < nblk:
                            qk(jk + 2)
                        pv(jk)
                        blk_done[0] += 1
                        while groups and blk_done[0] * n_groups >= (n_groups - len(groups) + 1) * 40:
                            groups.pop(0)()
                    r_ap, r_t = nextF()
                    act_recip(r_ap[:], PSb[sbk][:], [PSt[sbk]], r_t)
                    u_ap, u_t = nextF()
                    tt("dve", u_ap[:], PSb[ob][:], r_ap[:], ALU.mult, [PSt[ob], r_t], [u_t])
                    tt("pool", G[:, h, tsl], u_ap[:], G[:, h, tsl], ALU.mult, [u_t, Gt[h][i]], [Gt[h][i]])
                    for fn in extra[i]:
                        fn()
                while groups:
                    groups.pop(0)()

            load_head_w(0)
            load_head_w(1)
            for p in proj_pieces(0):
                for g_ in p:
                    g_()
            for h in range(NH):
                extra = [[] for _ in range(NT)]
                groups = []
                if h + 1 < NH:
                    groups = [g_ for p in proj_pieces(h + 1) for g_ in p]
                attention(h, extra, groups)
                if h + 2 < NH:
                    load_head_w(h + 2)

            for half in range(2):
                slot = w_acquire()
                for mm_ in range(4):
                    m = half * 4 + mm_
                    for t in range(NT):
                        tsl = slice(t * TW, (t + 1) * TW)
                        b = proj_bank()
                        for kc in range(8):
                            mm(PSb[b][:], WR[:, slot, kc, mm_ * 128:(mm_ + 1) * 128], G[:, kc, tsl], kc == 0, kc == 7,
                               [WRt[slot], Gt[kc][t]], [PSt[b]], kc == 7)
                        tt("dve", X[:, m, tsl], PSb[b][:], X[:, m, tsl], ALU.add, [PSt[b], Xt[m][t]], [Xt[m][t]])
                w_release()

        def conv_layer(j, after_tile=None):
            vb = V_CONV + j * CONV_VW
            c_bin, c_dwb, c_lng, c_lnb, c_bout = vb + 8, vb + 32, vb + 40, vb + 48, vb + 56
            Ht = [[T("h%d_%d" % (c, t)) for t in range(NT)] for c in range(8)]
            take("R1", [x for r in Ht for x in r])
            rmsnorm(vb, lambda c, tsl: H[:, c, tsl], lambda c, t: Ht[c][t])
            U = R3[:, :].rearrange("p (c n) -> p c n", c=8)
            Upad = T("upad")
            Ut = [[T("u%d_%d" % (c, t)) for t in range(NT)] for c in range(8)]
            take("R3", [Upad] + [x for r in Ut for x in r])
            sch.op("dve", lambda e: e.memset(U[:, :, 0:32], 0.0), writes=[Upad])
            Gt = [[T("g%d_%d" % (c, t)) for t in range(NT)] for c in range(8)]
            take("R2", [x for r in Gt for x in r])
            udT = [T("ud%d" % p) for p in range(8)]
            pc = [0]
            for p in range(8):
                if p % 2 == 0:
                    slot = w_acquire()
                off = (p % 2) * 256
                for t in range(NT):
                    tsl = slice(t * TW, (t + 1) * TW)
                    ba = (2 * pc[0]) % 4
                    bb_ = ba + 1
                    pc[0] += 1
                    for kc in range(8):
                        mm(PSb[ba][:], WR[:, slot, kc, off:off + 128], H[:, kc, tsl], kc == 0, kc == 7,
                           [WRt[slot], Ht[kc][t]], [PSt[ba]], kc == 7)
                    for kc in range(8):
                        mm(PSb[bb_][:], WR[:, slot, kc, off + 128:off + 256], H[:, kc, tsl], kc == 0, kc == 7,
                           [WRt[slot], Ht[kc][t]], [PSt[bb_]], kc == 7)
                    s_ap, s_t = nextF()
                    act(AF.Sigmoid, s_ap[:], PSb[bb_][:], [PSt[bb_], vecT], [s_t], bias=vcol(c_bin + 2 * p + 1))
                    stt("dve", U[:, p, 32 + t * TW:32 + (t + 1) * TW], PSb[ba][:], vcol(c_bin + 2 * p), s_ap[:],
                        ALU.add, ALU.mult, [PSt[ba], s_t, vecT], [Ut[p][t]])
                sch.dma("sp", ud[j, p * 128:(p + 1) * 128, :], U[:, p, :],
                        reads=[Upad] + Ut[p], writes=[udT[p]], key="ud%d" % p)
                if p % 2 == 1:
                    w_release()
            gc = [0]
            for gi in range(8):
                if gi % 4 == 0:
                    slot = w_acquire()
                off = (gi % 4) * 128
                for t in range(NT):
                    tsl = slice(t * TW, (t + 1) * TW)
                    b = 4 + (gc[0] % 2)
                    gc[0] += 1
                    for kc in range(8):
                        mm(PSb[b][:], WR[:, slot, kc, off:off + 128], H[:, kc, tsl], kc == 0, kc == 7,
                           [WRt[slot], Ht[kc][t]], [PSt[b]], kc == 7)
                    act(AF.Silu, G[:, gi, tsl], PSb[b][:], [PSt[b], vecT], [Gt[gi][t]], bias=vcol(c_bin + 16 + gi))
                if gi % 4 == 3:
                    w_release()
            Cv = R1[:, 0:8192].bitcast(F32)
            L = R1[:, 8192:16384].rearrange("p (g m c) -> p g m c", g=32, m=8)
            Ct = [T("c%d" % p) for p in range(8)]
            Lt = [T("L%d" % p) for p in range(8)]
            take("R1", Ct + Lt)
            w4c = vb + 64
            for p in range(8):
                sch.op("dve" if p % 2 == 0 else "pool",
                       lambda e, p=p: e.tensor_tensor(
                           out=L[:, 4 * p:4 * p + 4, :, :].rearrange("p g m c -> p (g m) c"),
                           in0=i4[:].unsqueeze(1).to_broadcast([128, 32, 32]),
                           in1=vecs[:, w4c + 32 * p:w4c + 32 * p + 32].unsqueeze(2).to_broadcast([128, 32, 32]),
                           op=ALU.mult),
                       [constT, vecT], [Lt[p]])
            NUB = 4
            UB = [R3[:, s_ * 2176:(s_ + 1) * 2176] for s_ in range(NUB)]
            UBt = [T("ub%d" % s_) for s_ in range(NUB)]
            take("R3", UBt)
            for s_ in range(NUB):
                sch.op("dve" if s_ % 2 == 0 else "pool", lambda e, s_=s_: e.memset(UB[s_][:, 0:2176], 0.0),
                       writes=[UBt[s_]])
            ws0 = w_acquire()
            ws1 = w_acquire()
            wso = [ws0, ws1]
            dcnt = 0
            oc = [0]
            yc = [0]
            mu_ap, mu_t = Fp[0], Ft[0]
            m2_ap, m2_t = Fp[1], Ft[1]
            pend = {}

            def conv_mm(n):
                t, p = divmod(n, 8)
                s_ = n % NUB
                cb = n % 2
                for jj in range(4):
                    wdt = 540 if jj < 3 else 539
                    c0 = t * TW + 2 + jj
                    sch.dma("sp",
                            UB[s_][32 * jj:32 * jj + 32, 0:2176].rearrange("p (g x) -> p g x", g=4)[:, :, 0:wdt],
                            ud[j, p * 128:(p + 1) * 128, c0:c0 + wdt].rearrange("(g c) x -> c g x", g=4),
                            reads=[udT[p]], writes=[UBt[s_]], key="ub%d" % s_)
                for m in range(8):
                    for g_ in range(4):
                        last = (m == 7 and g_ == 3)
                        out_ap = PSb[cb][32 * g_:32 * g_ + 32, :]
                        lhs = L[:, 4 * p + g_, m, :]
                        rhs = UB[s_][:, g_ * 544 + 4 * m:g_ * 544 + 4 * m + TW]
                        sch.op("pe", lambda e, out_ap=out_ap, lhs=lhs, rhs=rhs, m=m, g_=g_: e.matmul(
                            out_ap, lhsT=lhs, rhs=rhs, start=(m == 0), stop=(m == 7), tile_position=(0, 32 * g_)),
                            [Lt[p], UBt[s_]], [PSt[cb]], last)

            def evac(n):
                t, p = divmod(n, 8)
                cb = n % 2
                Cp = Cv[:, p * TW:(p + 1) * TW]
                act(AF.Identity, Cp, PSb[cb][:], [PSt[cb], vecT], [Ct[p]], bias=vcol(c_dwb + p))
                q_ap, q_t = nextB()
                act(AF.Square, q_ap[:], PSb[cb][:], [PSt[cb], vecT], [q_t], bias=vcol(c_dwb + p))
                l_ap, l_t = nextB()
                cp("dve", l_ap[:], Cp, [Ct[p]], [l_t])
                pend[n] = (l_ap, l_t, q_ap, q_t)

            def stats_mm(n):
                t, p = divmod(n, 8)
                l_ap, l_t, q_ap, q_t = pend.pop(n)
                mm(PSb[6][:], ones[:], l_ap[:], p == 0, p == 7, [onesT, l_t], [PSt[6]], True)
                mm(PSb[7][:], ones[:], q_ap[:], p == 0, p == 7, [onesT, q_t], [PSt[7]], True)

            def ln_stats():
                ts2("dve", mu_ap[:], PSb[6][:], 1.0 / D, None, ALU.mult, None, [PSt[6]], [mu_t])
                tt("dve", m2_ap[:], mu_ap[:], mu_ap[:], ALU.mult, [mu_t], [m2_t])
                stt("dve", m2_ap[:], PSb[7][:], 1.0 / D, m2_ap[:], ALU.mult, ALU.subtract, [PSt[7], m2_t], [m2_t])
                act_rsqrt(m2_ap[:], m2_ap[:], 1.0, epsl[:], [m2_t], m2_t)

            ypend = {}

            def dve_norm(n):
                t, p = divmod(n, 8)
                Cp = Cv[:, p * TW:(p + 1) * TW]
                y_ap, y_t = Fp[2 + yc[0] % 3], Ft[2 + yc[0] % 3]
                yc[0] += 1
                tt("dve", y_ap[:], Cp, mu_ap[:], ALU.subtract, [Ct[p], mu_t], [y_t])
                tt("dve", y_ap[:], y_ap[:], m2_ap[:], ALU.mult, [y_t, m2_t], [y_t])
                ypend[n] = (y_ap, y_t)

            def act_fin(n):
                if n not in ypend:
                    return
                t, p = divmod(n, 8)
                tsl = slice(t * TW, (t + 1) * TW)
                y_ap, y_t = ypend.pop(n)
                act(AF.Silu, y_ap[:], y_ap[:], [y_t, vecT], [y_t], bias=vcol(c_lnb + p), scale=vcol(c_lng + p))
                tt("pool", G[:, p, tsl], y_ap[:], G[:, p, tsl], ALU.mult, [y_t, Gt[p][t]], [Gt[p][t]])

            def outproj(t):
                tsl = slice(t * TW, (t + 1) * TW)
                for m in range(8):
                    b = 2 + (oc[0] % 4)
                    oc[0] += 1
                    slot = wso[m // 4]
                    for kc in range(8):
                        mm(PSb[b][:], WR[:, slot, kc, (m % 4) * 128:(m % 4 + 1) * 128], G[:, kc, tsl], kc == 0, kc == 7,
                           [WRt[slot], Gt[kc][t]], [PSt[b]], kc == 7)
                    stt("dve", X[:, m, tsl], PSb[b][:], vcol(c_bout + m), X[:, m, tsl], ALU.add, ALU.add,
                        [PSt[b], Xt[m][t], vecT], [Xt[m][t]])

            NCH = NT * 8
            fpool[0] = [5]
            for n in range(NCH):
                t, p = divmod(n, 8)
                conv_mm(n)
                if n >= 1:
                    stats_mm(n - 1)
                if p == 0 and t > 0:
                    ln_stats()
                if n >= 9:
                    act_fin(n - 9)
                if n >= 8:
                    dve_norm(n - 8)
                if p == 7 and t > 0:
                    act_fin(n - 8)
                    outproj(t - 1)
                    if after_tile is not None:
                        after_tile(t - 1, 2 + (oc[0] % 4))
                        oc[0] += 1
                evac(n)
            stats_mm(NCH - 1)
            ln_stats()
            for n in range(NCH - 8, NCH):
                dve_norm(n)
                if n > NCH - 8:
                    act_fin(n - 1)
            act_fin(NCH - 1)
            outproj(NT - 1)
            if after_tile is not None:
                after_tile(NT - 1, 2 + (oc[0] % 4))
            fpool[0] = list(range(NF))
            w_release()
            w_release()

        def finish_tile(t, bank_=None):
            if final_norm:
                rmsnorm(V_FINAL, lambda c, tsl: X[:, c, tsl], lambda c, t_: Xt[c][t_], tiles=[t], bank_=bank_)
            sch.dma("sp", yT[:, t * TW:(t + 1) * TW].rearrange("(c p) n -> p c n", p=128),
                    X[:, :, t * TW:(t + 1) * TW], reads=[Xt[c][t] for c in range(8)], key="out")

        mi = ci = 0
        for li, kind in enumerate(layer_kinds):
            if kind == "mla":
                conv_next = ci if (li + 1 < n_layers and layer_kinds[li + 1] == "conv") else None
                mla_layer(mi, conv_next)
                mi += 1
            else:
                conv_layer(ci, finish_tile if li == n_layers - 1 else None)
                ci += 1

        if layer_kinds[-1] != "conv":
            for t in range(NT):
                finish_tile(t)
        sch.wait_all("sp", "out")
        sch.emit()
    return nc


def _col8(v):
    v = np.asarray(v, np.float32)
    n = v.shape[0] // 128
    return np.ascontiguousarray(v.reshape(n, 128).T)


def _perm_half(w):
    return np.concatenate([w[..., 32:64], w[..., 0:32]], axis=-1)


def prep_shared(inp, layer_kinds):
    f32 = np.float32
    vecs = np.zeros((128, NV), f32)
    vecs[:, V_FINAL:V_FINAL + 8] = _col8(inp["final_norm_g"])
    inv = (np.float32(10000.0) ** (-(np.arange(0, 64, 2, dtype=np.float32)) / np.float32(64))).astype(f32)
    vecs[:, V_INVF] = np.tile(inv, 4)
    ph = np.zeros(128, f32)
    ph[0:64] = np.float32(math.pi / 2)
    ph[64:96] = np.float32(math.pi)
    vecs[:, V_PHASE] = ph
    out = {}
    mi = ci = 0
    for kind in layer_kinds:
        if kind == "mla":
            vb = V_MLA + mi * MLA_VW
            vecs[:, vb:vb + 8] = _col8(inp["mla_norm_g"][mi])
            vecs[:, vb + 8:vb + 11] = _col8(inp["mla_q_norm_g"][mi])
            vecs[:, vb + 11:vb + 13] = _col8(inp["mla_kv_norm_g"][mi])
            w = np.asarray(inp["mla_w_in"][mi], f32)
            kpe = w[:, 640:704]
            out["mwin%d" % mi] = np.ascontiguousarray(
                np.concatenate([w[:, 0:640], kpe, _perm_half(kpe), w[:, 704:1728]], axis=1))
            wq = np.asarray(inp["mla_w_qb"][mi], f32).reshape(384, 8, 192)
            out["mwqb%d" % mi] = np.ascontiguousarray(
                np.concatenate([wq, _perm_half(wq[:, :, 128:192])], axis=2).reshape(384, 2048))
            out["mwkvb%d" % mi] = np.ascontiguousarray(np.asarray(inp["mla_w_kvb"][mi], f32))
            out["mwout%d" % mi] = np.ascontiguousarray(np.asarray(inp["mla_w_out"][mi], f32))
            mi += 1
        else:
            vb = V_CONV + ci * CONV_VW
            vecs[:, vb:vb + 8] = _col8(inp["conv_norm_g"][ci])
            bin_ = np.asarray(inp["conv_b_in"][ci], f32)
            ba, bb, bg = _col8(bin_[0:1024]), _col8(bin_[1024:2048]), _col8(bin_[2048:3072])
            for p in range(8):
                vecs[:, vb + 8 + 2 * p] = ba[:, p]
                vecs[:, vb + 8 + 2 * p + 1] = bb[:, p]
            vecs[:, vb + 24:vb + 32] = bg
            vecs[:, vb + 32:vb + 40] = _col8(inp["conv_dw_b"][ci])
            vecs[:, vb + 40:vb + 48] = _col8(inp["conv_ln_g"][ci])
            vecs[:, vb + 48:vb + 56] = _col8(inp["conv_ln_b"][ci])
            vecs[:, vb + 56:vb + 64] = _col8(inp["conv_b_out"][ci])
            dw = np.asarray(inp["conv_dw_w"][ci], f32)
            dwp = np.concatenate([dw, np.zeros((1, 1024), f32)], axis=0)
            w4 = dwp.reshape(8, 4, 8, 4, 32).transpose(1, 4, 2, 3, 0).reshape(128, 256)
            vecs[:, vb + 64:vb + 64 + 256] = w4
            w = np.asarray(inp["conv_w_in"][ci], f32)
            cols = []
            for p in range(8):
                cols.append(w[:, p * 128:(p + 1) * 128])
                cols.append(w[:, 1024 + p * 128:1024 + (p + 1) * 128])
            cols.append(w[:, 2048:3072])
            out["cwin%d" % ci] = np.ascontiguousarray(np.concatenate(cols, axis=1))
            out["cwout%d" % ci] = np.ascontiguousarray(np.asarray(inp["conv_w_out"][ci], f32))
            ci += 1
    out["vecs"] = vecs
    out["ident"] = np.eye(128, dtype=f32)
    out["i4"] = np.tile(np.eye(32, dtype=f32), (4, 1))
    tk = np.arange(128)[:, None]
    tq = np.arange(128)[None, :]
    out["maskT"] = np.where(tk <= tq, 0.0, NEG).astype(f32)
    return out


LAYERS = ["mla", "conv", "mla", "conv"]
_CACHE = {}


def kernel(**inputs):
    x = np.asarray(inputs["x"], np.float32)
    pos = np.asarray(inputs["positions"], np.int32)
    B = x.shape[0]
    shared = prep_shared(inputs, LAYERS)
    if "nc" not in _CACHE:
        _CACHE["nc"] = build_program(LAYERS, final_norm=True)
    nc = _CACHE["nc"]
    in_maps = []
    for b in range(B):
        m = dict(shared)
        m["xT"] = np.ascontiguousarray(x[b].T)
        m["pos"] = np.ascontiguousarray(pos[b][None, :])
        in_maps.append(m)
    res = run_bass_kernel_spmd(nc, in_maps, core_ids=list(range(B)))
    out = np.stack([np.ascontiguousarray(res.results[b]["yT"].T) for b in range(B)], axis=0)
    return out.astype(np.float32)
```

```python
import math
from contextlib import ExitStack

import numpy as np
import concourse.bass as bass
import concourse.mybir as mybir
from concourse.bass_utils import run_bass_kernel_spmd

F32 = mybir.dt.float32
BF16 = mybir.dt.bfloat16
I32 = mybir.dt.int32
AF = mybir.ActivationFunctionType
ALU = mybir.AluOpType

S = 2048
D = 1024
NT = 4
TW = 512
NH = 8
SCALE = 1.0 / math.sqrt(192.0)
RMS_EPS = 1e-6
LN_EPS = 1e-5
NEG = -30000.0
TWO_PI = 2.0 * math.pi
C1 = 6.28125
C2 = TWO_PI - C1
PI_LO = 3.1415925

V_FINAL = 0
V_INVF = 8
V_PHASE = 9
V_MLA = 10
MLA_VW = 13
V_CONV = V_MLA + 2 * MLA_VW
CONV_VW = 8 + 24 + 8 + 8 + 8 + 8 + 256
NV = V_CONV + 2 * CONV_VW


class T:
    __slots__ = ("name", "w", "r")

    def __init__(self, name):
        self.name = name
        self.w = None
        self.r = {}


class Sched:
    ENGS = ("pe", "act", "dve", "pool", "sp")

    def __init__(self, nc, st):
        self.nc, self.st = nc, st
        self.prog = {e: [] for e in self.ENGS}
        self.sem = {}
        self.cnt = {}
        self.seen = {e: {} for e in self.ENGS}
        self.uid = 0

    def getsem(self, key):
        if key not in self.sem:
            self.sem[key] = self.st.enter_context(self.nc.semaphore("s_" + key))
            self.cnt[key] = 0
        return self.sem[key]

    def _deps(self, eng, reads, writes, is_dma):
        deps = {}

        def add(d, keep_same):
            if d is None:
                return
            k, v = d
            if k == eng and not keep_same:
                return
            if deps.get(k, 0) < v:
                deps[k] = v

        for t in reads:
            add(t.w, is_dma or eng != "pe")
        for t in writes:
            add(t.w, is_dma)
            for k, v in t.r.items():
                add((k, v), is_dma)
        for k, v in deps.items():
            if self.seen[eng].get(k, 0) < v:
                self.prog[eng].append(("wait", k, v))
                self.seen[eng][k] = v

    def _mark(self, d, reads, writes):
        k, v = d
        for t in reads:
            if t.r.get(k, 0) < v:
                t.r[k] = v
        for t in writes:
            t.w = d
            t.r = {}

    def op(self, eng, fn, reads=(), writes=(), sig=True):
        self._deps(eng, reads, writes, False)
        self.getsem(eng)
        val = self.cnt[eng] + 1
        if sig:
            self.cnt[eng] = val
        self.prog[eng].append(("op", fn, eng if sig else None, 1))
        self._mark((eng, val), reads, writes)

    def dma(self, q, out_ap, in_ap, reads=(), writes=(), key=None):
        if key is None:
            key = "d%d" % self.uid
            self.uid += 1
        self._deps(q, reads, writes, True)
        self.getsem(key)
        self.cnt[key] += 16
        self.prog[q].append(("op", lambda e: e.dma_start(out=out_ap, in_=in_ap), key, 16))
        d = (key, self.cnt[key])
        self._mark(d, reads, writes)
        return d

    def wait_all(self, eng, key):
        v = self.cnt[key]
        if self.seen[eng].get(key, 0) < v:
            self.prog[eng].append(("wait", key, v))
            self.seen[eng][key] = v

    def emit(self):
        blk = self.st.enter_context(self.nc.Block())

        def body(name):
            def run(e):
                for it in self.prog[name]:
                    if it[0] == "wait":
                        e.wait_ge(self.sem[it[1]], it[2])
                    else:
                        ins = it[1](e)
                        if it[2] is not None:
                            ins.then_inc(self.sem[it[2]], it[3])
            return run

        blk.tensor(body("pe"))
        blk.scalar(body("act"))
        blk.vector(body("dve"))
        blk.gpsimd(body("pool"))
        blk.sync(body("sp"))


def handoff(old, new):
    u = {}
    for t in old:
        if t.w is not None:
            k, v = t.w
            if u.get(k, 0) < v:
                u[k] = v
        for k, v in t.r.items():
            if u.get(k, 0) < v:
                u[k] = v
    for t in new:
        t.w = None
        t.r = dict(u)


def build_program(layer_kinds, final_norm=True):
    nc = bass.Bass("TRN2", target_bir_lowering=False)
    n_layers = len(layer_kinds)
    dram = {}
    dram["xT"] = nc.dram_tensor("xT", [D, S], F32, kind="ExternalInput").ap()
    dram["pos"] = nc.dram_tensor("pos", [1, S], I32, kind="ExternalInput").ap()
    dram["vecs"] = nc.dram_tensor("vecs", [128, NV], F32, kind="ExternalInput").ap()
    dram["ident"] = nc.dram_tensor("ident", [128, 128], F32, kind="ExternalInput").ap()
    dram["maskT"] = nc.dram_tensor("maskT", [128, 128], F32, kind="ExternalInput").ap()
    dram["i4"] = nc.dram_tensor("i4", [128, 32], F32, kind="ExternalInput").ap()
    n_mla = sum(1 for k in layer_kinds if k == "mla")
    n_conv = n_layers - n_mla
    for j in range(n_mla):
        dram["mwin%d" % j] = nc.dram_tensor("mwin%d" % j, [D, 1792], F32, kind="ExternalInput").ap()
        dram["mwqb%d" % j] = nc.dram_tensor("mwqb%d" % j, [384, 2048], F32, kind="ExternalInput").ap()
        dram["mwkvb%d" % j] = nc.dram_tensor("mwkvb%d" % j, [256, 2048], F32, kind="ExternalInput").ap()
        dram["mwout%d" % j] = nc.dram_tensor("mwout%d" % j, [D, D], F32, kind="ExternalInput").ap()
    for j in range(n_conv):
        dram["cwin%d" % j] = nc.dram_tensor("cwin%d" % j, [D, 3072], F32, kind="ExternalInput").ap()
        dram["cwout%d" % j] = nc.dram_tensor("cwout%d" % j, [D, D], F32, kind="ExternalInput").ap()
    yT = nc.dram_tensor("yT", [D, S], F32, kind="ExternalOutput").ap()

    ud = nc.dram_tensor("ud", [max(n_conv, 1), D, 2080], BF16, kind="Internal").ap()

    st = ExitStack()
    with st:
        def sb(name, shape, dt):
            return st.enter_context(nc.sbuf_tensor(name, shape, dt))

        X = sb("X", [128, 8, S], F32)
        R1 = sb("R1", [128, 16384], BF16)
        R2 = sb("R2", [128, 16384], BF16)
        R3 = sb("R3", [128, 16640], BF16)
        WH = sb("WH", [128, 2, 1280], BF16)
        WR = sb("WR", [128, 2, 8, 512], BF16)
        NF, NB = 6, 6
        Fp = [sb("F%d" % i, [128, TW], F32) for i in range(NF)]
        Bp = [sb("B%d" % i, [128, TW], BF16) for i in range(NB)]
        i4 = sb("i4b", [128, 32], BF16)
        ident = sb("identb", [128, 128], BF16)
        maskT = sb("maskb", [128, 128], BF16)
        ones = sb("onesb", [128, 128], BF16)
        vecs = sb("vecs_sb", [128, NV], F32)
        epsr = sb("epsr", [128, 1], F32)
        epsl = sb("epsl", [128, 1], F32)
        PSb = [st.enter_context(nc.psum_tensor("ps%d" % i, [128, TW], F32)) for i in range(8)]

        sch = Sched(nc, st)

        Xt = [[T("X%d_%d" % (c, t)) for t in range(NT)] for c in range(8)]
        PSt = [T("ps%d" % i) for i in range(8)]
        Ft = [T("F%d" % i) for i in range(NF)]
        Bt = [T("B%d" % i) for i in range(NB)]
        WRt = [T("WR0"), T("WR1")]
        WHt = [T("WH0"), T("WH1")]
        constT = T("const")
        vecT = T("vecs")
        region = {"R1": [], "R2": [], "R3": []}

        def take(rname, tiles):
            handoff(region[rname], tiles)
            region[rname] = list(tiles)

        fi = [0]
        bi = [0]

        fpool = [list(range(NF))]

        def nextF():
            pool = fpool[0]
            i = pool[fi[0] % len(pool)]
            fi[0] += 1
            return Fp[i], Ft[i]

        def nextB():
            i = bi[0] % NB
            bi[0] += 1
            return Bp[i], Bt[i]

        def mm(out, lhsT, rhs, start, stop, reads, writes, sig):
            sch.op("pe", lambda e: e.matmul(out, lhsT=lhsT, rhs=rhs, start=start, stop=stop),
                   reads, writes, sig)

        def act(func, out, in_, reads, writes, bias=None, scale=None):
            kw = {}
            if bias is not None:
                kw["bias"] = bias
            if scale is not None:
                kw["scale"] = scale
            sch.op("act", lambda e: e.activation(out=out, in_=in_, func=func, **kw), reads, writes)

        def tt(eng, out, in0, in1, op, reads, writes):
            sch.op(eng, lambda e: e.tensor_tensor(out=out, in0=in0, in1=in1, op=op), reads, writes)

        def ts2(eng, out, in0, s1, s2, op0, op1, reads, writes):
            if s2 is None:
                sch.op(eng, lambda e: e.tensor_scalar(out=out, in0=in0, scalar1=s1, scalar2=None, op0=op0),
                       reads, writes)
            else:
                sch.op(eng, lambda e: e.tensor_scalar(out=out, in0=in0, scalar1=s1, scalar2=s2, op0=op0, op1=op1),
                       reads, writes)

        def stt(eng, out, in0, scalar, in1, op0, op1, reads, writes):
            sch.op(eng, lambda e: e.scalar_tensor_tensor(out=out, in0=in0, scalar=scalar, in1=in1, op0=op0, op1=op1),
                   reads, writes)

        def cp(eng, out, in_, reads, writes):
            sch.op(eng, lambda e: e.tensor_copy(out=out, in_=in_), reads, writes)

        def recip(out, in_, reads, writes):
            sch.op("dve", lambda e: e.reciprocal(out=out, in_=in_), reads, writes)

        def vcol(c):
            return vecs[:, c:c + 1]

        def act_rsqrt(out, in_, scale, eps_ap, reads, wt):
            act(AF.Ln, out, in_, list(reads) + [epsT], [wt], bias=eps_ap, scale=scale)
            act(AF.Exp, out, out, [wt], [wt], scale=-0.5)

        def act_recip(out, in_, reads, wt):
            act(AF.Ln, out, in_, list(reads), [wt])
            act(AF.Exp, out, out, [wt], [wt], scale=-1.0)

        witems = []
        mi = ci = 0
        for kind in layer_kinds:
            if kind == "mla":
                w = dram["mwin%d" % mi]
                witems += [(w[:, 0:512], 512), (w[:, 512:1024], 512), (w[:, 1024:1536], 512), (w[:, 1536:1792], 256)]
                w = dram["mwout%d" % mi]
                witems += [(w[:, 0:512], 512), (w[:, 512:1024], 512)]
                mi += 1
            else:
                w = dram["cwin%d" % ci]
                witems += [(w[:, i * 512:(i + 1) * 512], 512) for i in range(6)]
                w = dram["cwout%d" % ci]
                witems += [(w[:, 0:512], 512), (w[:, 512:1024], 512)]
                ci += 1
        wstate = {"loaded": 0, "acq": 0, "rel": 0}

        def w_fill():
            while wstate["loaded"] < min(len(witems), wstate["rel"] + 2):
                i = wstate["loaded"]
                ap, ncols = witems[i]
                slot = i % 2
                sch.dma("pool", WR[:, slot, :, 0:ncols], ap.rearrange("(c p) n -> p c n", p=128),
                        writes=[WRt[slot]], key="wr%d" % slot)
                wstate["loaded"] += 1

        def w_acquire():
            i = wstate["acq"]
            assert i < wstate["loaded"], "weight item not loaded"
            wstate["acq"] += 1
            return i % 2

        def w_release():
            wstate["rel"] += 1
            w_fill()

        sch.dma("sp", vecs[:], dram["vecs"][:, :], writes=[vecT])
        first_pos = [True]
        posi0 = R1[:, 0:4096].bitcast(I32)
        pos0T = T("posi")
        sch.dma("sp", posi0, dram["pos"].partition_broadcast(128), writes=[pos0T])
        for t in range(NT):
            sch.dma("sp", X[:, :, t * TW:(t + 1) * TW],
                    dram["xT"][:, t * TW:(t + 1) * TW].rearrange("(c p) n -> p c n", p=128),
                    writes=[Xt[c][t] for c in range(8)])
        sch.dma("pool", ident[:], dram["ident"][:, :], writes=[constT], key="cst")
        sch.dma("pool", maskT[:], dram["maskT"][:, :], writes=[constT], key="cst")
        sch.dma("pool", i4[:], dram["i4"][:, :], writes=[constT], key="cst")
        onesT = T("ones")
        sch.op("dve", lambda e: e.memset(ones[:], 1.0), writes=[onesT])
        epsT = T("eps")
        sch.op("dve", lambda e: e.memset(epsr[:], RMS_EPS), writes=[epsT])
        sch.op("dve", lambda e: e.memset(epsl[:], LN_EPS), writes=[epsT])
        w_fill()

        def rmsnorm(gcol, dst_ap_fn, dst_t_fn, tiles=None, bank_=None):
            for t in (range(NT) if tiles is None else tiles):
                tsl = slice(t * TW, (t + 1) * TW)
                bank = (6 + (t % 2)) if bank_ is None else bank_
                for c in range(8):
                    b_ap, b_t = nextB()
                    act(AF.Square, b_ap[:], X[:, c, tsl], [Xt[c][t]], [b_t])
                    mm(PSb[bank][:], ones[:], b_ap[:], c == 0, c == 7, [b_t, onesT], [PSt[bank]], True)
                f_ap, f_t = nextF()
                act_rsqrt(f_ap[:], PSb[bank][:], 1.0 / D, epsr[:], [PSt[bank]], f_t)
                for c in range(8):
                    stt("dve", dst_ap_fn(c, tsl), X[:, c, tsl], vcol(gcol + c), f_ap[:], ALU.mult, ALU.mult,
                        [Xt[c][t], f_t, vecT], [dst_t_fn(c, t)])

        H = R1[:, :].rearrange("p (c n) -> p c n", c=8)
        G = R2[:, :].rearrange("p (c n) -> p c n", c=8)

        def mla_layer(j, conv_next):
            vb = V_MLA + j * MLA_VW
            QN = R3[:, 0:6144].rearrange("p (c n) -> p c n", c=3)
            KVN = R3[:, 6144:10240].rearrange("p (c n) -> p c n", c=2)
            KPE = R3[:, 10240:12288]
            ROPE = R3[:, 12288:16384].bitcast(F32)
            LAT = [QN[:, 0], QN[:, 1], QN[:, 2], KVN[:, 0], KVN[:, 1]]
            posi = R1[:, 0:4096].bitcast(I32)
            t1 = R1[:, 4096:8192].bitcast(F32)
            ki = R1[:, 8192:12288].bitcast(I32)
            kf = R1[:, 12288:16384].bitcast(F32)
            tmpT = [T("posi"), T("t1"), T("ki"), T("kf")]
            if first_pos[0]:
                tmpT[0] = pos0T
                region["R1"] = [pos0T]
            take("R1", tmpT[1:] if first_pos[0] else tmpT)
            if first_pos[0]:
                region["R1"] = list(tmpT)
            ropeT = T("rope")
            LATt = [[T("lat%d_%d" % (f, t)) for t in range(NT)] for f in range(5)]
            KPEt = [T("kpe%d" % t) for t in range(NT)]
            take("R3", [ropeT] + [x for r in LATt for x in r] + KPEt)
            if not first_pos[0]:
                sch.dma("sp", posi, dram["pos"].partition_broadcast(128), writes=[tmpT[0]])
            first_pos[0] = False
            cp("dve", t1, posi, [tmpT[0]], [tmpT[1]])
            ts2("dve", t1, t1, vcol(V_INVF), vcol(V_PHASE), ALU.mult, ALU.add, [tmpT[1], vecT], [tmpT[1]])
            ts2("dve", kf, t1, 1.0 / TWO_PI, None, ALU.mult, None, [tmpT[1]], [tmpT[3]])
            cp("dve", ki, kf, [tmpT[3]], [tmpT[2]])
            cp("dve", kf, ki, [tmpT[2]], [tmpT[3]])
            stt("dve", t1, kf, -C1, t1, ALU.mult, ALU.add, [tmpT[3], tmpT[1]], [tmpT[1]])
            stt("dve", t1, kf, -C2, t1, ALU.mult, ALU.add, [tmpT[3], tmpT[1]], [tmpT[1]])
            ts2("dve", t1, t1, PI_LO, -PI_LO, ALU.min, ALU.max, [tmpT[1]], [tmpT[1]])
            act(AF.Sin, ROPE, t1, [tmpT[1]], [ropeT])

            Ht = [[T("h%d_%d" % (c, t)) for t in range(NT)] for c in range(8)]
            take("R1", [x for r in Ht for x in r])
            rmsnorm(vb, lambda c, tsl: H[:, c, tsl], lambda c, t: Ht[c][t])

            RAWv = R2[:, 0:10240].bitcast(F32)
            RAWt = [[T("raw%d_%d" % (s_, f)) for f in range(5)] for s_ in range(2)]
            take("R2", [x for r in RAWt for x in r])
            s0 = w_acquire()
            s1 = w_acquire()
            wsl = [(s0, 0), (s0, 128), (s0, 256), (s0, 384), (s1, 0), (s1, 128)]
            pcnt = [0]

            def proj_bank():
                b = pcnt[0] % 4
                pcnt[0] += 1
                return b

            def rope_apply(ps_ap, ps_t, out_ap, out_t, tsl):
                a_ap, a_t = nextF()
                tt("dve", a_ap[:], ps_ap, ROPE[:, tsl], ALU.mult, [ps_t, ropeT], [a_t])
                b_ap, b_t = nextF()
                cp("dve", b_ap[0:64, :], a_ap[64:128, :], [a_t], [b_t])
                tt("dve", out_ap, a_ap[0:64, :], b_ap[0:64, :], ALU.add, [a_t, b_t], [out_t])

            for t in range(NT):
                tsl = slice(t * TW, (t + 1) * TW)
                rs = t % 2
                for f in range(6):
                    slot, off = wsl[f]
                    b = proj_bank()
                    for kc in range(8):
                        mm(PSb[b][:], WR[:, slot, kc, off:off + 128], H[:, kc, tsl], kc == 0, kc == 7,
                           [WRt[slot], Ht[kc][t]], [PSt[b]], kc == 7)
                    if f < 5:
                        raw = RAWv[:, (rs * 5 + f) * TW:(rs * 5 + f + 1) * TW]
                        act(AF.Copy, raw, PSb[b][:], [PSt[b]], [RAWt[rs][f]])
                        q_ap, q_t = nextB()
                        act(AF.Square, q_ap[:], PSb[b][:], [PSt[b]], [q_t])
                        sbank = 6 if f < 3 else 7
                        first = f in (0, 3)
                        last = f in (2, 4)
                        mm(PSb[sbank][:], ones[:], q_ap[:], first, last, [q_t, onesT], [PSt[sbank]], True)
                    else:
                        rope_apply(PSb[b][:], PSt[b], KPE[0:64, tsl], KPEt[t], tsl)
                rq_ap, rq_t = nextF()
                act_rsqrt(rq_ap[:], PSb[6][:], 1.0 / 384, epsr[:], [PSt[6]], rq_t)
                rk_ap, rk_t = nextF()
                act_rsqrt(rk_ap[:], PSb[7][:], 1.0 / 256, epsr[:], [PSt[7]], rk_t)
                for f in range(5):
                    raw = RAWv[:, (rs * 5 + f) * TW:(rs * 5 + f + 1) * TW]
                    r_ap, r_t = (rq_ap, rq_t) if f < 3 else (rk_ap, rk_t)
                    stt("dve", LAT[f][:, tsl], raw, vcol(vb + 8 + f), r_ap[:], ALU.mult, ALU.mult,
                        [RAWt[rs][f], r_t, vecT], [LATt[f][t]])

            Gt = [[T("g%d_%d" % (c, t)) for t in range(NT)] for c in range(8)]
            take("R2", [x for r in Gt for x in r])
            cur = s1
            held = [s0, s1]
            for gi in range(8):
                if gi == 0:
                    slot, off = s1, 256
                elif gi == 1:
                    slot, off = s1, 384
                elif gi < 6:
                    if gi == 2:
                        w_release()
                        s2 = w_acquire()
                    slot, off = s2, (gi - 2) * 128
                else:
                    if gi == 6:
                        w_release()
                        s3 = w_acquire()
                    slot, off = s3, (gi - 6) * 128
                for t in range(NT):
                    tsl = slice(t * TW, (t + 1) * TW)
                    b = proj_bank()
                    for kc in range(8):
                        mm(PSb[b][:], WR[:, slot, kc, off:off + 128], H[:, kc, tsl], kc == 0, kc == 7,
                           [WRt[slot], Ht[kc][t]], [PSt[b]], kc == 7)
                    act(AF.Silu, G[:, gi, tsl], PSb[b][:], [PSt[b]], [Gt[gi][t]])
                if gi == 5:
                    w_release()
            w_release()

            HB = []
            for s_ in range(2):
                base = s_ * 8192
                HB.append(dict(
                    qn=R1[:, base:base + 2048], qp=R1[:, base + 2048:base + 4096],
                    kh=R1[:, base + 4096:base + 6144],
                    vh=R1[:, base + 6144:base + 8192].rearrange("p (b n) -> p b n", n=128),
                    qnt=[T("qn%d_%d" % (s_, t)) for t in range(NT)],
                    qpt=[T("qp%d_%d" % (s_, t)) for t in range(NT)],
                    kht=[T("kh%d_%d" % (s_, t)) for t in range(NT)],
                    vht=[T("vh%d_%d" % (s_, t)) for t in range(NT)],
                ))
            take("R1", [x for hb in HB for key in ("qnt", "qpt", "kht", "vht") for x in hb[key]])
            wq_d = dram["mwqb%d" % j]
            wkv_d = dram["mwkvb%d" % j]

            def load_head_w(h):
                s_ = h % 2
                sch.dma("pool", WH[:, s_, 0:768].rearrange("p (c n) -> p c n", c=3),
                        wq_d[:, h * 256:(h + 1) * 256].rearrange("(c p) n -> p c n", p=128),
                        writes=[WHt[s_]], key="wh%d" % s_)
                sch.dma("pool", WH[:, s_, 768:1280].rearrange("p (c n) -> p c n", c=2),
                        wkv_d[:, h * 256:(h + 1) * 256].rearrange("(c p) n -> p c n", p=128),
                        writes=[WHt[s_]], key="wh%d" % s_)

            jb = [0]

            def jit_bank():
                return 7

            def proj_pieces(h):
                s_ = h % 2
                hb = HB[s_]
                wq = WH[:, s_, 0:768].rearrange("p (c n) -> p c n", c=3)
                wkv = WH[:, s_, 768:1280].rearrange("p (c n) -> p c n", c=2)
                pieces = []
                for t in range(NT):
                    tsl = slice(t * TW, (t + 1) * TW)

                    def g_qn(t=t, tsl=tsl):
                        b = jit_bank()
                        for kc in range(3):
                            mm(PSb[b][:], wq[:, kc, 0:128], QN[:, kc, tsl], kc == 0, kc == 2,
                               [WHt[s_], LATt[kc][t]], [PSt[b]], kc == 2)
                        cp("dve", hb["qn"][:, tsl], PSb[b][:], [PSt[b]], [hb["qnt"][t]])

                    def g_qp(t=t, tsl=tsl):
                        b = jit_bank()
                        for kc in range(3):
                            mm(PSb[b][:], wq[:, kc, 128:256], QN[:, kc, tsl], kc == 0, kc == 2,
                               [WHt[s_], LATt[kc][t]], [PSt[b]], kc == 2)
                        rope_apply(PSb[b][:], PSt[b], hb["qp"][0:64, tsl], hb["qpt"][t], tsl)

                    def g_k(t=t, tsl=tsl):
                        b = jit_bank()
                        for kc in range(2):
                            mm(PSb[b][:], wkv[:, kc, 0:128], KVN[:, kc, tsl], kc == 0, kc == 1,
                               [WHt[s_], LATt[3 + kc][t]], [PSt[b]], kc == 1)
                        cp("dve", hb["kh"][:, tsl], PSb[b][:], [PSt[b]], [hb["kht"][t]])

                    def g_v(t=t, tsl=tsl):
                        b = jit_bank()
                        for bb in range(4):
                            tb = 4 * t + bb
                            for kc in range(2):
                                mm(PSb[b][:, bb * 128:(bb + 1) * 128], KVN[:, kc, tb * 128:(tb + 1) * 128],
                                   wkv[:, kc, 128:256], kc == 0, kc == 1,
                                   [WHt[s_], LATt[3 + kc][t]], [PSt[b]], (bb == 3 and kc == 1))
                        cp("dve", hb["vh"][:, 4 * t:4 * t + 4, :],
                           PSb[b][:].rearrange("p (b n) -> p b n", n=128), [PSt[b]], [hb["vht"][t]])
                    pieces.append([g_qn, g_k, g_qp, g_v])
                return pieces

            def attention(h, extra, groups):
                s_ = h % 2
                hb = HB[s_]
                n_groups = max(len(groups), 1)
                blk_done = [0]
                for i in range(NT):
                    ob, sbk = 3 + 2 * (i % 2), 4 + 2 * (i % 2)
                    nblk = 4 * i + 4
                    tsl = slice(i * TW, (i + 1) * TW)

                    def c0_of(jk):
                        return max(jk - 4 * i, 0) * 128

                    def qk(jk):
                        bank = jk % 3
                        c0 = c0_of(jk)
                        diag = jk >= 4 * i
                        tq0 = i * TW + c0
                        tk = slice(jk * 128, (jk + 1) * 128)
                        mm(PSb[bank][:, c0:TW], hb["kh"][:, tk], hb["qn"][:, tq0:(i + 1) * TW], True, False,
                           [hb["kht"][jk // 4], hb["qnt"][i]], [PSt[bank]], False)
                        mm(PSb[bank][:, c0:TW], KPE[0:64, tk], hb["qp"][0:64, tq0:(i + 1) * TW], False, not diag,
                           [KPEt[jk // 4], hb["qpt"][i]], [PSt[bank]], not diag)
                        if diag:
                            mm(PSb[bank][:, c0:c0 + 128], ident[:], maskT[:], False, True,
                               [constT], [PSt[bank]], True)

                    def pv(jk):
                        bank = jk % 3
                        c0 = c0_of(jk)
                        p_ap, p_t = nextB()
                        act(AF.Exp, p_ap[:, c0:TW], PSb[bank][:, c0:TW], [PSt[bank]], [p_t], scale=SCALE)
                        mm(PSb[ob][:, c0:TW], hb["vh"][:, jk, :], p_ap[:, c0:TW], jk == 0, jk == nblk - 1,
                           [hb["vht"][jk // 4], p_t], [PSt[ob]], False)
                        mm(PSb[sbk][:, c0:TW], ones[:], p_ap[:, c0:TW], jk == 0, jk == nblk - 1,
                           [onesT, p_t], [PSt[sbk]], True)

                    qk(0)
                    if nblk > 1:
                        qk(1)
                    for jk in range(nblk):
                        if jk + 2 < nblk:
                            qk(jk + 2)
                        pv(jk)
                        blk_done[0] += 1
                        while groups and blk_done[0] * n_groups >= (n_groups - len(groups) + 1) * 40:
                            groups.pop(0)()
                    r_ap, r_t = nextF()
                    act_recip(r_ap[:], PSb[sbk][:], [PSt[sbk]], r_t)
                    u_ap, u_t = nextF()
                    tt("dve", u_ap[:], PSb[ob][:], r_ap[:], ALU.mult, [PSt[ob], r_t], [u_t])
                    tt("pool", G[:, h, tsl], u_ap[:], G[:, h, tsl], ALU.mult, [u_t, Gt[h][i]], [Gt[h][i]])
                    for fn in extra[i]:
                        fn()
                while groups:
                    groups.pop(0)()

            load_head_w(0)
            load_head_w(1)
            for p in proj_pieces(0):
                for g_ in p:
                    g_()
            for h in range(NH):
                extra = [[] for _ in range(NT)]
                groups = []
                if h + 1 < NH:
                    groups = [g_ for p in proj_pieces(h + 1) for g_ in p]
                attention(h, extra, groups)
                if h + 2 < NH:
                    load_head_w(h + 2)

            for half in range(2):
                slot = w_acquire()
                for mm_ in range(4):
                    m = half * 4 + mm_
                    for t in range(NT):
                        tsl = slice(t * TW, (t + 1) * TW)
                        b = proj_bank()
                        for kc in range(8):
                            mm(PSb[b][:], WR[:, slot, kc, mm_ * 128:(mm_ + 1) * 128], G[:, kc, tsl], kc == 0, kc == 7,
                               [WRt[slot], Gt[kc][t]], [PSt[b]], kc == 7)
                        tt("dve", X[:, m, tsl], PSb[b][:], X[:, m, tsl], ALU.add, [PSt[b], Xt[m][t]], [Xt[m][t]])
                w_release()

        def conv_layer(j, after_tile=None):
            vb = V_CONV + j * CONV_VW
            c_bin, c_dwb, c_lng, c_lnb, c_bout = vb + 8, vb + 32, vb + 40, vb + 48, vb + 56
            Ht = [[T("h%d_%d" % (c, t)) for t in range(NT)] for c in range(8)]
            take("R1", [x for r in Ht for x in r])
            rmsnorm(vb, lambda c, tsl: H[:, c, tsl], lambda c, t: Ht[c][t])
            U = R3[:, :].rearrange("p (c n) -> p c n", c=8)
            Upad = T("upad")
            Ut = [[T("u%d_%d" % (c, t)) for t in range(NT)] for c in range(8)]
            take("R3", [Upad] + [x for r in Ut for x in r])
            sch.op("dve", lambda e: e.memset(U[:, :, 0:32], 0.0), writes=[Upad])
            Gt = [[T("g%d_%d" % (c, t)) for t in range(NT)] for c in range(8)]
            take("R2", [x for r in Gt for x in r])
            udT = [T("ud%d" % p) for p in range(8)]
            pc = [0]
            for p in range(8):
                if p % 2 == 0:
                    slot = w_acquire()
                off = (p % 2) * 256
                for t in range(NT):
                    tsl = slice(t * TW, (t + 1) * TW)
                    ba = (2 * pc[0]) % 4
                    bb_ = ba + 1
                    pc[0] += 1
                    for kc in range(8):
                        mm(PSb[ba][:], WR[:, slot, kc, off:off + 128], H[:, kc, tsl], kc == 0, kc == 7,
                           [WRt[slot], Ht[kc][t]], [PSt[ba]], kc == 7)
                    for kc in range(8):
                        mm(PSb[bb_][:], WR[:, slot, kc, off + 128:off + 256], H[:, kc, tsl], kc == 0, kc == 7,
                           [WRt[slot], Ht[kc][t]], [PSt[bb_]], kc == 7)
                    s_ap, s_t = nextF()
                    act(AF.Sigmoid, s_ap[:], PSb[bb_][:], [PSt[bb_], vecT], [s_t], bias=vcol(c_bin + 2 * p + 1))
                    stt("dve", U[:, p, 32 + t * TW:32 + (t + 1) * TW], PSb[ba][:], vcol(c_bin + 2 * p), s_ap[:],
                        ALU.add, ALU.mult, [PSt[ba], s_t, vecT], [Ut[p][t]])
                sch.dma("sp", ud[j, p * 128:(p + 1) * 128, :], U[:, p, :],
                        reads=[Upad] + Ut[p], writes=[udT[p]], key="ud%d" % p)
                if p % 2 == 1:
                    w_release()
            gc = [0]
            for gi in range(8):
                if gi % 4 == 0:
                    slot = w_acquire()
                off = (gi % 4) * 128
                for t in range(NT):
                    tsl = slice(t * TW, (t + 1) * TW)
                    b = 4 + (gc[0] % 2)
                    gc[0] += 1
                    for kc in range(8):
                        mm(PSb[b][:], WR[:, slot, kc, off:off + 128], H[:, kc, tsl], kc == 0, kc == 7,
                           [WRt[slot], Ht[kc][t]], [PSt[b]], kc == 7)
                    act(AF.Silu, G[:, gi, tsl], PSb[b][:], [PSt[b], vecT], [Gt[gi][t]], bias=vcol(c_bin + 16 + gi))
                if gi % 4 == 3:
                    w_release()
            Cv = R1[:, 0:8192].bitcast(F32)
            L = R1[:, 8192:16384].rearrange("p (g m c) -> p g m c", g=32, m=8)
            Ct = [T("c%d" % p) for p in range(8)]
            Lt = [T("L%d" % p) for p in range(8)]
            take("R1", Ct + Lt)
            w4c = vb + 64
            for p in range(8):
                sch.op("dve" if p % 2 == 0 else "pool",
                       lambda e, p=p: e.tensor_tensor(
                           out=L[:, 4 * p:4 * p + 4, :, :].rearrange("p g m c -> p (g m) c"),
                           in0=i4[:].unsqueeze(1).to_broadcast([128, 32, 32]),
                           in1=vecs[:, w4c + 32 * p:w4c + 32 * p + 32].unsqueeze(2).to_broadcast([128, 32, 32]),
                           op=ALU.mult),
                       [constT, vecT], [Lt[p]])
            NUB = 4
            UB = [R3[:, s_ * 2176:(s_ + 1) * 2176] for s_ in range(NUB)]
            UBt = [[T("ub%d_%d" % (s_, jj)) for jj in range(4)] for s_ in range(NUB)]
            take("R3", [x for r in UBt for x in r])
            for s_ in range(NUB):
                sch.op("dve" if s_ % 2 == 0 else "pool", lambda e, s_=s_: e.memset(UB[s_][:, 0:2176], 0.0),
                       writes=UBt[s_])
            ws0 = w_acquire()
            ws1 = w_acquire()
            wso = [ws0, ws1]
            dcnt = 0
            oc = [0]
            yc = [0]
            mu_ap, mu_t = Fp[0], Ft[0]
            m2_ap, m2_t = Fp[1], Ft[1]
            pend = {}

            def conv_mm(n):
                t, p = divmod(n, 8)
                s_ = n % NUB
                cb = n % 2
                for jj in range(4):
                    wdt = 540 if jj < 3 else 539
                    c0 = t * TW + 2 + jj
                    sch.dma("sp",
                            UB[s_][32 * jj:32 * jj + 32, 0:2176].rearrange("p (g x) -> p g x", g=4)[:, :, 0:wdt],
                            ud[j, p * 128:(p + 1) * 128, c0:c0 + wdt].rearrange("(g c) x -> c g x", g=4),
                            reads=[udT[p]], writes=[UBt[s_][jj]], key="ub%d_%d" % (s_, jj))
                for m in range(8):
                    for g_ in range(4):
                        last = (m == 7 and g_ == 3)
                        out_ap = PSb[cb][32 * g_:32 * g_ + 32, :]
                        lhs = L[:, 4 * p + g_, m, :]
                        rhs = UB[s_][:, g_ * 544 + 4 * m:g_ * 544 + 4 * m + TW]
                        sch.op("pe", lambda e, out_ap=out_ap, lhs=lhs, rhs=rhs, m=m, g_=g_: e.matmul(
                            out_ap, lhsT=lhs, rhs=rhs, start=(m == 0), stop=(m == 7), tile_position=(0, 32 * g_)),
                            [Lt[p]] + UBt[s_], [PSt[cb]], last)

            def evac(n):
                t, p = divmod(n, 8)
                cb = n % 2
                Cp = Cv[:, p * TW:(p + 1) * TW]
                act(AF.Identity, Cp, PSb[cb][:], [PSt[cb], vecT], [Ct[p]], bias=vcol(c_dwb + p))
                q_ap, q_t = nextB()
                act(AF.Square, q_ap[:], PSb[cb][:], [PSt[cb], vecT], [q_t], bias=vcol(c_dwb + p))
                l_ap, l_t = nextB()
                cp("dve", l_ap[:], Cp, [Ct[p]], [l_t])
                pend[n] = (l_ap, l_t, q_ap, q_t)

            def stats_mm(n):
                t, p = divmod(n, 8)
                l_ap, l_t, q_ap, q_t = pend.pop(n)
                mm(PSb[6][:], ones[:], l_ap[:], p == 0, p == 7, [onesT, l_t], [PSt[6]], True)
                mm(PSb[7][:], ones[:], q_ap[:], p == 0, p == 7, [onesT, q_t], [PSt[7]], True)

            def ln_stats():
                ts2("dve", mu_ap[:], PSb[6][:], 1.0 / D, None, ALU.mult, None, [PSt[6]], [mu_t])
                tt("dve", m2_ap[:], mu_ap[:], mu_ap[:], ALU.mult, [mu_t], [m2_t])
                stt("dve", m2_ap[:], PSb[7][:], 1.0 / D, m2_ap[:], ALU.mult, ALU.subtract, [PSt[7], m2_t], [m2_t])
                act_rsqrt(m2_ap[:], m2_ap[:], 1.0, epsl[:], [m2_t], m2_t)

            ypend = {}

            def dve_norm(n):
                t, p = divmod(n, 8)
                Cp = Cv[:, p * TW:(p + 1) * TW]
                y_ap, y_t = Fp[2 + yc[0] % 3], Ft[2 + yc[0] % 3]
                yc[0] += 1
                tt("dve", y_ap[:], Cp, mu_ap[:], ALU.subtract, [Ct[p], mu_t], [y_t])
                tt("dve", y_ap[:], y_ap[:], m2_ap[:], ALU.mult, [y_t, m2_t], [y_t])
                ypend[n] = (y_ap, y_t)

            def act_fin(n):
                if n not in ypend:
                    return
                t, p = divmod(n, 8)
                tsl = slice(t * TW, (t + 1) * TW)
                y_ap, y_t = ypend.pop(n)
                act(AF.Silu, y_ap[:], y_ap[:], [y_t, vecT], [y_t], bias=vcol(c_lnb + p), scale=vcol(c_lng + p))
                tt("pool", G[:, p, tsl], y_ap[:], G[:, p, tsl], ALU.mult, [y_t, Gt[p][t]], [Gt[p][t]])

            def outproj(t):
                tsl = slice(t * TW, (t + 1) * TW)
                for m in range(8):
                    b = 2 + (oc[0] % 4)
                    oc[0] += 1
                    slot = wso[m // 4]
                    for kc in range(8):
                        mm(PSb[b][:], WR[:, slot, kc, (m % 4) * 128:(m % 4 + 1) * 128], G[:, kc, tsl], kc == 0, kc == 7,
                           [WRt[slot], Gt[kc][t]], [PSt[b]], kc == 7)
                    stt("dve", X[:, m, tsl], PSb[b][:], vcol(c_bout + m), X[:, m, tsl], ALU.add, ALU.add,
                        [PSt[b], Xt[m][t], vecT], [Xt[m][t]])

            NCH = NT * 8
            fpool[0] = [5]
            for n in range(NCH):
                t, p = divmod(n, 8)
                conv_mm(n)
                if n >= 1:
                    stats_mm(n - 1)
                if p == 0 and t > 0:
                    ln_stats()
                if n >= 9:
                    act_fin(n - 9)
                if n >= 8:
                    dve_norm(n - 8)
                if p == 7 and t > 0:
                    act_fin(n - 8)
                    outproj(t - 1)
                    if after_tile is not None:
                        after_tile(t - 1, 2 + (oc[0] % 4))
                        oc[0] += 1
                evac(n)
            stats_mm(NCH - 1)
            ln_stats()
            for n in range(NCH - 8, NCH):
                dve_norm(n)
                if n > NCH - 8:
                    act_fin(n - 1)
            act_fin(NCH - 1)
            outproj(NT - 1)
            if after_tile is not None:
                after_tile(NT - 1, 2 + (oc[0] % 4))
            fpool[0] = list(range(NF))
            w_release()
            w_release()

        def finish_tile(t, bank_=None):
            if final_norm:
                rmsnorm(V_FINAL, lambda c, tsl: X[:, c, tsl], lambda c, t_: Xt[c][t_], tiles=[t], bank_=bank_)
            sch.dma("sp", yT[:, t * TW:(t + 1) * TW].rearrange("(c p) n -> p c n", p=128),
                    X[:, :, t * TW:(t + 1) * TW], reads=[Xt[c][t] for c in range(8)], key="out")

        mi = ci = 0
        for li, kind in enumerate(layer_kinds):
            if kind == "mla":
                conv_next = ci if (li + 1 < n_layers and layer_kinds[li + 1] == "conv") else None
                mla_layer(mi, conv_next)
                mi += 1
            else:
                conv_layer(ci, finish_tile if li == n_layers - 1 else None)
                ci += 1

        if layer_kinds[-1] != "conv":
            for t in range(NT):
                finish_tile(t)
        sch.wait_all("sp", "out")
        sch.emit()
    return nc


def _col8(v):
    v = np.asarray(v, np.float32)
    n = v.shape[0] // 128
    return np.ascontiguousarray(v.reshape(n, 128).T)


def _perm_half(w):
    return np.concatenate([w[..., 32:64], w[..., 0:32]], axis=-1)


def prep_shared(inp, layer_kinds):
    f32 = np.float32
    vecs = np.zeros((128, NV), f32)
    vecs[:, V_FINAL:V_FINAL + 8] = _col8(inp["final_norm_g"])
    inv = (np.float32(10000.0) ** (-(np.arange(0, 64, 2, dtype=np.float32)) / np.float32(64))).astype(f32)
    vecs[:, V_INVF] = np.tile(inv, 4)
    ph = np.zeros(128, f32)
    ph[0:64] = np.float32(math.pi / 2)
    ph[64:96] = np.float32(math.pi)
    vecs[:, V_PHASE] = ph
    out = {}
    mi = ci = 0
    for kind in layer_kinds:
        if kind == "mla":
            vb = V_MLA + mi * MLA_VW
            vecs[:, vb:vb + 8] = _col8(inp["mla_norm_g"][mi])
            vecs[:, vb + 8:vb + 11] = _col8(inp["mla_q_norm_g"][mi])
            vecs[:, vb + 11:vb + 13] = _col8(inp["mla_kv_norm_g"][mi])
            w = np.asarray(inp["mla_w_in"][mi], f32)
            kpe = w[:, 640:704]
            out["mwin%d" % mi] = np.ascontiguousarray(
                np.concatenate([w[:, 0:640], kpe, _perm_half(kpe), w[:, 704:1728]], axis=1))
            wq = np.asarray(inp["mla_w_qb"][mi], f32).reshape(384, 8, 192)
            out["mwqb%d" % mi] = np.ascontiguousarray(
                np.concatenate([wq, _perm_half(wq[:, :, 128:192])], axis=2).reshape(384, 2048))
            out["mwkvb%d" % mi] = np.ascontiguousarray(np.asarray(inp["mla_w_kvb"][mi], f32))
            out["mwout%d" % mi] = np.ascontiguousarray(np.asarray(inp["mla_w_out"][mi], f32))
            mi += 1
        else:
            vb = V_CONV + ci * CONV_VW
            vecs[:, vb:vb + 8] = _col8(inp["conv_norm_g"][ci])
            bin_ = np.asarray(inp["conv_b_in"][ci], f32)
            ba, bb, bg = _col8(bin_[0:1024]), _col8(bin_[1024:2048]), _col8(bin_[2048:3072])
            for p in range(8):
                vecs[:, vb + 8 + 2 * p] = ba[:, p]
                vecs[:, vb + 8 + 2 * p + 1] = bb[:, p]
            vecs[:, vb + 24:vb + 32] = bg
            vecs[:, vb + 32:vb + 40] = _col8(inp["conv_dw_b"][ci])
            vecs[:, vb + 40:vb + 48] = _col8(inp["conv_ln_g"][ci])
            vecs[:, vb + 48:vb + 56] = _col8(inp["conv_ln_b"][ci])
            vecs[:, vb + 56:vb + 64] = _col8(inp["conv_b_out"][ci])
            dw = np.asarray(inp["conv_dw_w"][ci], f32)
            dwp = np.concatenate([dw, np.zeros((1, 1024), f32)], axis=0)
            w4 = dwp.reshape(8, 4, 8, 4, 32).transpose(1, 4, 2, 3, 0).reshape(128, 256)
            vecs[:, vb + 64:vb + 64 + 256] = w4
            w = np.asarray(inp["conv_w_in"][ci], f32)
            cols = []
            for p in range(8):
                cols.append(w[:, p * 128:(p + 1) * 128])
                cols.append(w[:, 1024 + p * 128:1024 + (p + 1) * 128])
            cols.append(w[:, 2048:3072])
            out["cwin%d" % ci] = np.ascontiguousarray(np.concatenate(cols, axis=1))
            out["cwout%d" % ci] = np.ascontiguousarray(np.asarray(inp["conv_w_out"][ci], f32))
            ci += 1
    out["vecs"] = vecs
    out["ident"] = np.eye(128, dtype=f32)
    out["i4"] = np.tile(np.eye(32, dtype=f32), (4, 1))
    tk = np.arange(128)[:, None]
    tq = np.arange(128)[None, :]
    out["maskT"] = np.where(tk <= tq, 0.0, NEG).astype(f32)
    return out


LAYERS = ["mla", "conv", "mla", "conv"]
_CACHE = {}


def kernel(**inputs):
    x = np.asarray(inputs["x"], np.float32)
    pos = np.asarray(inputs["positions"], np.int32)
    B = x.shape[0]
    shared = prep_shared(inputs, LAYERS)
    if "nc" not in _CACHE:
        _CACHE["nc"] = build_program(LAYERS, final_norm=True)
    nc = _CACHE["nc"]
    in_maps = []
    for b in range(B):
        m = dict(shared)
        m["xT"] = np.ascontiguousarray(x[b].T)
        m["pos"] = np.ascontiguousarray(pos[b][None, :])
        in_maps.append(m)
    res = run_bass_kernel_spmd(nc, in_maps, core_ids=list(range(B)))
    out = np.stack([np.ascontiguousarray(res.results[b]["yT"].T) for b in range(B)], axis=0)
    return out.astype(np.float32)
```

```python
import math
from contextlib import ExitStack

import numpy as np
import concourse.bass as bass
import concourse.mybir as mybir
from concourse.bass_utils import run_bass_kernel_spmd

F32 = mybir.dt.float32
BF16 = mybir.dt.bfloat16
I32 = mybir.dt.int32
AF = mybir.ActivationFunctionType
ALU = mybir.AluOpType

S = 2048
D = 1024
NT = 4
TW = 512
NH = 8
SCALE = 1.0 / math.sqrt(192.0)
RMS_EPS = 1e-6
LN_EPS = 1e-5
NEG = -30000.0
TWO_PI = 2.0 * math.pi
C1 = 6.28125
C2 = TWO_PI - C1
PI_LO = 3.1415925

V_FINAL = 0
V_INVF = 8
V_PHASE = 9
V_MLA = 10
MLA_VW = 13
V_CONV = V_MLA + 2 * MLA_VW
CONV_VW = 8 + 24 + 8 + 8 + 8 + 8 + 256
NV = V_CONV + 2 * CONV_VW


class T:
    __slots__ = ("name", "w", "r")

    def __init__(self, name):
        self.name = name
        self.w = None
        self.r = {}


class Sched:
    ENGS = ("pe", "act", "dve", "pool", "sp")

    def __init__(self, nc, st):
        self.nc, self.st = nc, st
        self.prog = {e: [] for e in self.ENGS}
        self.sem = {}
        self.cnt = {}
        self.seen = {e: {} for e in self.ENGS}
        self.uid = 0

    def getsem(self, key):
        if key not in self.sem:
            self.sem[key] = self.st.enter_context(self.nc.semaphore("s_" + key))
            self.cnt[key] = 0
        return self.sem[key]

    def _deps(self, eng, reads, writes, is_dma):
        deps = {}

        def add(d, keep_same):
            if d is None:
                return
            k, v = d
            if k == eng and not keep_same:
                return
            if deps.get(k, 0) < v:
                deps[k] = v

        for t in reads:
            add(t.w, is_dma or eng != "pe")
        for t in writes:
            add(t.w, is_dma)
            for k, v in t.r.items():
                add((k, v), is_dma)
        for k, v in deps.items():
            if self.seen[eng].get(k, 0) < v:
                self.prog[eng].append(("wait", k, v))
                self.seen[eng][k] = v

    def _mark(self, d, reads, writes):
        k, v = d
        for t in reads:
            if t.r.get(k, 0) < v:
                t.r[k] = v
        for t in writes:
            t.w = d
            t.r = {}

    def op(self, eng, fn, reads=(), writes=(), sig=True):
        self._deps(eng, reads, writes, False)
        self.getsem(eng)
        val = self.cnt[eng] + 1
        if sig:
            self.cnt[eng] = val
        self.prog[eng].append(("op", fn, eng if sig else None, 1))
        self._mark((eng, val), reads, writes)

    def dma(self, q, out_ap, in_ap, reads=(), writes=(), key=None):
        if key is None:
            key = "d%d" % self.uid
            self.uid += 1
        self._deps(q, reads, writes, True)
        self.getsem(key)
        self.cnt[key] += 16
        self.prog[q].append(("op", lambda e: e.dma_start(out=out_ap, in_=in_ap), key, 16))
        d = (key, self.cnt[key])
        self._mark(d, reads, writes)
        return d

    def wait_all(self, eng, key):
        v = self.cnt[key]
        if self.seen[eng].get(key, 0) < v:
            self.prog[eng].append(("wait", key, v))
            self.seen[eng][key] = v

    def emit(self):
        blk = self.st.enter_context(self.nc.Block())

        def body(name):
            def run(e):
                for it in self.prog[name]:
                    if it[0] == "wait":
                        e.wait_ge(self.sem[it[1]], it[2])
                    else:
                        ins = it[1](e)
                        if it[2] is not None:
                            ins.then_inc(self.sem[it[2]], it[3])
            return run

        blk.tensor(body("pe"))
        blk.scalar(body("act"))
        blk.vector(body("dve"))
        blk.gpsimd(body("pool"))
        blk.sync(body("sp"))


def handoff(old, new):
    u = {}
    for t in old:
        if t.w is not None:
            k, v = t.w
            if u.get(k, 0) < v:
                u[k] = v
        for k, v in t.r.items():
            if u.get(k, 0) < v:
                u[k] = v
    for t in new:
        t.w = None
        t.r = dict(u)


def build_program(layer_kinds, final_norm=True):
    nc = bass.Bass("TRN2", target_bir_lowering=False)
    n_layers = len(layer_kinds)
    dram = {}
    dram["xT"] = nc.dram_tensor("xT", [D, S], F32, kind="ExternalInput").ap()
    dram["pos"] = nc.dram_tensor("pos", [1, S], I32, kind="ExternalInput").ap()
    dram["vecs"] = nc.dram_tensor("vecs", [128, NV], F32, kind="ExternalInput").ap()
    dram["ident"] = nc.dram_tensor("ident", [128, 128], F32, kind="ExternalInput").ap()
    dram["maskT"] = nc.dram_tensor("maskT", [128, 128], F32, kind="ExternalInput").ap()
    dram["i4"] = nc.dram_tensor("i4", [128, 32], F32, kind="ExternalInput").ap()
    n_mla = sum(1 for k in layer_kinds if k == "mla")
    n_conv = n_layers - n_mla
    for j in range(n_mla):
        dram["mwin%d" % j] = nc.dram_tensor("mwin%d" % j, [D, 1792], F32, kind="ExternalInput").ap()
        dram["mwqb%d" % j] = nc.dram_tensor("mwqb%d" % j, [384, 2048], F32, kind="ExternalInput").ap()
        dram["mwkvb%d" % j] = nc.dram_tensor("mwkvb%d" % j, [256, 2048], F32, kind="ExternalInput").ap()
        dram["mwout%d" % j] = nc.dram_tensor("mwout%d" % j, [D, D], F32, kind="ExternalInput").ap()
    for j in range(n_conv):
        dram["cwin%d" % j] = nc.dram_tensor("cwin%d" % j, [D, 3072], F32, kind="ExternalInput").ap()
        dram["cwout%d" % j] = nc.dram_tensor("cwout%d" % j, [D, D], F32, kind="ExternalInput").ap()
    yT = nc.dram_tensor("yT", [D, S], F32, kind="ExternalOutput").ap()

    ud = nc.dram_tensor("ud", [max(n_conv, 1), D, 2080], BF16, kind="Internal").ap()

    st = ExitStack()
    with st:
        def sb(name, shape, dt):
            return st.enter_context(nc.sbuf_tensor(name, shape, dt))

        X = sb("X", [128, 8, S], F32)
        R1 = sb("R1", [128, 16384], BF16)
        R2 = sb("R2", [128, 16384], BF16)
        R3 = sb("R3", [128, 16640], BF16)
        WH = sb("WH", [128, 2, 1280], BF16)
        WR = sb("WR", [128, 2, 8, 512], BF16)
        NF, NB = 6, 6
        Fp = [sb("F%d" % i, [128, TW], F32) for i in range(NF)]
        Bp = [sb("B%d" % i, [128, TW], BF16) for i in range(NB)]
        i4 = sb("i4b", [128, 32], BF16)
        ident = sb("identb", [128, 128], BF16)
        maskT = sb("maskb", [128, 128], BF16)
        ones = sb("onesb", [128, 128], BF16)
        vecs = sb("vecs_sb", [128, NV], F32)
        epsr = sb("epsr", [128, 1], F32)
        epsl = sb("epsl", [128, 1], F32)
        PSb = [st.enter_context(nc.psum_tensor("ps%d" % i, [128, TW], F32)) for i in range(8)]

        sch = Sched(nc, st)

        Xt = [[T("X%d_%d" % (c, t)) for t in range(NT)] for c in range(8)]
        PSt = [T("ps%d" % i) for i in range(8)]
        Ft = [T("F%d" % i) for i in range(NF)]
        Bt = [T("B%d" % i) for i in range(NB)]
        WRt = [T("WR0"), T("WR1")]
        WHt = [T("WH0"), T("WH1")]
        constT = T("const")
        vecT = T("vecs")
        region = {"R1": [], "R2": [], "R3": []}

        def take(rname, tiles):
            handoff(region[rname], tiles)
            region[rname] = list(tiles)

        fi = [0]
        bi = [0]

        fpool = [list(range(NF))]

        def nextF():
            pool = fpool[0]
            i = pool[fi[0] % len(pool)]
            fi[0] += 1
            return Fp[i], Ft[i]

        def nextB():
            i = bi[0] % NB
            bi[0] += 1
            return Bp[i], Bt[i]

        def mm(out, lhsT, rhs, start, stop, reads, writes, sig):
            sch.op("pe", lambda e: e.matmul(out, lhsT=lhsT, rhs=rhs, start=start, stop=stop),
                   reads, writes, sig)

        def act(func, out, in_, reads, writes, bias=None, scale=None):
            kw = {}
            if bias is not None:
                kw["bias"] = bias
            if scale is not None:
                kw["scale"] = scale
            sch.op("act", lambda e: e.activation(out=out, in_=in_, func=func, **kw), reads, writes)

        def tt(eng, out, in0, in1, op, reads, writes):
            sch.op(eng, lambda e: e.tensor_tensor(out=out, in0=in0, in1=in1, op=op), reads, writes)

        def ts2(eng, out, in0, s1, s2, op0, op1, reads, writes):
            if s2 is None:
                sch.op(eng, lambda e: e.tensor_scalar(out=out, in0=in0, scalar1=s1, scalar2=None, op0=op0),
                       reads, writes)
            else:
                sch.op(eng, lambda e: e.tensor_scalar(out=out, in0=in0, scalar1=s1, scalar2=s2, op0=op0, op1=op1),
                       reads, writes)

        def stt(eng, out, in0, scalar, in1, op0, op1, reads, writes):
            sch.op(eng, lambda e: e.scalar_tensor_tensor(out=out, in0=in0, scalar=scalar, in1=in1, op0=op0, op1=op1),
                   reads, writes)

        def cp(eng, out, in_, reads, writes):
            sch.op(eng, lambda e: e.tensor_copy(out=out, in_=in_), reads, writes)

        def recip(out, in_, reads, writes):
            sch.op("dve", lambda e: e.reciprocal(out=out, in_=in_), reads, writes)

        def vcol(c):
            return vecs[:, c:c + 1]

        def act_rsqrt(out, in_, scale, eps_ap, reads, wt):
            act(AF.Ln, out, in_, list(reads) + [epsT], [wt], bias=eps_ap, scale=scale)
            act(AF.Exp, out, out, [wt], [wt], scale=-0.5)

        def act_recip(out, in_, reads, wt):
            act(AF.Ln, out, in_, list(reads), [wt])
            act(AF.Exp, out, out, [wt], [wt], scale=-1.0)

        witems = []
        mi = ci = 0
        for kind in layer_kinds:
            if kind == "mla":
                w = dram["mwin%d" % mi]
                witems += [(w[:, 0:512], 512), (w[:, 512:1024], 512), (w[:, 1024:1536], 512), (w[:, 1536:1792], 256)]
                w = dram["mwout%d" % mi]
                witems += [(w[:, 0:512], 512), (w[:, 512:1024], 512)]
                mi += 1
            else:
                w = dram["cwin%d" % ci]
                witems += [(w[:, i * 512:(i + 1) * 512], 512) for i in range(6)]
                w = dram["cwout%d" % ci]
                witems += [(w[:, 0:512], 512), (w[:, 512:1024], 512)]
                ci += 1
        wstate = {"loaded": 0, "acq": 0, "rel": 0}

        def w_fill():
            while wstate["loaded"] < min(len(witems), wstate["rel"] + 2):
                i = wstate["loaded"]
                ap, ncols = witems[i]
                slot = i % 2
                sch.dma("pool", WR[:, slot, :, 0:ncols], ap.rearrange("(c p) n -> p c n", p=128),
                        writes=[WRt[slot]], key="wr%d" % slot)
                wstate["loaded"] += 1

        def w_acquire():
            i = wstate["acq"]
            assert i < wstate["loaded"], "weight item not loaded"
            wstate["acq"] += 1
            return i % 2

        def w_release():
            wstate["rel"] += 1
            w_fill()

        sch.dma("sp", vecs[:], dram["vecs"][:, :], writes=[vecT])
        first_pos = [True]
        posi0 = R1[:, 0:4096].bitcast(I32)
        pos0T = T("posi")
        sch.dma("sp", posi0, dram["pos"].partition_broadcast(128), writes=[pos0T])
        for t in range(NT):
            sch.dma("sp", X[:, :, t * TW:(t + 1) * TW],
                    dram["xT"][:, t * TW:(t + 1) * TW].rearrange("(c p) n -> p c n", p=128),
                    writes=[Xt[c][t] for c in range(8)])
        cT = [T("c_ident"), T("c_mask"), T("c_i4")]
        sch.dma("pool", ident[:], dram["ident"][:, :], writes=[cT[0]], key="cst0")
        sch.dma("pool", maskT[:], dram["maskT"][:, :], writes=[cT[1]], key="cst1")
        sch.dma("pool", i4[:], dram["i4"][:, :], writes=[cT[2]], key="cst2")
        onesT = T("ones")
        sch.op("dve", lambda e: e.memset(ones[:], 1.0), writes=[onesT])
        epsT = T("eps")
        sch.op("dve", lambda e: e.memset(epsr[:], RMS_EPS), writes=[epsT])
        sch.op("dve", lambda e: e.memset(epsl[:], LN_EPS), writes=[epsT])
        w_fill()

        def rmsnorm(gcol, dst_ap_fn, dst_t_fn, tiles=None, bank_=None):
            for t in (range(NT) if tiles is None else tiles):
                tsl = slice(t * TW, (t + 1) * TW)
                bank = (6 + (t % 2)) if bank_ is None else bank_
                for c in range(8):
                    b_ap, b_t = nextB()
                    act(AF.Square, b_ap[:], X[:, c, tsl], [Xt[c][t]], [b_t])
                    mm(PSb[bank][:], ones[:], b_ap[:], c == 0, c == 7, [b_t, onesT], [PSt[bank]], True)
                f_ap, f_t = nextF()
                act_rsqrt(f_ap[:], PSb[bank][:], 1.0 / D, epsr[:], [PSt[bank]], f_t)
                for c in range(8):
                    stt("dve", dst_ap_fn(c, tsl), X[:, c, tsl], vcol(gcol + c), f_ap[:], ALU.mult, ALU.mult,
                        [Xt[c][t], f_t, vecT], [dst_t_fn(c, t)])

        H = R1[:, :].rearrange("p (c n) -> p c n", c=8)
        G = R2[:, :].rearrange("p (c n) -> p c n", c=8)

        def mla_layer(j, conv_next):
            vb = V_MLA + j * MLA_VW
            QN = R3[:, 0:6144].rearrange("p (c n) -> p c n", c=3)
            KVN = R3[:, 6144:10240].rearrange("p (c n) -> p c n", c=2)
            KPE = R3[:, 10240:12288]
            ROPE = R3[:, 12288:16384].bitcast(F32)
            LAT = [QN[:, 0], QN[:, 1], QN[:, 2], KVN[:, 0], KVN[:, 1]]
            wq_d = dram["mwqb%d" % j]
            wkv_d = dram["mwkvb%d" % j]

            def load_head_w(h):
                s_ = h % 2
                sch.dma("pool", WH[:, s_, 0:768].rearrange("p (c n) -> p c n", c=3),
                        wq_d[:, h * 256:(h + 1) * 256].rearrange("(c p) n -> p c n", p=128),
                        writes=[WHt[s_]], key="wh%d" % s_)
                sch.dma("pool", WH[:, s_, 768:1280].rearrange("p (c n) -> p c n", c=2),
                        wkv_d[:, h * 256:(h + 1) * 256].rearrange("(c p) n -> p c n", p=128),
                        writes=[WHt[s_]], key="wh%d" % s_)

            load_head_w(0)
            load_head_w(1)
            posi = R1[:, 0:4096].bitcast(I32)
            t1 = R1[:, 4096:8192].bitcast(F32)
            ki = R1[:, 8192:12288].bitcast(I32)
            kf = R1[:, 12288:16384].bitcast(F32)
            tmpT = [T("posi"), T("t1"), T("ki"), T("kf")]
            if first_pos[0]:
                tmpT[0] = pos0T
                region["R1"] = [pos0T]
            take("R1", tmpT[1:] if first_pos[0] else tmpT)
            if first_pos[0]:
                region["R1"] = list(tmpT)
            ropeT = T("rope")
            LATt = [[T("lat%d_%d" % (f, t)) for t in range(NT)] for f in range(5)]
            KPEt = [T("kpe%d" % t) for t in range(NT)]
            take("R3", [ropeT] + [x for r in LATt for x in r] + KPEt)
            if not first_pos[0]:
                sch.dma("sp", posi, dram["pos"].partition_broadcast(128), writes=[tmpT[0]])
            first_pos[0] = False
            cp("dve", t1, posi, [tmpT[0]], [tmpT[1]])
            ts2("dve", t1, t1, vcol(V_INVF), vcol(V_PHASE), ALU.mult, ALU.add, [tmpT[1], vecT], [tmpT[1]])
            ts2("dve", kf, t1, 1.0 / TWO_PI, None, ALU.mult, None, [tmpT[1]], [tmpT[3]])
            cp("dve", ki, kf, [tmpT[3]], [tmpT[2]])
            cp("dve", kf, ki, [tmpT[2]], [tmpT[3]])
            stt("dve", t1, kf, -C1, t1, ALU.mult, ALU.add, [tmpT[3], tmpT[1]], [tmpT[1]])
            stt("dve", t1, kf, -C2, t1, ALU.mult, ALU.add, [tmpT[3], tmpT[1]], [tmpT[1]])
            ts2("dve", t1, t1, PI_LO, -PI_LO, ALU.min, ALU.max, [tmpT[1]], [tmpT[1]])
            act(AF.Sin, ROPE, t1, [tmpT[1]], [ropeT])

            Ht = [[T("h%d_%d" % (c, t)) for t in range(NT)] for c in range(8)]
            take("R1", [x for r in Ht for x in r])
            rmsnorm(vb, lambda c, tsl: H[:, c, tsl], lambda c, t: Ht[c][t])

            RAWv = R2[:, 0:10240].bitcast(F32)
            RAWt = [[T("raw%d_%d" % (s_, f)) for f in range(5)] for s_ in range(2)]
            take("R2", [x for r in RAWt for x in r])
            s0 = w_acquire()
            s1 = w_acquire()
            wsl = [(s0, 0), (s0, 128), (s0, 256), (s0, 384), (s1, 0), (s1, 128)]
            pcnt = [0]

            def proj_bank():
                b = pcnt[0] % 4
                pcnt[0] += 1
                return b

            def rope_apply(ps_ap, ps_t, out_ap, out_t, tsl):
                a_ap, a_t = nextF()
                tt("dve", a_ap[:], ps_ap, ROPE[:, tsl], ALU.mult, [ps_t, ropeT], [a_t])
                b_ap, b_t = nextF()
                cp("dve", b_ap[0:64, :], a_ap[64:128, :], [a_t], [b_t])
                tt("dve", out_ap, a_ap[0:64, :], b_ap[0:64, :], ALU.add, [a_t, b_t], [out_t])

            for t in range(NT):
                tsl = slice(t * TW, (t + 1) * TW)
                rs = t % 2
                for f in range(6):
                    slot, off = wsl[f]
                    b = proj_bank()
                    for kc in range(8):
                        mm(PSb[b][:], WR[:, slot, kc, off:off + 128], H[:, kc, tsl], kc == 0, kc == 7,
                           [WRt[slot], Ht[kc][t]], [PSt[b]], kc == 7)
                    if f < 5:
                        raw = RAWv[:, (rs * 5 + f) * TW:(rs * 5 + f + 1) * TW]
                        act(AF.Copy, raw, PSb[b][:], [PSt[b]], [RAWt[rs][f]])
                        q_ap, q_t = nextB()
                        act(AF.Square, q_ap[:], PSb[b][:], [PSt[b]], [q_t])
                        sbank = 6 if f < 3 else 7
                        first = f in (0, 3)
                        last = f in (2, 4)
                        mm(PSb[sbank][:], ones[:], q_ap[:], first, last, [q_t, onesT], [PSt[sbank]], True)
                    else:
                        rope_apply(PSb[b][:], PSt[b], KPE[0:64, tsl], KPEt[t], tsl)
                rq_ap, rq_t = nextF()
                act_rsqrt(rq_ap[:], PSb[6][:], 1.0 / 384, epsr[:], [PSt[6]], rq_t)
                rk_ap, rk_t = nextF()
                act_rsqrt(rk_ap[:], PSb[7][:], 1.0 / 256, epsr[:], [PSt[7]], rk_t)
                for f in range(5):
                    raw = RAWv[:, (rs * 5 + f) * TW:(rs * 5 + f + 1) * TW]
                    r_ap, r_t = (rq_ap, rq_t) if f < 3 else (rk_ap, rk_t)
                    stt("dve", LAT[f][:, tsl], raw, vcol(vb + 8 + f), r_ap[:], ALU.mult, ALU.mult,
                        [RAWt[rs][f], r_t, vecT], [LATt[f][t]])

            Gt = [[T("g%d_%d" % (c, t)) for t in range(NT)] for c in range(8)]
            take("R2", [x for r in Gt for x in r])
            cur = s1
            held = [s0, s1]
            for gi in range(8):
                if gi == 0:
                    slot, off = s1, 256
                elif gi == 1:
                    slot, off = s1, 384
                elif gi < 6:
                    if gi == 2:
                        w_release()
                        s2 = w_acquire()
                    slot, off = s2, (gi - 2) * 128
                else:
                    if gi == 6:
                        w_release()
                        s3 = w_acquire()
                    slot, off = s3, (gi - 6) * 128
                for t in range(NT):
                    tsl = slice(t * TW, (t + 1) * TW)
                    b = proj_bank()
                    for kc in range(8):
                        mm(PSb[b][:], WR[:, slot, kc, off:off + 128], H[:, kc, tsl], kc == 0, kc == 7,
                           [WRt[slot], Ht[kc][t]], [PSt[b]], kc == 7)
                    act(AF.Silu, G[:, gi, tsl], PSb[b][:], [PSt[b]], [Gt[gi][t]])
                if gi == 5:
                    w_release()
            w_release()

            HB = []
            for s_ in range(2):
                base = s_ * 8192
                HB.append(dict(
                    qn=R1[:, base:base + 2048], qp=R1[:, base + 2048:base + 4096],
                    kh=R1[:, base + 4096:base + 6144],
                    vh=R1[:, base + 6144:base + 8192].rearrange("p (b n) -> p b n", n=128),
                    qnt=[T("qn%d_%d" % (s_, t)) for t in range(NT)],
                    qpt=[T("qp%d_%d" % (s_, t)) for t in range(NT)],
                    kht=[T("kh%d_%d" % (s_, t)) for t in range(NT)],
                    vht=[T("vh%d_%d" % (s_, t)) for t in range(NT)],
                ))
            take("R1", [x for hb in HB for key in ("qnt", "qpt", "kht", "vht") for x in hb[key]])
            jb = [0]

            def jit_bank():
                return 7


            def proj_pieces(h):
                s_ = h % 2
                hb = HB[s_]
                wq = WH[:, s_, 0:768].rearrange("p (c n) -> p c n", c=3)
                wkv = WH[:, s_, 768:1280].rearrange("p (c n) -> p c n", c=2)
                pieces = []
                for t in range(NT):
                    tsl = slice(t * TW, (t + 1) * TW)

                    def g_qn(t=t, tsl=tsl):
                        b = jit_bank()
                        for kc in range(3):
                            mm(PSb[b][:], wq[:, kc, 0:128], QN[:, kc, tsl], kc == 0, kc == 2,
                               [WHt[s_], LATt[kc][t]], [PSt[b]], kc == 2)
                        cp("dve", hb["qn"][:, tsl], PSb[b][:], [PSt[b]], [hb["qnt"][t]])

                    def g_qp(t=t, tsl=tsl):
                        b = jit_bank()
                        for kc in range(3):
                            mm(PSb[b][:], wq[:, kc, 128:256], QN[:, kc, tsl], kc == 0, kc == 2,
                               [WHt[s_], LATt[kc][t]], [PSt[b]], kc == 2)
                        rope_apply(PSb[b][:], PSt[b], hb["qp"][0:64, tsl], hb["qpt"][t], tsl)

                    def g_k(t=t, tsl=tsl):
                        b = jit_bank()
                        for kc in range(2):
                            mm(PSb[b][:], wkv[:, kc, 0:128], KVN[:, kc, tsl], kc == 0, kc == 1,
                               [WHt[s_], LATt[3 + kc][t]], [PSt[b]], kc == 1)
                        cp("dve", hb["kh"][:, tsl], PSb[b][:], [PSt[b]], [hb["kht"][t]])

                    def g_v(t=t, tsl=tsl):
                        b = jit_bank()
                        for bb in range(4):
                            tb = 4 * t + bb
                            for kc in range(2):
                                mm(PSb[b][:, bb * 128:(bb + 1) * 128], KVN[:, kc, tb * 128:(tb + 1) * 128],
                                   wkv[:, kc, 128:256], kc == 0, kc == 1,
                                   [WHt[s_], LATt[3 + kc][t]], [PSt[b]], (bb == 3 and kc == 1))
                        cp("dve", hb["vh"][:, 4 * t:4 * t + 4, :],
                           PSb[b][:].rearrange("p (b n) -> p b n", n=128), [PSt[b]], [hb["vht"][t]])
                    pieces.append([g_qn, g_k, g_qp, g_v])
                return pieces

            def attention(h, extra, groups):
                s_ = h % 2
                hb = HB[s_]
                n_groups = max(len(groups), 1)
                blk_done = [0]
                for i in range(NT):
                    ob, sbk = 3 + 2 * (i % 2), 4 + 2 * (i % 2)
                    nblk = 4 * i + 4
                    tsl = slice(i * TW, (i + 1) * TW)

                    def c0_of(jk):
                        return max(jk - 4 * i, 0) * 128

                    def qk(jk):
                        bank = jk % 3
                        c0 = c0_of(jk)
                        diag = jk >= 4 * i
                        tq0 = i * TW + c0
                        tk = slice(jk * 128, (jk + 1) * 128)
                        mm(PSb[bank][:, c0:TW], hb["kh"][:, tk], hb["qn"][:, tq0:(i + 1) * TW], True, False,
                           [hb["kht"][jk // 4], hb["qnt"][i]], [PSt[bank]], False)
                        mm(PSb[bank][:, c0:TW], KPE[0:64, tk], hb["qp"][0:64, tq0:(i + 1) * TW], False, not diag,
                           [KPEt[jk // 4], hb["qpt"][i]], [PSt[bank]], not diag)
                        if diag:
                            mm(PSb[bank][:, c0:c0 + 128], ident[:], maskT[:], False, True,
                               cT, [PSt[bank]], True)

                    def pv(jk):
                        bank = jk % 3
                        c0 = c0_of(jk)
                        p_ap, p_t = nextB()
                        act(AF.Exp, p_ap[:, c0:TW], PSb[bank][:, c0:TW], [PSt[bank]], [p_t], scale=SCALE)
                        mm(PSb[ob][:, c0:TW], hb["vh"][:, jk, :], p_ap[:, c0:TW], jk == 0, jk == nblk - 1,
                           [hb["vht"][jk // 4], p_t], [PSt[ob]], False)
                        mm(PSb[sbk][:, c0:TW], ones[:], p_ap[:, c0:TW], jk == 0, jk == nblk - 1,
                           [onesT, p_t], [PSt[sbk]], True)

                    qk(0)
                    if nblk > 1:
                        qk(1)
                    for jk in range(nblk):
                        if jk + 2 < nblk:
                            qk(jk + 2)
                        pv(jk)
                        blk_done[0] += 1
                        while groups and blk_done[0] * n_groups >= (n_groups - len(groups) + 1) * 40:
                            groups.pop(0)()
                    r_ap, r_t = nextF()
                    act_recip(r_ap[:], PSb[sbk][:], [PSt[sbk]], r_t)
                    u_ap, u_t = nextF()
                    tt("dve", u_ap[:], PSb[ob][:], r_ap[:], ALU.mult, [PSt[ob], r_t], [u_t])
                    tt("dve", G[:, h, tsl], u_ap[:], G[:, h, tsl], ALU.mult, [u_t, Gt[h][i]], [Gt[h][i]])
                    for fn in extra[i]:
                        fn()
                while groups:
                    groups.pop(0)()

            for p in proj_pieces(0):
                for g_ in p:
                    g_()
            for h in range(NH):
                extra = [[] for _ in range(NT)]
                groups = []
                if h + 1 < NH:
                    groups = [g_ for p in proj_pieces(h + 1) for g_ in p]
                attention(h, extra, groups)
                if h + 2 < NH:
                    load_head_w(h + 2)

            for half in range(2):
                slot = w_acquire()
                for mm_ in range(4):
                    m = half * 4 + mm_
                    for t in range(NT):
                        tsl = slice(t * TW, (t + 1) * TW)
                        b = proj_bank()
                        for kc in range(8):
                            mm(PSb[b][:], WR[:, slot, kc, mm_ * 128:(mm_ + 1) * 128], G[:, kc, tsl], kc == 0, kc == 7,
                               [WRt[slot], Gt[kc][t]], [PSt[b]], kc == 7)
                        tt("dve", X[:, m, tsl], PSb[b][:], X[:, m, tsl], ALU.add, [PSt[b], Xt[m][t]], [Xt[m][t]])
                w_release()

        def conv_layer(j, after_tile=None):
            vb = V_CONV + j * CONV_VW
            c_bin, c_dwb, c_lng, c_lnb, c_bout = vb + 8, vb + 32, vb + 40, vb + 48, vb + 56
            Ht = [[T("h%d_%d" % (c, t)) for t in range(NT)] for c in range(8)]
            take("R1", [x for r in Ht for x in r])
            rmsnorm(vb, lambda c, tsl: H[:, c, tsl], lambda c, t: Ht[c][t])
            U = R3[:, :].rearrange("p (c n) -> p c n", c=8)
            Upad = T("upad")
            Ut = [[T("u%d_%d" % (c, t)) for t in range(NT)] for c in range(8)]
            take("R3", [Upad] + [x for r in Ut for x in r])
            sch.op("dve", lambda e: e.memset(U[:, :, 0:32], 0.0), writes=[Upad])
            Gt = [[T("g%d_%d" % (c, t)) for t in range(NT)] for c in range(8)]
            take("R2", [x for r in Gt for x in r])
            udT = [T("ud%d" % p) for p in range(8)]
            pc = [0]
            for p in range(8):
                if p % 2 == 0:
                    slot = w_acquire()
                off = (p % 2) * 256
                for t in range(NT):
                    tsl = slice(t * TW, (t + 1) * TW)
                    ba = (2 * pc[0]) % 4
                    bb_ = ba + 1
                    pc[0] += 1
                    for kc in range(8):
                        mm(PSb[ba][:], WR[:, slot, kc, off:off + 128], H[:, kc, tsl], kc == 0, kc == 7,
                           [WRt[slot], Ht[kc][t]], [PSt[ba]], kc == 7)
                    for kc in range(8):
                        mm(PSb[bb_][:], WR[:, slot, kc, off + 128:off + 256], H[:, kc, tsl], kc == 0, kc == 7,
                           [WRt[slot], Ht[kc][t]], [PSt[bb_]], kc == 7)
                    s_ap, s_t = nextF()
                    act(AF.Sigmoid, s_ap[:], PSb[bb_][:], [PSt[bb_], vecT], [s_t], bias=vcol(c_bin + 2 * p + 1))
                    stt("dve", U[:, p, 32 + t * TW:32 + (t + 1) * TW], PSb[ba][:], vcol(c_bin + 2 * p), s_ap[:],
                        ALU.add, ALU.mult, [PSt[ba], s_t, vecT], [Ut[p][t]])
                sch.dma("sp", ud[j, p * 128:(p + 1) * 128, :], U[:, p, :],
                        reads=[Upad] + Ut[p], writes=[udT[p]], key="ud%d" % p)
                if p % 2 == 1:
                    w_release()
            gc = [0]
            for gi in range(8):
                if gi % 4 == 0:
                    slot = w_acquire()
                off = (gi % 4) * 128
                for t in range(NT):
                    tsl = slice(t * TW, (t + 1) * TW)
                    b = 4 + (gc[0] % 2)
                    gc[0] += 1
                    for kc in range(8):
                        mm(PSb[b][:], WR[:, slot, kc, off:off + 128], H[:, kc, tsl], kc == 0, kc == 7,
                           [WRt[slot], Ht[kc][t]], [PSt[b]], kc == 7)
                    act(AF.Silu, G[:, gi, tsl], PSb[b][:], [PSt[b], vecT], [Gt[gi][t]], bias=vcol(c_bin + 16 + gi))
                if gi % 4 == 3:
                    w_release()
            Cv = R1[:, 0:8192].bitcast(F32)
            L = R1[:, 8192:16384].rearrange("p (g m c) -> p g m c", g=32, m=8)
            Ct = [T("c%d" % p) for p in range(8)]
            Lt = [T("L%d" % p) for p in range(8)]
            take("R1", Ct + Lt)
            w4c = vb + 64
            for p in range(8):
                sch.op("dve" if p % 2 == 0 else "pool",
                       lambda e, p=p: e.tensor_tensor(
                           out=L[:, 4 * p:4 * p + 4, :, :].rearrange("p g m c -> p (g m) c"),
                           in0=i4[:].unsqueeze(1).to_broadcast([128, 32, 32]),
                           in1=vecs[:, w4c + 32 * p:w4c + 32 * p + 32].unsqueeze(2).to_broadcast([128, 32, 32]),
                           op=ALU.mult),
                       cT + [vecT], [Lt[p]])
            NUB = 4
            UB = [R3[:, s_ * 2176:(s_ + 1) * 2176] for s_ in range(NUB)]
            UBt = [[T("ub%d_%d" % (s_, jj)) for jj in range(4)] for s_ in range(NUB)]
            take("R3", [x for r in UBt for x in r])
            for s_ in range(NUB):
                sch.op("dve" if s_ % 2 == 0 else "pool", lambda e, s_=s_: e.memset(UB[s_][:, 0:2176], 0.0),
                       writes=UBt[s_])
            ws0 = w_acquire()
            ws1 = w_acquire()
            wso = [ws0, ws1]
            dcnt = 0
            oc = [0]
            yc = [0]
            mu_ap, mu_t = Fp[0], Ft[0]
            m2_ap, m2_t = Fp[1], Ft[1]
            pend = {}

            def conv_mm(n):
                t, p = divmod(n, 8)
                s_ = n % NUB
                cb = n % 2
                for jj in range(4):
                    wdt = 540 if jj < 3 else 539
                    c0 = t * TW + 2 + jj
                    sch.dma("sp",
                            UB[s_][32 * jj:32 * jj + 32, 0:2176].rearrange("p (g x) -> p g x", g=4)[:, :, 0:wdt],
                            ud[j, p * 128:(p + 1) * 128, c0:c0 + wdt].rearrange("(g c) x -> c g x", g=4),
                            reads=[udT[p]], writes=[UBt[s_][jj]], key="ub%d_%d" % (s_, jj))
                for m in range(8):
                    for g_ in range(4):
                        last = (m == 7 and g_ == 3)
                        out_ap = PSb[cb][32 * g_:32 * g_ + 32, :]
                        lhs = L[:, 4 * p + g_, m, :]
                        rhs = UB[s_][:, g_ * 544 + 4 * m:g_ * 544 + 4 * m + TW]
                        sch.op("pe", lambda e, out_ap=out_ap, lhs=lhs, rhs=rhs, m=m, g_=g_: e.matmul(
                            out_ap, lhsT=lhs, rhs=rhs, start=(m == 0), stop=(m == 7), tile_position=(0, 32 * g_)),
                            [Lt[p]] + UBt[s_], [PSt[cb]], last)

            def evac(n):
                t, p = divmod(n, 8)
                cb = n % 2
                Cp = Cv[:, p * TW:(p + 1) * TW]
                act(AF.Identity, Cp, PSb[cb][:], [PSt[cb], vecT], [Ct[p]], bias=vcol(c_dwb + p))
                q_ap, q_t = nextB()
                act(AF.Square, q_ap[:], PSb[cb][:], [PSt[cb], vecT], [q_t], bias=vcol(c_dwb + p))
                l_ap, l_t = nextB()
                cp("dve", l_ap[:], Cp, [Ct[p]], [l_t])
                pend[n] = (l_ap, l_t, q_ap, q_t)

            def stats_mm(n):
                t, p = divmod(n, 8)
                l_ap, l_t, q_ap, q_t = pend.pop(n)
                mm(PSb[6][:], ones[:], l_ap[:], p == 0, p == 7, [onesT, l_t], [PSt[6]], True)
                mm(PSb[7][:], ones[:], q_ap[:], p == 0, p == 7, [onesT, q_t], [PSt[7]], True)

            def ln_stats():
                ts2("dve", mu_ap[:], PSb[6][:], 1.0 / D, None, ALU.mult, None, [PSt[6]], [mu_t])
                tt("dve", m2_ap[:], mu_ap[:], mu_ap[:], ALU.mult, [mu_t], [m2_t])
                stt("dve", m2_ap[:], PSb[7][:], 1.0 / D, m2_ap[:], ALU.mult, ALU.subtract, [PSt[7], m2_t], [m2_t])
                act_rsqrt(m2_ap[:], m2_ap[:], 1.0, epsl[:], [m2_t], m2_t)

            ypend = {}

            def dve_norm(n):
                t, p = divmod(n, 8)
                Cp = Cv[:, p * TW:(p + 1) * TW]
                y_ap, y_t = Fp[2 + yc[0] % 3], Ft[2 + yc[0] % 3]
                yc[0] += 1
                tt("dve", y_ap[:], Cp, mu_ap[:], ALU.subtract, [Ct[p], mu_t], [y_t])
                tt("dve", y_ap[:], y_ap[:], m2_ap[:], ALU.mult, [y_t, m2_t], [y_t])
                ypend[n] = (y_ap, y_t)

            def act_fin(n):
                if n not in ypend:
                    return
                t, p = divmod(n, 8)
                tsl = slice(t * TW, (t + 1) * TW)
                y_ap, y_t = ypend.pop(n)
                act(AF.Silu, y_ap[:], y_ap[:], [y_t, vecT], [y_t], bias=vcol(c_lnb + p), scale=vcol(c_lng + p))
                tt("dve", G[:, p, tsl], y_ap[:], G[:, p, tsl], ALU.mult, [y_t, Gt[p][t]], [Gt[p][t]])

            def outproj(t):
                tsl = slice(t * TW, (t + 1) * TW)
                for m in range(8):
                    b = 2 + (oc[0] % 4)
                    oc[0] += 1
                    slot = wso[m // 4]
                    for kc in range(8):
                        mm(PSb[b][:], WR[:, slot, kc, (m % 4) * 128:(m % 4 + 1) * 128], G[:, kc, tsl], kc == 0, kc == 7,
                           [WRt[slot], Gt[kc][t]], [PSt[b]], kc == 7)
                    stt("dve", X[:, m, tsl], PSb[b][:], vcol(c_bout + m), X[:, m, tsl], ALU.add, ALU.add,
                        [PSt[b], Xt[m][t], vecT], [Xt[m][t]])

            NCH = NT * 8
            fpool[0] = [5]
            for n in range(NCH):
                t, p = divmod(n, 8)
                conv_mm(n)
                if n >= 1:
                    stats_mm(n - 1)
                if p == 0 and t > 0:
                    ln_stats()
                if n >= 9:
                    act_fin(n - 9)
                if n >= 8:
                    dve_norm(n - 8)
                if p == 7 and t > 0:
                    act_fin(n - 8)
                    outproj(t - 1)
                    if after_tile is not None:
                        after_tile(t - 1, 2 + (oc[0] % 4))
                        oc[0] += 1
                evac(n)
            stats_mm(NCH - 1)
            ln_stats()
            for n in range(NCH - 8, NCH):
                dve_norm(n)
                if n > NCH - 8:
                    act_fin(n - 1)
            act_fin(NCH - 1)
            outproj(NT - 1)
            if after_tile is not None:
                after_tile(NT - 1, 2 + (oc[0] % 4))
            fpool[0] = list(range(NF))
            w_release()
            w_release()

        def finish_tile(t, bank_=None):
            if final_norm:
                rmsnorm(V_FINAL, lambda c, tsl: X[:, c, tsl], lambda c, t_: Xt[c][t_], tiles=[t], bank_=bank_)
            sch.dma("pool", yT[:, t * TW:(t + 1) * TW].rearrange("(c p) n -> p c n", p=128),
                    X[:, :, t * TW:(t + 1) * TW], reads=[Xt[c][t] for c in range(8)], key="out")

        mi = ci = 0
        for li, kind in enumerate(layer_kinds):
            if kind == "mla":
                conv_next = ci if (li + 1 < n_layers and layer_kinds[li + 1] == "conv") else None
                mla_layer(mi, conv_next)
                mi += 1
            else:
                conv_layer(ci, finish_tile if li == n_layers - 1 else None)
                ci += 1

        if layer_kinds[-1] != "conv":
            for t in range(NT):
                finish_tile(t)
        sch.wait_all("pool", "out")
        sch.emit()
    return nc


def _col8(v):
    v = np.asarray(v, np.float32)
    n = v.shape[0] // 128
    return np.ascontiguousarray(v.reshape(n, 128).T)


def _perm_half(w):
    return np.concatenate([w[..., 32:64], w[..., 0:32]], axis=-1)


def prep_shared(inp, layer_kinds):
    f32 = np.float32
    vecs = np.zeros((128, NV), f32)
    vecs[:, V_FINAL:V_FINAL + 8] = _col8(inp["final_norm_g"])
    inv = (np.float32(10000.0) ** (-(np.arange(0, 64, 2, dtype=np.float32)) / np.float32(64))).astype(f32)
    vecs[:, V_INVF] = np.tile(inv, 4)
    ph = np.zeros(128, f32)
    ph[0:64] = np.float32(math.pi / 2)
    ph[64:96] = np.float32(math.pi)
    vecs[:, V_PHASE] = ph
    out = {}
    mi = ci = 0
    for kind in layer_kinds:
        if kind == "mla":
            vb = V_MLA + mi * MLA_VW
            vecs[:, vb:vb + 8] = _col8(inp["mla_norm_g"][mi])
            vecs[:, vb + 8:vb + 11] = _col8(inp["mla_q_norm_g"][mi])
            vecs[:, vb + 11:vb + 13] = _col8(inp["mla_kv_norm_g"][mi])
            w = np.asarray(inp["mla_w_in"][mi], f32)
            kpe = w[:, 640:704]
            out["mwin%d" % mi] = np.ascontiguousarray(
                np.concatenate([w[:, 0:640], kpe, _perm_half(kpe), w[:, 704:1728]], axis=1))
            wq = np.asarray(inp["mla_w_qb"][mi], f32).reshape(384, 8, 192)
            out["mwqb%d" % mi] = np.ascontiguousarray(
                np.concatenate([wq, _perm_half(wq[:, :, 128:192])], axis=2).reshape(384, 2048))
            out["mwkvb%d" % mi] = np.ascontiguousarray(np.asarray(inp["mla_w_kvb"][mi], f32))
            out["mwout%d" % mi] = np.ascontiguousarray(np.asarray(inp["mla_w_out"][mi], f32))
            mi += 1
        else:
            vb = V_CONV + ci * CONV_VW
            vecs[:, vb:vb + 8] = _col8(inp["conv_norm_g"][ci])
            bin_ = np.asarray(inp["conv_b_in"][ci], f32)
            ba, bb, bg = _col8(bin_[0:1024]), _col8(bin_[1024:2048]), _col8(bin_[2048:3072])
            for p in range(8):
                vecs[:, vb + 8 + 2 * p] = ba[:, p]
                vecs[:, vb + 8 + 2 * p + 1] = bb[:, p]
            vecs[:, vb + 24:vb + 32] = bg
            vecs[:, vb + 32:vb + 40] = _col8(inp["conv_dw_b"][ci])
            vecs[:, vb + 40:vb + 48] = _col8(inp["conv_ln_g"][ci])
            vecs[:, vb + 48:vb + 56] = _col8(inp["conv_ln_b"][ci])
            vecs[:, vb + 56:vb + 64] = _col8(inp["conv_b_out"][ci])
            dw = np.asarray(inp["conv_dw_w"][ci], f32)
            dwp = np.concatenate([dw, np.zeros((1, 1024), f32)], axis=0)
            w4 = dwp.reshape(8, 4, 8, 4, 32).transpose(1, 4, 2, 3, 0).reshape(128, 256)
            vecs[:, vb + 64:vb + 64 + 256] = w4
            w = np.asarray(inp["conv_w_in"][ci], f32)
            cols = []
            for p in range(8):
                cols.append(w[:, p * 128:(p + 1) * 128])
                cols.append(w[:, 1024 + p * 128:1024 + (p + 1) * 128])
            cols.append(w[:, 2048:3072])
            out["cwin%d" % ci] = np.ascontiguousarray(np.concatenate(cols, axis=1))
            out["cwout%d" % ci] = np.ascontiguousarray(np.asarray(inp["conv_w_out"][ci], f32))
            ci += 1
    out["vecs"] = vecs
    out["ident"] = np.eye(128, dtype=f32)
    out["i4"] = np.tile(np.eye(32, dtype=f32), (4, 1))
    tk = np.arange(128)[:, None]
    tq = np.arange(128)[None, :]
    out["maskT"] = np.where(tk <= tq, 0.0, NEG).astype(f32)
    return out


LAYERS = ["mla", "conv", "mla", "conv"]
_CACHE = {}


def kernel(**inputs):
    x = np.asarray(inputs["x"], np.float32)
    pos = np.asarray(inputs["positions"], np.int32)
    B = x.shape[0]
    shared = prep_shared(inputs, LAYERS)
    if "nc" not in _CACHE:
        _CACHE["nc"] = build_program(LAYERS, final_norm=True)
    nc = _CACHE["nc"]
    in_maps = []
    for b in range(B):
        m = dict(shared)
        m["xT"] = np.ascontiguousarray(x[b].T)
        m["pos"] = np.ascontiguousarray(pos[b][None, :])
        in_maps.append(m)
    res = run_bass_kernel_spmd(nc, in_maps, core_ids=list(range(B)))
    out = np.stack([np.ascontiguousarray(res.results[b]["yT"].T) for b in range(B)], axis=0)
    return out.astype(np.float32)
```

```python
import math
from contextlib import ExitStack

import numpy as np
import concourse.bass as bass
import concourse.mybir as mybir
from concourse.bass_utils import run_bass_kernel_spmd

F32 = mybir.dt.float32
BF16 = mybir.dt.bfloat16
I32 = mybir.dt.int32
AF = mybir.ActivationFunctionType
ALU = mybir.AluOpType

S = 2048
D = 1024
NT = 4
TW = 512
NH = 8
SCALE = 1.0 / math.sqrt(192.0)
RMS_EPS = 1e-6
LN_EPS = 1e-5
NEG = -30000.0
TWO_PI = 2.0 * math.pi
C1 = 6.28125
C2 = TWO_PI - C1
PI_LO = 3.1415925

V_FINAL = 0
V_INVF = 8
V_PHASE = 9
V_MLA = 10
MLA_VW = 13
V_CONV = V_MLA + 2 * MLA_VW
CONV_VW = 8 + 24 + 8 + 8 + 8 + 8 + 256
NV = V_CONV + 2 * CONV_VW


class T:
    __slots__ = ("name", "w", "r")

    def __init__(self, name):
        self.name = name
        self.w = None
        self.r = {}


class Sched:
    ENGS = ("pe", "act", "dve", "pool", "sp")

    def __init__(self, nc, st):
        self.nc, self.st = nc, st
        self.prog = {e: [] for e in self.ENGS}
        self.sem = {}
        self.cnt = {}
        self.seen = {e: {} for e in self.ENGS}
        self.uid = 0

    def getsem(self, key):
        if key not in self.sem:
            self.sem[key] = self.st.enter_context(self.nc.semaphore("s_" + key))
            self.cnt[key] = 0
        return self.sem[key]

    def _deps(self, eng, reads, writes, is_dma):
        deps = {}

        def add(d, keep_same):
            if d is None:
                return
            k, v = d
            if k == eng and not keep_same:
                return
            if deps.get(k, 0) < v:
                deps[k] = v

        for t in reads:
            add(t.w, is_dma or eng != "pe")
        for t in writes:
            add(t.w, is_dma)
            for k, v in t.r.items():
                add((k, v), is_dma)
        for k, v in deps.items():
            if self.seen[eng].get(k, 0) < v:
                self.prog[eng].append(("wait", k, v))
                self.seen[eng][k] = v

    def _mark(self, d, reads, writes):
        k, v = d
        for t in reads:
            if t.r.get(k, 0) < v:
                t.r[k] = v
        for t in writes:
            t.w = d
            t.r = {}

    def op(self, eng, fn, reads=(), writes=(), sig=True):
        self._deps(eng, reads, writes, False)
        self.getsem(eng)
        val = self.cnt[eng] + 1
        if sig:
            self.cnt[eng] = val
        self.prog[eng].append(("op", fn, eng if sig else None, 1))
        self._mark((eng, val), reads, writes)

    def dma(self, q, out_ap, in_ap, reads=(), writes=(), key=None):
        if key is None:
            key = "d%d" % self.uid
            self.uid += 1
        self._deps(q, reads, writes, True)
        self.getsem(key)
        self.cnt[key] += 16
        self.prog[q].append(("op", lambda e: e.dma_start(out=out_ap, in_=in_ap), key, 16))
        d = (key, self.cnt[key])
        self._mark(d, reads, writes)
        return d

    def wait_all(self, eng, key):
        v = self.cnt[key]
        if self.seen[eng].get(key, 0) < v:
            self.prog[eng].append(("wait", key, v))
            self.seen[eng][key] = v

    def emit(self):
        blk = self.st.enter_context(self.nc.Block())

        def body(name):
            def run(e):
                for it in self.prog[name]:
                    if it[0] == "wait":
                        e.wait_ge(self.sem[it[1]], it[2])
                    else:
                        ins = it[1](e)
                        if it[2] is not None:
                            ins.then_inc(self.sem[it[2]], it[3])
            return run

        blk.tensor(body("pe"))
        blk.scalar(body("act"))
        blk.vector(body("dve"))
        blk.gpsimd(body("pool"))
        blk.sync(body("sp"))


def handoff(old, new):
    u = {}
    for t in old:
        if t.w is not None:
            k, v = t.w
            if u.get(k, 0) < v:
                u[k] = v
        for k, v in t.r.items():
            if u.get(k, 0) < v:
                u[k] = v
    for t in new:
        t.w = None
        t.r = dict(u)


def build_program(layer_kinds, final_norm=True):
    nc = bass.Bass("TRN2", target_bir_lowering=False)
    n_layers = len(layer_kinds)
    dram = {}
    dram["xT"] = nc.dram_tensor("xT", [D, S], F32, kind="ExternalInput").ap()
    dram["pos"] = nc.dram_tensor("pos", [1, S], I32, kind="ExternalInput").ap()
    dram["vecs"] = nc.dram_tensor("vecs", [128, NV], F32, kind="ExternalInput").ap()
    dram["ident"] = nc.dram_tensor("ident", [128, 128], F32, kind="ExternalInput").ap()
    dram["maskT"] = nc.dram_tensor("maskT", [128, 128], F32, kind="ExternalInput").ap()
    dram["i4"] = nc.dram_tensor("i4", [128, 32], F32, kind="ExternalInput").ap()
    n_mla = sum(1 for k in layer_kinds if k == "mla")
    n_conv = n_layers - n_mla
    for j in range(n_mla):
        dram["mwin%d" % j] = nc.dram_tensor("mwin%d" % j, [D, 1792], F32, kind="ExternalInput").ap()
        dram["mwqb%d" % j] = nc.dram_tensor("mwqb%d" % j, [384, 2048], F32, kind="ExternalInput").ap()
        dram["mwkvb%d" % j] = nc.dram_tensor("mwkvb%d" % j, [256, 2048], F32, kind="ExternalInput").ap()
        dram["mwout%d" % j] = nc.dram_tensor("mwout%d" % j, [D, D], F32, kind="ExternalInput").ap()
    for j in range(n_conv):
        dram["cwin%d" % j] = nc.dram_tensor("cwin%d" % j, [D, 3072], F32, kind="ExternalInput").ap()
        dram["cwout%d" % j] = nc.dram_tensor("cwout%d" % j, [D, D], F32, kind="ExternalInput").ap()
    yT = nc.dram_tensor("yT", [D, S], F32, kind="ExternalOutput").ap()

    ud = nc.dram_tensor("ud", [max(n_conv, 1), D, 2080], BF16, kind="Internal").ap()

    st = ExitStack()
    with st:
        def sb(name, shape, dt):
            return st.enter_context(nc.sbuf_tensor(name, shape, dt))

        X = sb("X", [128, 8, S], F32)
        R1 = sb("R1", [128, 16384], BF16)
        R2 = sb("R2", [128, 16384], BF16)
        R3 = sb("R3", [128, 16640], BF16)
        WH = sb("WH", [128, 2, 1280], BF16)
        WR = sb("WR", [128, 2, 8, 512], BF16)
        NF, NB = 6, 6
        Fp = [sb("F%d" % i, [128, TW], F32) for i in range(NF)]
        Bp = [sb("B%d" % i, [128, TW], BF16) for i in range(NB)]
        i4 = sb("i4b", [128, 32], BF16)
        ident = sb("identb", [128, 128], BF16)
        maskT = sb("maskb", [128, 128], BF16)
        ones = sb("onesb", [128, 128], BF16)
        vecs = sb("vecs_sb", [128, NV], F32)
        epsr = sb("epsr", [128, 1], F32)
        epsl = sb("epsl", [128, 1], F32)
        PSb = [st.enter_context(nc.psum_tensor("ps%d" % i, [128, TW], F32)) for i in range(8)]

        sch = Sched(nc, st)

        Xt = [[T("X%d_%d" % (c, t)) for t in range(NT)] for c in range(8)]
        PSt = [T("ps%d" % i) for i in range(8)]
        Ft = [T("F%d" % i) for i in range(NF)]
        Bt = [T("B%d" % i) for i in range(NB)]
        WRt = [T("WR0"), T("WR1")]
        WHt = [T("WH0"), T("WH1")]
        constT = T("const")
        vecT = T("vecs")
        region = {"R1": [], "R2": [], "R3": []}

        def take(rname, tiles):
            handoff(region[rname], tiles)
            region[rname] = list(tiles)

        fi = [0]
        bi = [0]

        fpool = [list(range(NF))]

        def nextF():
            pool = fpool[0]
            i = pool[fi[0] % len(pool)]
            fi[0] += 1
            return Fp[i], Ft[i]

        def nextB():
            i = bi[0] % NB
            bi[0] += 1
            return Bp[i], Bt[i]

        def mm(out, lhsT, rhs, start, stop, reads, writes, sig):
            sch.op("pe", lambda e: e.matmul(out, lhsT=lhsT, rhs=rhs, start=start, stop=stop),
                   reads, writes, sig)

        def act(func, out, in_, reads, writes, bias=None, scale=None):
            kw = {}
            if bias is not None:
                kw["bias"] = bias
            if scale is not None:
                kw["scale"] = scale
            sch.op("act", lambda e: e.activation(out=out, in_=in_, func=func, **kw), reads, writes)

        def tt(eng, out, in0, in1, op, reads, writes):
            sch.op(eng, lambda e: e.tensor_tensor(out=out, in0=in0, in1=in1, op=op), reads, writes)

        def ts2(eng, out, in0, s1, s2, op0, op1, reads, writes):
            if s2 is None:
                sch.op(eng, lambda e: e.tensor_scalar(out=out, in0=in0, scalar1=s1, scalar2=None, op0=op0),
                       reads, writes)
            else:
                sch.op(eng, lambda e: e.tensor_scalar(out=out, in0=in0, scalar1=s1, scalar2=s2, op0=op0, op1=op1),
                       reads, writes)

        def stt(eng, out, in0, scalar, in1, op0, op1, reads, writes):
            sch.op(eng, lambda e: e.scalar_tensor_tensor(out=out, in0=in0, scalar=scalar, in1=in1, op0=op0, op1=op1),
                   reads, writes)

        def cp(eng, out, in_, reads, writes):
            sch.op(eng, lambda e: e.tensor_copy(out=out, in_=in_), reads, writes)

        def recip(out, in_, reads, writes):
            sch.op("dve", lambda e: e.reciprocal(out=out, in_=in_), reads, writes)

        def vcol(c):
            return vecs[:, c:c + 1]

        def act_rsqrt(out, in_, scale, eps_ap, reads, wt):
            act(AF.Ln, out, in_, list(reads) + [epsT], [wt], bias=eps_ap, scale=scale)
            act(AF.Exp, out, out, [wt], [wt], scale=-0.5)

        def act_recip(out, in_, reads, wt):
            act(AF.Ln, out, in_, list(reads), [wt])
            act(AF.Exp, out, out, [wt], [wt], scale=-1.0)

        witems = []
        mi = ci = 0
        for kind in layer_kinds:
            if kind == "mla":
                w = dram["mwin%d" % mi]
                witems += [(w[:, 0:512], 512), (w[:, 512:1024], 512), (w[:, 1024:1536], 512), (w[:, 1536:1792], 256)]
                w = dram["mwout%d" % mi]
                witems += [(w[:, 0:512], 512), (w[:, 512:1024], 512)]
                mi += 1
            else:
                w = dram["cwin%d" % ci]
                witems += [(w[:, i * 512:(i + 1) * 512], 512) for i in range(6)]
                w = dram["cwout%d" % ci]
                witems += [(w[:, 0:512], 512), (w[:, 512:1024], 512)]
                ci += 1
        wstate = {"loaded": 0, "acq": 0, "rel": 0}

        def w_fill():
            while wstate["loaded"] < min(len(witems), wstate["rel"] + 2):
                i = wstate["loaded"]
                ap, ncols = witems[i]
                slot = i % 2
                sch.dma("pool", WR[:, slot, :, 0:ncols], ap.rearrange("(c p) n -> p c n", p=128),
                        writes=[WRt[slot]], key="wr%d" % slot)
                wstate["loaded"] += 1

        def w_acquire():
            i = wstate["acq"]
            assert i < wstate["loaded"], "weight item not loaded"
            wstate["acq"] += 1
            return i % 2

        def w_release():
            wstate["rel"] += 1
            w_fill()

        sch.dma("sp", vecs[:], dram["vecs"][:, :], writes=[vecT])
        first_pos = [True]
        posi0 = R1[:, 0:4096].bitcast(I32)
        pos0T = T("posi")
        sch.dma("sp", posi0, dram["pos"].partition_broadcast(128), writes=[pos0T])
        for t in range(NT):
            sch.dma("sp", X[:, :, t * TW:(t + 1) * TW],
                    dram["xT"][:, t * TW:(t + 1) * TW].rearrange("(c p) n -> p c n", p=128),
                    writes=[Xt[c][t] for c in range(8)])
        cT = [T("c_ident"), T("c_mask"), T("c_i4")]
        sch.dma("pool", ident[:], dram["ident"][:, :], writes=[cT[0]], key="cst0")
        sch.dma("pool", maskT[:], dram["maskT"][:, :], writes=[cT[1]], key="cst1")
        sch.dma("pool", i4[:], dram["i4"][:, :], writes=[cT[2]], key="cst2")
        onesT = T("ones")
        sch.op("dve", lambda e: e.memset(ones[:], 1.0), writes=[onesT])
        epsT = T("eps")
        sch.op("dve", lambda e: e.memset(epsr[:], RMS_EPS), writes=[epsT])
        sch.op("dve", lambda e: e.memset(epsl[:], LN_EPS), writes=[epsT])
        w_fill()

        def rmsnorm(gcol, dst_ap_fn, dst_t_fn, tiles=None, bank_=None):
            for t in (range(NT) if tiles is None else tiles):
                tsl = slice(t * TW, (t + 1) * TW)
                bank = (6 + (t % 2)) if bank_ is None else bank_
                for c in range(8):
                    b_ap, b_t = nextB()
                    act(AF.Square, b_ap[:], X[:, c, tsl], [Xt[c][t]], [b_t])
                    mm(PSb[bank][:], ones[:], b_ap[:], c == 0, c == 7, [b_t, onesT], [PSt[bank]], True)
                f_ap, f_t = nextF()
                act_rsqrt(f_ap[:], PSb[bank][:], 1.0 / D, epsr[:], [PSt[bank]], f_t)
                for c in range(8):
                    stt("dve", dst_ap_fn(c, tsl), X[:, c, tsl], vcol(gcol + c), f_ap[:], ALU.mult, ALU.mult,
                        [Xt[c][t], f_t, vecT], [dst_t_fn(c, t)])

        rope_state = [None]

        def rope_chain(posi, t1, ki, kf, tmpT, ROPE, ropeT):
            cp("dve", t1, posi, [tmpT[0]], [tmpT[1]])
            ts2("dve", t1, t1, vcol(V_INVF), vcol(V_PHASE), ALU.mult, ALU.add, [tmpT[1], vecT], [tmpT[1]])
            ts2("dve", kf, t1, 1.0 / TWO_PI, None, ALU.mult, None, [tmpT[1]], [tmpT[3]])
            cp("dve", ki, kf, [tmpT[3]], [tmpT[2]])
            cp("dve", kf, ki, [tmpT[2]], [tmpT[3]])
            stt("dve", t1, kf, -C1, t1, ALU.mult, ALU.add, [tmpT[3], tmpT[1]], [tmpT[1]])
            stt("dve", t1, kf, -C2, t1, ALU.mult, ALU.add, [tmpT[3], tmpT[1]], [tmpT[1]])
            ts2("dve", t1, t1, PI_LO, -PI_LO, ALU.min, ALU.max, [tmpT[1]], [tmpT[1]])
            act(AF.Sin, ROPE, t1, [tmpT[1]], [ropeT])

        H = R1[:, :].rearrange("p (c n) -> p c n", c=8)
        G = R2[:, :].rearrange("p (c n) -> p c n", c=8)

        def mla_layer(j, conv_next):
            vb = V_MLA + j * MLA_VW
            QN = R3[:, 0:6144].rearrange("p (c n) -> p c n", c=3)
            KVN = R3[:, 6144:10240].rearrange("p (c n) -> p c n", c=2)
            KPE = R3[:, 10240:12288]
            ROPE = R3[:, 12288:16384].bitcast(F32)
            LAT = [QN[:, 0], QN[:, 1], QN[:, 2], KVN[:, 0], KVN[:, 1]]
            wq_d = dram["mwqb%d" % j]
            wkv_d = dram["mwkvb%d" % j]

            def load_head_w(h):
                s_ = h % 2
                sch.dma("pool", WH[:, s_, 0:768].rearrange("p (c n) -> p c n", c=3),
                        wq_d[:, h * 256:(h + 1) * 256].rearrange("(c p) n -> p c n", p=128),
                        writes=[WHt[s_]], key="wh%d" % s_)
                sch.dma("pool", WH[:, s_, 768:1280].rearrange("p (c n) -> p c n", c=2),
                        wkv_d[:, h * 256:(h + 1) * 256].rearrange("(c p) n -> p c n", p=128),
                        writes=[WHt[s_]], key="wh%d" % s_)

            load_head_w(0)
            load_head_w(1)
            posi = R1[:, 0:4096].bitcast(I32)
            t1 = R1[:, 4096:8192].bitcast(F32)
            ki = R1[:, 8192:12288].bitcast(I32)
            kf = R1[:, 12288:16384].bitcast(F32)
            tmpT = [T("posi"), T("t1"), T("ki"), T("kf")]
            if first_pos[0]:
                tmpT[0] = pos0T
                region["R1"] = [pos0T]
            take("R1", tmpT[1:] if first_pos[0] else tmpT)
            if first_pos[0]:
                region["R1"] = list(tmpT)
            ropeT = T("rope")
            LATt = [[T("lat%d_%d" % (f, t)) for t in range(NT)] for f in range(5)]
            KPEt = [T("kpe%d" % t) for t in range(NT)]
            if rope_state[0] is None:
                take("R3", [ropeT] + [x for r in LATt for x in r] + KPEt)
            else:
                ropeT = rope_state[0]
                take("R3", [x for r in LATt for x in r] + KPEt)
            first_pos[0] = False
            if rope_state[0] is None:
                rope_chain(posi, t1, ki, kf, tmpT, ROPE, ropeT)
                rope_state[0] = ropeT

            Ht = [[T("h%d_%d" % (c, t)) for t in range(NT)] for c in range(8)]
            take("R1", [x for r in Ht for x in r])
            rmsnorm(vb, lambda c, tsl: H[:, c, tsl], lambda c, t: Ht[c][t])

            RAWv = R2[:, 0:10240].bitcast(F32)
            RAWt = [[T("raw%d_%d" % (s_, f)) for f in range(5)] for s_ in range(2)]
            take("R2", [x for r in RAWt for x in r])
            s0 = w_acquire()
            s1 = w_acquire()
            wsl = [(s0, 0), (s0, 128), (s0, 256), (s0, 384), (s1, 0), (s1, 128)]
            pcnt = [0]

            def proj_bank():
                b = pcnt[0] % 4
                pcnt[0] += 1
                return b

            def rope_apply(ps_ap, ps_t, out_ap, out_t, tsl):
                a_ap, a_t = nextF()
                tt("dve", a_ap[:], ps_ap, ROPE[:, tsl], ALU.mult, [ps_t, ropeT], [a_t])
                b_ap, b_t = nextF()
                cp("dve", b_ap[0:64, :], a_ap[64:128, :], [a_t], [b_t])
                tt("dve", out_ap, a_ap[0:64, :], b_ap[0:64, :], ALU.add, [a_t, b_t], [out_t])

            for t in range(NT):
                tsl = slice(t * TW, (t + 1) * TW)
                rs = t % 2
                for f in range(6):
                    slot, off = wsl[f]
                    b = proj_bank()
                    for kc in range(8):
                        mm(PSb[b][:], WR[:, slot, kc, off:off + 128], H[:, kc, tsl], kc == 0, kc == 7,
                           [WRt[slot], Ht[kc][t]], [PSt[b]], kc == 7)
                    if f < 5:
                        raw = RAWv[:, (rs * 5 + f) * TW:(rs * 5 + f + 1) * TW]
                        act(AF.Copy, raw, PSb[b][:], [PSt[b]], [RAWt[rs][f]])
                        q_ap, q_t = nextB()
                        act(AF.Square, q_ap[:], PSb[b][:], [PSt[b]], [q_t])
                        sbank = 6 if f < 3 else 7
                        first = f in (0, 3)
                        last = f in (2, 4)
                        mm(PSb[sbank][:], ones[:], q_ap[:], first, last, [q_t, onesT], [PSt[sbank]], True)
                    else:
                        rope_apply(PSb[b][:], PSt[b], KPE[0:64, tsl], KPEt[t], tsl)
                rq_ap, rq_t = nextF()
                act_rsqrt(rq_ap[:], PSb[6][:], 1.0 / 384, epsr[:], [PSt[6]], rq_t)
                rk_ap, rk_t = nextF()
                act_rsqrt(rk_ap[:], PSb[7][:], 1.0 / 256, epsr[:], [PSt[7]], rk_t)
                for f in range(5):
                    raw = RAWv[:, (rs * 5 + f) * TW:(rs * 5 + f + 1) * TW]
                    r_ap, r_t = (rq_ap, rq_t) if f < 3 else (rk_ap, rk_t)
                    stt("dve", LAT[f][:, tsl], raw, vcol(vb + 8 + f), r_ap[:], ALU.mult, ALU.mult,
                        [RAWt[rs][f], r_t, vecT], [LATt[f][t]])

            Gt = [[T("g%d_%d" % (c, t)) for t in range(NT)] for c in range(8)]
            take("R2", [x for r in Gt for x in r])
            cur = s1
            held = [s0, s1]
            for gi in range(8):
                if gi == 0:
                    slot, off = s1, 256
                elif gi == 1:
                    slot, off = s1, 384
                elif gi < 6:
                    if gi == 2:
                        w_release()
                        s2 = w_acquire()
                    slot, off = s2, (gi - 2) * 128
                else:
                    if gi == 6:
                        w_release()
                        s3 = w_acquire()
                    slot, off = s3, (gi - 6) * 128
                for t in range(NT):
                    tsl = slice(t * TW, (t + 1) * TW)
                    b = proj_bank()
                    for kc in range(8):
                        mm(PSb[b][:], WR[:, slot, kc, off:off + 128], H[:, kc, tsl], kc == 0, kc == 7,
                           [WRt[slot], Ht[kc][t]], [PSt[b]], kc == 7)
                    act(AF.Silu, G[:, gi, tsl], PSb[b][:], [PSt[b]], [Gt[gi][t]])
                if gi == 5:
                    w_release()
            w_release()

            HB = []
            for s_ in range(2):
                base = s_ * 8192
                HB.append(dict(
                    qn=R1[:, base:base + 2048], qp=R1[:, base + 2048:base + 4096],
                    kh=R1[:, base + 4096:base + 6144],
                    vh=R1[:, base + 6144:base + 8192].rearrange("p (b n) -> p b n", n=128),
                    qnt=[T("qn%d_%d" % (s_, t)) for t in range(NT)],
                    qpt=[T("qp%d_%d" % (s_, t)) for t in range(NT)],
                    kht=[T("kh%d_%d" % (s_, t)) for t in range(NT)],
                    vht=[T("vh%d_%d" % (s_, t)) for t in range(NT)],
                ))
            take("R1", [x for hb in HB for key in ("qnt", "qpt", "kht", "vht") for x in hb[key]])
            jb = [0]

            def jit_bank():
                return 7


            def proj_pieces(h):
                s_ = h % 2
                hb = HB[s_]
                wq = WH[:, s_, 0:768].rearrange("p (c n) -> p c n", c=3)
                wkv = WH[:, s_, 768:1280].rearrange("p (c n) -> p c n", c=2)
                pieces = []
                for t in range(NT):
                    tsl = slice(t * TW, (t + 1) * TW)

                    def g_qn(t=t, tsl=tsl):
                        b = jit_bank()
                        for kc in range(3):
                            mm(PSb[b][:], wq[:, kc, 0:128], QN[:, kc, tsl], kc == 0, kc == 2,
                               [WHt[s_], LATt[kc][t]], [PSt[b]], kc == 2)
                        cp("dve", hb["qn"][:, tsl], PSb[b][:], [PSt[b]], [hb["qnt"][t]])

                    def g_qp(t=t, tsl=tsl):
                        b = jit_bank()
                        for kc in range(3):
                            mm(PSb[b][:], wq[:, kc, 128:256], QN[:, kc, tsl], kc == 0, kc == 2,
                               [WHt[s_], LATt[kc][t]], [PSt[b]], kc == 2)
                        rope_apply(PSb[b][:], PSt[b], hb["qp"][0:64, tsl], hb["qpt"][t], tsl)

                    def g_k(t=t, tsl=tsl):
                        b = jit_bank()
                        for kc in range(2):
                            mm(PSb[b][:], wkv[:, kc, 0:128], KVN[:, kc, tsl], kc == 0, kc == 1,
                               [WHt[s_], LATt[3 + kc][t]], [PSt[b]], kc == 1)
                        cp("dve", hb["kh"][:, tsl], PSb[b][:], [PSt[b]], [hb["kht"][t]])

                    def g_v(t=t, tsl=tsl):
                        b = jit_bank()
                        for bb in range(4):
                            tb = 4 * t + bb
                            for kc in range(2):
                                mm(PSb[b][:, bb * 128:(bb + 1) * 128], KVN[:, kc, tb * 128:(tb + 1) * 128],
                                   wkv[:, kc, 128:256], kc == 0, kc == 1,
                                   [WHt[s_], LATt[3 + kc][t]], [PSt[b]], (bb == 3 and kc == 1))
                        cp("dve", hb["vh"][:, 4 * t:4 * t + 4, :],
                           PSb[b][:].rearrange("p (b n) -> p b n", n=128), [PSt[b]], [hb["vht"][t]])
                    pieces.append([g_qn, g_k, g_qp, g_v])
                return pieces

            def attention(h, extra, groups):
                s_ = h % 2
                hb = HB[s_]
                n_groups = max(len(groups), 1)
                blk_done = [0]
                for i in range(NT):
                    ob, sbk = 3 + 2 * (i % 2), 4 + 2 * (i % 2)
                    nblk = 4 * i + 4
                    tsl = slice(i * TW, (i + 1) * TW)

                    def c0_of(jk):
                        return max(jk - 4 * i, 0) * 128

                    def qk(jk):
                        bank = jk % 3
                        c0 = c0_of(jk)
                        diag = jk >= 4 * i
                        tq0 = i * TW + c0
                        tk = slice(jk * 128, (jk + 1) * 128)
                        mm(PSb[bank][:, c0:TW], hb["kh"][:, tk], hb["qn"][:, tq0:(i + 1) * TW], True, False,
                           [hb["kht"][jk // 4], hb["qnt"][i]], [PSt[bank]], False)
                        mm(PSb[bank][:, c0:TW], KPE[0:64, tk], hb["qp"][0:64, tq0:(i + 1) * TW], False, not diag,
                           [KPEt[jk // 4], hb["qpt"][i]], [PSt[bank]], not diag)
                        if diag:
                            mm(PSb[bank][:, c0:c0 + 128], ident[:], maskT[:], False, True,
                               cT, [PSt[bank]], True)

                    def pv(jk):
                        bank = jk % 3
                        c0 = c0_of(jk)
                        p_ap, p_t = nextB()
                        act(AF.Exp, p_ap[:, c0:TW], PSb[bank][:, c0:TW], [PSt[bank]], [p_t], scale=SCALE)
                        mm(PSb[ob][:, c0:TW], hb["vh"][:, jk, :], p_ap[:, c0:TW], jk == 0, jk == nblk - 1,
                           [hb["vht"][jk // 4], p_t], [PSt[ob]], False)
                        mm(PSb[sbk][:, c0:TW], ones[:], p_ap[:, c0:TW], jk == 0, jk == nblk - 1,
                           [onesT, p_t], [PSt[sbk]], True)

                    qk(0)
                    if nblk > 1:
                        qk(1)
                    for jk in range(nblk):
                        if jk + 2 < nblk:
                            qk(jk + 2)
                        pv(jk)
                        blk_done[0] += 1
                        while groups and blk_done[0] * n_groups >= (n_groups - len(groups) + 1) * 40:
                            groups.pop(0)()
                    r_ap, r_t = nextF()
                    act_recip(r_ap[:], PSb[sbk][:], [PSt[sbk]], r_t)
                    u_ap, u_t = nextF()
                    tt("dve", u_ap[:], PSb[ob][:], r_ap[:], ALU.mult, [PSt[ob], r_t], [u_t])
                    tt("dve", G[:, h, tsl], u_ap[:], G[:, h, tsl], ALU.mult, [u_t, Gt[h][i]], [Gt[h][i]])
                    for fn in extra[i]:
                        fn()
                while groups:
                    groups.pop(0)()

            for p in proj_pieces(0):
                for g_ in p:
                    g_()
            for h in range(NH):
                extra = [[] for _ in range(NT)]
                groups = []
                if h + 1 < NH:
                    groups = [g_ for p in proj_pieces(h + 1) for g_ in p]
                attention(h, extra, groups)
                if h + 2 < NH:
                    load_head_w(h + 2)

            for half in range(2):
                slot = w_acquire()
                for mm_ in range(4):
                    m = half * 4 + mm_
                    for t in range(NT):
                        tsl = slice(t * TW, (t + 1) * TW)
                        b = proj_bank()
                        for kc in range(8):
                            mm(PSb[b][:], WR[:, slot, kc, mm_ * 128:(mm_ + 1) * 128], G[:, kc, tsl], kc == 0, kc == 7,
                               [WRt[slot], Gt[kc][t]], [PSt[b]], kc == 7)
                        tt("dve", X[:, m, tsl], PSb[b][:], X[:, m, tsl], ALU.add, [PSt[b], Xt[m][t]], [Xt[m][t]])
                w_release()

        def conv_layer(j, after_tile=None):
            vb = V_CONV + j * CONV_VW
            c_bin, c_dwb, c_lng, c_lnb, c_bout = vb + 8, vb + 32, vb + 40, vb + 48, vb + 56
            Ht = [[T("h%d_%d" % (c, t)) for t in range(NT)] for c in range(8)]
            take("R1", [x for r in Ht for x in r])
            rmsnorm(vb, lambda c, tsl: H[:, c, tsl], lambda c, t: Ht[c][t])
            NUR = 3
            Ur = [R3[:, r_ * 2080:(r_ + 1) * 2080] for r_ in range(NUR)]
            Upad = [T("upad%d" % r_) for r_ in range(NUR)]
            Urt = [[T("ur%d_%d" % (r_, t)) for t in range(NT)] for r_ in range(NUR)]
            take("R3", Upad + [x for r in Urt for x in r])
            for r_ in range(NUR):
                sch.op("dve", lambda e, r_=r_: e.memset(Ur[r_][:, 0:32], 0.0), writes=[Upad[r_]])
            Gt = [[T("g%d_%d" % (c, t)) for t in range(NT)] for c in range(8)]
            take("R2", [x for r in Gt for x in r])
            udT = [T("ud%d" % p) for p in range(8)]
            pc = [0]
            for p in range(8):
                if p % 2 == 0:
                    slot = w_acquire()
                off = (p % 2) * 256
                for t in range(NT):
                    tsl = slice(t * TW, (t + 1) * TW)
                    ba = (2 * pc[0]) % 4
                    bb_ = ba + 1
                    pc[0] += 1
                    for kc in range(8):
                        mm(PSb[ba][:], WR[:, slot, kc, off:off + 128], H[:, kc, tsl], kc == 0, kc == 7,
                           [WRt[slot], Ht[kc][t]], [PSt[ba]], kc == 7)
                    for kc in range(8):
                        mm(PSb[bb_][:], WR[:, slot, kc, off + 128:off + 256], H[:, kc, tsl], kc == 0, kc == 7,
                           [WRt[slot], Ht[kc][t]], [PSt[bb_]], kc == 7)
                    s_ap, s_t = nextF()
                    act(AF.Sigmoid, s_ap[:], PSb[bb_][:], [PSt[bb_], vecT], [s_t], bias=vcol(c_bin + 2 * p + 1))
                    stt("dve", Ur[p % NUR][:, 32 + t * TW:32 + (t + 1) * TW], PSb[ba][:], vcol(c_bin + 2 * p), s_ap[:],
                        ALU.add, ALU.mult, [PSt[ba], s_t, vecT], [Urt[p % NUR][t]])
                sch.dma("sp", ud[j, p * 128:(p + 1) * 128, :], Ur[p % NUR][:, :],
                        reads=[Upad[p % NUR]] + Urt[p % NUR], writes=[udT[p]], key="ud%d" % p)
                if p % 2 == 1:
                    w_release()
            NUB = 4
            UB = [R3[:, s_ * 2176:(s_ + 1) * 2176] for s_ in range(NUB)]
            UBt = [[T("ub%d_%d" % (s_, jj)) for jj in range(4)] for s_ in range(NUB)]
            take("R3", [x for r in UBt for x in r])
            for s_ in range(NUB):
                sch.op("dve" if s_ % 2 == 0 else "pool", lambda e, s_=s_: e.memset(UB[s_][:, 0:2176], 0.0),
                       writes=UBt[s_])

            def ub_fetch(n):
                t, p = divmod(n, 8)
                s_ = n % NUB
                for jj in range(4):
                    wdt = 540 if jj < 3 else 539
                    c0 = t * TW + 2 + jj
                    sch.dma("sp",
                            UB[s_][32 * jj:32 * jj + 32, 0:2176].rearrange("p (g x) -> p g x", g=4)[:, :, 0:wdt],
                            ud[j, p * 128:(p + 1) * 128, c0:c0 + wdt].rearrange("(g c) x -> c g x", g=4),
                            reads=[udT[p]], writes=[UBt[s_][jj]], key="ub%d_%d" % (s_, jj))

            for n in range(NUB):
                ub_fetch(n)
            gc = [0]
            for gi in range(8):
                if gi % 4 == 0:
                    slot = w_acquire()
                off = (gi % 4) * 128
                for t in range(NT):
                    tsl = slice(t * TW, (t + 1) * TW)
                    b = 4 + (gc[0] % 2)
                    gc[0] += 1
                    for kc in range(8):
                        mm(PSb[b][:], WR[:, slot, kc, off:off + 128], H[:, kc, tsl], kc == 0, kc == 7,
                           [WRt[slot], Ht[kc][t]], [PSt[b]], kc == 7)
                    act(AF.Silu, G[:, gi, tsl], PSb[b][:], [PSt[b], vecT], [Gt[gi][t]], bias=vcol(c_bin + 16 + gi))
                if gi % 4 == 3:
                    w_release()
            Cv = R1[:, 0:8192].bitcast(F32)
            L = R1[:, 8192:16384].rearrange("p (g m c) -> p g m c", g=32, m=8)
            Ct = [T("c%d" % p) for p in range(8)]
            Lt = [T("L%d" % p) for p in range(8)]
            take("R1", Ct + Lt)
            w4c = vb + 64
            for p in range(8):
                sch.op("dve" if p % 2 == 0 else "pool",
                       lambda e, p=p: e.tensor_tensor(
                           out=L[:, 4 * p:4 * p + 4, :, :].rearrange("p g m c -> p (g m) c"),
                           in0=i4[:].unsqueeze(1).to_broadcast([128, 32, 32]),
                           in1=vecs[:, w4c + 32 * p:w4c + 32 * p + 32].unsqueeze(2).to_broadcast([128, 32, 32]),
                           op=ALU.mult),
                       cT + [vecT], [Lt[p]])
            ws0 = w_acquire()
            ws1 = w_acquire()
            wso = [ws0, ws1]
            oc = [0]
            yc = [0]
            mu_ap, mu_t = Fp[0], Ft[0]
            m2_ap, m2_t = Fp[1], Ft[1]
            pend = {}

            def conv_mm(n):
                t, p = divmod(n, 8)
                s_ = n % NUB
                cb = n % 2
                for m in range(8):
                    for g_ in range(4):
                        last = (m == 7 and g_ == 3)
                        out_ap = PSb[cb][32 * g_:32 * g_ + 32, :]
                        lhs = L[:, 4 * p + g_, m, :]
                        rhs = UB[s_][:, g_ * 544 + 4 * m:g_ * 544 + 4 * m + TW]
                        sch.op("pe", lambda e, out_ap=out_ap, lhs=lhs, rhs=rhs, m=m, g_=g_: e.matmul(
                            out_ap, lhsT=lhs, rhs=rhs, start=(m == 0), stop=(m == 7), tile_position=(0, 32 * g_)),
                            [Lt[p]] + UBt[s_], [PSt[cb]], last)
                if n + NUB < NT * 8:
                    ub_fetch(n + NUB)

            def evac(n):
                t, p = divmod(n, 8)
                cb = n % 2
                Cp = Cv[:, p * TW:(p + 1) * TW]
                act(AF.Identity, Cp, PSb[cb][:], [PSt[cb], vecT], [Ct[p]], bias=vcol(c_dwb + p))
                q_ap, q_t = nextB()
                act(AF.Square, q_ap[:], PSb[cb][:], [PSt[cb], vecT], [q_t], bias=vcol(c_dwb + p))
                l_ap, l_t = nextB()
                cp("dve", l_ap[:], Cp, [Ct[p]], [l_t])
                pend[n] = (l_ap, l_t, q_ap, q_t)

            def stats_mm(n):
                t, p = divmod(n, 8)
                l_ap, l_t, q_ap, q_t = pend.pop(n)
                mm(PSb[6][:], ones[:], l_ap[:], p == 0, p == 7, [onesT, l_t], [PSt[6]], True)
                mm(PSb[7][:], ones[:], q_ap[:], p == 0, p == 7, [onesT, q_t], [PSt[7]], True)

            def ln_stats():
                ts2("dve", mu_ap[:], PSb[6][:], 1.0 / D, None, ALU.mult, None, [PSt[6]], [mu_t])
                tt("dve", m2_ap[:], mu_ap[:], mu_ap[:], ALU.mult, [mu_t], [m2_t])
                stt("dve", m2_ap[:], PSb[7][:], 1.0 / D, m2_ap[:], ALU.mult, ALU.subtract, [PSt[7], m2_t], [m2_t])
                act_rsqrt(m2_ap[:], m2_ap[:], 1.0, epsl[:], [m2_t], m2_t)

            ypend = {}

            def dve_norm(n):
                t, p = divmod(n, 8)
                Cp = Cv[:, p * TW:(p + 1) * TW]
                y_ap, y_t = Fp[2 + yc[0] % 3], Ft[2 + yc[0] % 3]
                yc[0] += 1
                tt("dve", y_ap[:], Cp, mu_ap[:], ALU.subtract, [Ct[p], mu_t], [y_t])
                tt("dve", y_ap[:], y_ap[:], m2_ap[:], ALU.mult, [y_t, m2_t], [y_t])
                ypend[n] = (y_ap, y_t)

            def act_fin(n):
                if n not in ypend:
                    return
                t, p = divmod(n, 8)
                tsl = slice(t * TW, (t + 1) * TW)
                y_ap, y_t = ypend.pop(n)
                act(AF.Silu, y_ap[:], y_ap[:], [y_t, vecT], [y_t], bias=vcol(c_lnb + p), scale=vcol(c_lng + p))
                tt("dve", G[:, p, tsl], y_ap[:], G[:, p, tsl], ALU.mult, [y_t, Gt[p][t]], [Gt[p][t]])

            def outproj(t):
                tsl = slice(t * TW, (t + 1) * TW)
                for m in range(8):
                    b = 2 + (oc[0] % 4)
                    oc[0] += 1
                    slot = wso[m // 4]
                    for kc in range(8):
                        mm(PSb[b][:], WR[:, slot, kc, (m % 4) * 128:(m % 4 + 1) * 128], G[:, kc, tsl], kc == 0, kc == 7,
                           [WRt[slot], Gt[kc][t]], [PSt[b]], kc == 7)
                    stt("dve", X[:, m, tsl], PSb[b][:], vcol(c_bout + m), X[:, m, tsl], ALU.add, ALU.add,
                        [PSt[b], Xt[m][t], vecT], [Xt[m][t]])

            NCH = NT * 8
            fpool[0] = [5]
            for n in range(NCH):
                t, p = divmod(n, 8)
                conv_mm(n)
                if n >= 1:
                    stats_mm(n - 1)
                if p == 0 and t > 0:
                    ln_stats()
                if n >= 9:
                    act_fin(n - 9)
                if n >= 8:
                    dve_norm(n - 8)
                if p == 7 and t > 0:
                    act_fin(n - 8)
                    outproj(t - 1)
                    if after_tile is not None:
                        after_tile(t - 1, 2 + (oc[0] % 4))
                        oc[0] += 1
                evac(n)
            stats_mm(NCH - 1)
            ln_stats()
            for n in range(NCH - 8, NCH):
                dve_norm(n)
                if n > NCH - 8:
                    act_fin(n - 1)
            act_fin(NCH - 1)
            outproj(NT - 1)
            if after_tile is not None:
                after_tile(NT - 1, 2 + (oc[0] % 4))
            fpool[0] = list(range(NF))
            w_release()
            w_release()

        def finish_tile(t, bank_=None):
            if final_norm:
                rmsnorm(V_FINAL, lambda c, tsl: X[:, c, tsl], lambda c, t_: Xt[c][t_], tiles=[t], bank_=bank_)
            sch.dma("pool", yT[:, t * TW:(t + 1) * TW].rearrange("(c p) n -> p c n", p=128),
                    X[:, :, t * TW:(t + 1) * TW], reads=[Xt[c][t] for c in range(8)], key="out")

        mi = ci = 0
        for li, kind in enumerate(layer_kinds):
            if kind == "mla":
                conv_next = ci if (li + 1 < n_layers and layer_kinds[li + 1] == "conv") else None
                mla_layer(mi, conv_next)
                mi += 1
            else:
                conv_layer(ci, finish_tile if li == n_layers - 1 else None)
                ci += 1

        if layer_kinds[-1] != "conv":
            for t in range(NT):
                finish_tile(t)
        sch.wait_all("pool", "out")
        sch.emit()
    return nc


def _col8(v):
    v = np.asarray(v, np.float32)
    n = v.shape[0] // 128
    return np.ascontiguousarray(v.reshape(n, 128).T)


def _perm_half(w):
    return np.concatenate([w[..., 32:64], w[..., 0:32]], axis=-1)


def prep_shared(inp, layer_kinds):
    f32 = np.float32
    vecs = np.zeros((128, NV), f32)
    vecs[:, V_FINAL:V_FINAL + 8] = _col8(inp["final_norm_g"])
    inv = (np.float32(10000.0) ** (-(np.arange(0, 64, 2, dtype=np.float32)) / np.float32(64))).astype(f32)
    vecs[:, V_INVF] = np.tile(inv, 4)
    ph = np.zeros(128, f32)
    ph[0:64] = np.float32(math.pi / 2)
    ph[64:96] = np.float32(math.pi)
    vecs[:, V_PHASE] = ph
    out = {}
    mi = ci = 0
    for kind in layer_kinds:
        if kind == "mla":
            vb = V_MLA + mi * MLA_VW
            vecs[:, vb:vb + 8] = _col8(inp["mla_norm_g"][mi])
            vecs[:, vb + 8:vb + 11] = _col8(inp["mla_q_norm_g"][mi])
            vecs[:, vb + 11:vb + 13] = _col8(inp["mla_kv_norm_g"][mi])
            w = np.asarray(inp["mla_w_in"][mi], f32)
            kpe = w[:, 640:704]
            out["mwin%d" % mi] = np.ascontiguousarray(
                np.concatenate([w[:, 0:640], kpe, _perm_half(kpe), w[:, 704:1728]], axis=1))
            wq = np.asarray(inp["mla_w_qb"][mi], f32).reshape(384, 8, 192)
            out["mwqb%d" % mi] = np.ascontiguousarray(
                np.concatenate([wq, _perm_half(wq[:, :, 128:192])], axis=2).reshape(384, 2048))
            out["mwkvb%d" % mi] = np.ascontiguousarray(np.asarray(inp["mla_w_kvb"][mi], f32))
            out["mwout%d" % mi] = np.ascontiguousarray(np.asarray(inp["mla_w_out"][mi], f32))
            mi += 1
        else:
            vb = V_CONV + ci * CONV_VW
            vecs[:, vb:vb + 8] = _col8(inp["conv_norm_g"][ci])
            bin_ = np.asarray(inp["conv_b_in"][ci], f32)
            ba, bb, bg = _col8(bin_[0:1024]), _col8(bin_[1024:2048]), _col8(bin_[2048:3072])
            for p in range(8):
                vecs[:, vb + 8 + 2 * p] = ba[:, p]
                vecs[:, vb + 8 + 2 * p + 1] = bb[:, p]
            vecs[:, vb + 24:vb + 32] = bg
            vecs[:, vb + 32:vb + 40] = _col8(inp["conv_dw_b"][ci])
            vecs[:, vb + 40:vb + 48] = _col8(inp["conv_ln_g"][ci])
            vecs[:, vb + 48:vb + 56] = _col8(inp["conv_ln_b"][ci])
            vecs[:, vb + 56:vb + 64] = _col8(inp["conv_b_out"][ci])
            dw = np.asarray(inp["conv_dw_w"][ci], f32)
            dwp = np.concatenate([dw, np.zeros((1, 1024), f32)], axis=0)
            w4 = dwp.reshape(8, 4, 8, 4, 32).transpose(1, 4, 2, 3, 0).reshape(128, 256)
            vecs[:, vb + 64:vb + 64 + 256] = w4
            w = np.asarray(inp["conv_w_in"][ci], f32)
            cols = []
            for p in range(8):
                cols.append(w[:, p * 128:(p + 1) * 128])
                cols.append(w[:, 1024 + p * 128:1024 + (p + 1) * 128])
            cols.append(w[:, 2048:3072])
            out["cwin%d" % ci] = np.ascontiguousarray(np.concatenate(cols, axis=1))
            out["cwout%d" % ci] = np.ascontiguousarray(np.asarray(inp["conv_w_out"][ci], f32))
            ci += 1
    out["vecs"] = vecs
    out["ident"] = np.eye(128, dtype=f32)
    out["i4"] = np.tile(np.eye(32, dtype=f32), (4, 1))
    tk = np.arange(128)[:, None]
    tq = np.arange(128)[None, :]
    out["maskT"] = np.where(tk <= tq, 0.0, NEG).astype(f32)
    return out


LAYERS = ["mla", "conv", "mla", "conv"]
_CACHE = {}


def kernel(**inputs):
    x = np.asarray(inputs["x"], np.float32)
    pos = np.asarray(inputs["positions"], np.int32)
    B = x.shape[0]
    shared = prep_shared(inputs, LAYERS)
    if "nc" not in _CACHE:
        _CACHE["nc"] = build_program(LAYERS, final_norm=True)
    nc = _CACHE["nc"]
    in_maps = []
    for b in range(B):
        m = dict(shared)
        m["xT"] = np.ascontiguousarray(x[b].T)
        m["pos"] = np.ascontiguousarray(pos[b][None, :])
        in_maps.append(m)
    res = run_bass_kernel_spmd(nc, in_maps, core_ids=list(range(B)))
    out = np.stack([np.ascontiguousarray(res.results[b]["yT"].T) for b in range(B)], axis=0)
    return out.astype(np.float32)
```

```python
import math
from contextlib import ExitStack

import numpy as np
import concourse.bass as bass
import concourse.mybir as mybir
from concourse.bass_utils import run_bass_kernel_spmd

F32 = mybir.dt.float32
BF16 = mybir.dt.bfloat16
I32 = mybir.dt.int32
AF = mybir.ActivationFunctionType
ALU = mybir.AluOpType

S = 2048
D = 1024
NT = 4
TW = 512
NH = 8
SCALE = 1.0 / math.sqrt(192.0)
RMS_EPS = 1e-6
LN_EPS = 1e-5
NEG = -30000.0
TWO_PI = 2.0 * math.pi
C1 = 6.28125
C2 = TWO_PI - C1
PI_LO = 3.1415925

V_FINAL = 0
V_INVF = 8
V_PHASE = 9
V_MLA = 10
MLA_VW = 13
V_CONV = V_MLA + 2 * MLA_VW
CONV_VW = 8 + 24 + 8 + 8 + 8 + 8 + 256
NV = V_CONV + 2 * CONV_VW


class T:
    __slots__ = ("name", "w", "r")

    def __init__(self, name):
        self.name = name
        self.w = None
        self.r = {}


class Sched:
    ENGS = ("pe", "act", "dve", "pool", "sp")

    def __init__(self, nc, st):
        self.nc, self.st = nc, st
        self.prog = {e: [] for e in self.ENGS}
        self.sem = {}
        self.cnt = {}
        self.seen = {e: {} for e in self.ENGS}
        self.uid = 0

    def getsem(self, key):
        if key not in self.sem:
            self.sem[key] = self.st.enter_context(self.nc.semaphore("s_" + key))
            self.cnt[key] = 0
        return self.sem[key]

    def _deps(self, eng, reads, writes, is_dma):
        deps = {}

        def add(d, keep_same):
            if d is None:
                return
            k, v = d
            if k == eng and not keep_same:
                return
            if deps.get(k, 0) < v:
                deps[k] = v

        for t in reads:
            add(t.w, is_dma or eng != "pe")
        for t in writes:
            add(t.w, is_dma)
            for k, v in t.r.items():
                add((k, v), is_dma)
        for k, v in deps.items():
            if self.seen[eng].get(k, 0) < v:
                self.prog[eng].append(("wait", k, v))
                self.seen[eng][k] = v

    def _mark(self, d, reads, writes):
        k, v = d
        for t in reads:
            if t.r.get(k, 0) < v:
                t.r[k] = v
        for t in writes:
            t.w = d
            t.r = {}

    def op(self, eng, fn, reads=(), writes=(), sig=True):
        self._deps(eng, reads, writes, False)
        self.getsem(eng)
        val = self.cnt[eng] + 1
        if sig:
            self.cnt[eng] = val
        self.prog[eng].append(("op", fn, eng if sig else None, 1))
        self._mark((eng, val), reads, writes)

    def dma(self, q, out_ap, in_ap, reads=(), writes=(), key=None):
        if key is None:
            key = "d%d" % self.uid
            self.uid += 1
        self._deps(q, reads, writes, True)
        self.getsem(key)
        self.cnt[key] += 16
        self.prog[q].append(("op", lambda e: e.dma_start(out=out_ap, in_=in_ap), key, 16))
        d = (key, self.cnt[key])
        self._mark(d, reads, writes)
        return d

    def wait_all(self, eng, key):
        v = self.cnt[key]
        if self.seen[eng].get(key, 0) < v:
            self.prog[eng].append(("wait", key, v))
            self.seen[eng][key] = v

    def emit(self):
        blk = self.st.enter_context(self.nc.Block())

        def body(name):
            def run(e):
                for it in self.prog[name]:
                    if it[0] == "wait":
                        e.wait_ge(self.sem[it[1]], it[2])
                    else:
                        ins = it[1](e)
                        if it[2] is not None:
                            ins.then_inc(self.sem[it[2]], it[3])
            return run

        blk.tensor(body("pe"))
        blk.scalar(body("act"))
        blk.vector(body("dve"))
        blk.gpsimd(body("pool"))
        blk.sync(body("sp"))


def handoff(old, new):
    u = {}
    for t in old:
        if t.w is not None:
            k, v = t.w
            if u.get(k, 0) < v:
                u[k] = v
        for k, v in t.r.items():
            if u.get(k, 0) < v:
                u[k] = v
    for t in new:
        t.w = None
        t.r = dict(u)


def build_program(layer_kinds, final_norm=True):
    nc = bass.Bass("TRN2", target_bir_lowering=False)
    n_layers = len(layer_kinds)
    dram = {}
    dram["xT"] = nc.dram_tensor("xT", [D, S], F32, kind="ExternalInput").ap()
    dram["pos"] = nc.dram_tensor("pos", [1, S], I32, kind="ExternalInput").ap()
    dram["vecs"] = nc.dram_tensor("vecs", [128, NV], F32, kind="ExternalInput").ap()
    dram["ident"] = nc.dram_tensor("ident", [128, 128], F32, kind="ExternalInput").ap()
    dram["maskT"] = nc.dram_tensor("maskT", [128, 128], F32, kind="ExternalInput").ap()
    dram["i4"] = nc.dram_tensor("i4", [128, 32], F32, kind="ExternalInput").ap()
    n_mla = sum(1 for k in layer_kinds if k == "mla")
    n_conv = n_layers - n_mla
    for j in range(n_mla):
        dram["mwin%d" % j] = nc.dram_tensor("mwin%d" % j, [D, 1792], F32, kind="ExternalInput").ap()
        dram["mwqb%d" % j] = nc.dram_tensor("mwqb%d" % j, [384, 2048], F32, kind="ExternalInput").ap()
        dram["mwkvb%d" % j] = nc.dram_tensor("mwkvb%d" % j, [256, 2048], F32, kind="ExternalInput").ap()
        dram["mwout%d" % j] = nc.dram_tensor("mwout%d" % j, [D, D], F32, kind="ExternalInput").ap()
    for j in range(n_conv):
        dram["cwin%d" % j] = nc.dram_tensor("cwin%d" % j, [D, 3072], F32, kind="ExternalInput").ap()
        dram["cwout%d" % j] = nc.dram_tensor("cwout%d" % j, [D, D], F32, kind="ExternalInput").ap()
    yT = nc.dram_tensor("yT", [D, S], F32, kind="ExternalOutput").ap()

    ud = nc.dram_tensor("ud", [max(n_conv, 1), D, 2080], BF16, kind="Internal").ap()

    st = ExitStack()
    with st:
        def sb(name, shape, dt):
            return st.enter_context(nc.sbuf_tensor(name, shape, dt))

        X = sb("X", [128, 8, S], F32)
        R1 = sb("R1", [128, 16384], BF16)
        R2 = sb("R2", [128, 16384], BF16)
        R3 = sb("R3", [128, 16640], BF16)
        WH = sb("WH", [128, 2, 1280], BF16)
        WR = sb("WR", [128, 2, 8, 512], BF16)
        NF, NB = 6, 6
        Fp = [sb("F%d" % i, [128, TW], F32) for i in range(NF)]
        Bp = [sb("B%d" % i, [128, TW], BF16) for i in range(NB)]
        i4 = sb("i4b", [128, 32], BF16)
        ident = sb("identb", [128, 128], BF16)
        maskT = sb("maskb", [128, 128], BF16)
        ones = sb("onesb", [128, 128], BF16)
        vecs = sb("vecs_sb", [128, NV], F32)
        epsr = sb("epsr", [128, 1], F32)
        epsl = sb("epsl", [128, 1], F32)
        PSb = [st.enter_context(nc.psum_tensor("ps%d" % i, [128, TW], F32)) for i in range(8)]

        sch = Sched(nc, st)

        Xt = [[T("X%d_%d" % (c, t)) for t in range(NT)] for c in range(8)]
        PSt = [T("ps%d" % i) for i in range(8)]
        Ft = [T("F%d" % i) for i in range(NF)]
        Bt = [T("B%d" % i) for i in range(NB)]
        WRt = [T("WR0"), T("WR1")]
        WHt = [T("WH0"), T("WH1")]
        constT = T("const")
        vecT = T("vecs")
        region = {"R1": [], "R2": [], "R3": []}

        def take(rname, tiles):
            handoff(region[rname], tiles)
            region[rname] = list(tiles)

        fi = [0]
        bi = [0]

        fpool = [list(range(NF))]

        def nextF():
            pool = fpool[0]
            i = pool[fi[0] % len(pool)]
            fi[0] += 1
            return Fp[i], Ft[i]

        def nextB():
            i = bi[0] % NB
            bi[0] += 1
            return Bp[i], Bt[i]

        def mm(out, lhsT, rhs, start, stop, reads, writes, sig):
            sch.op("pe", lambda e: e.matmul(out, lhsT=lhsT, rhs=rhs, start=start, stop=stop),
                   reads, writes, sig)

        def act(func, out, in_, reads, writes, bias=None, scale=None):
            kw = {}
            if bias is not None:
                kw["bias"] = bias
            if scale is not None:
                kw["scale"] = scale
            sch.op("act", lambda e: e.activation(out=out, in_=in_, func=func, **kw), reads, writes)

        def tt(eng, out, in0, in1, op, reads, writes):
            sch.op(eng, lambda e: e.tensor_tensor(out=out, in0=in0, in1=in1, op=op), reads, writes)

        def ts2(eng, out, in0, s1, s2, op0, op1, reads, writes):
            if s2 is None:
                sch.op(eng, lambda e: e.tensor_scalar(out=out, in0=in0, scalar1=s1, scalar2=None, op0=op0),
                       reads, writes)
            else:
                sch.op(eng, lambda e: e.tensor_scalar(out=out, in0=in0, scalar1=s1, scalar2=s2, op0=op0, op1=op1),
                       reads, writes)

        def stt(eng, out, in0, scalar, in1, op0, op1, reads, writes):
            sch.op(eng, lambda e: e.scalar_tensor_tensor(out=out, in0=in0, scalar=scalar, in1=in1, op0=op0, op1=op1),
                   reads, writes)

        def cp(eng, out, in_, reads, writes):
            sch.op(eng, lambda e: e.tensor_copy(out=out, in_=in_), reads, writes)

        def recip(out, in_, reads, writes):
            sch.op("dve", lambda e: e.reciprocal(out=out, in_=in_), reads, writes)

        def vcol(c):
            return vecs[:, c:c + 1]

        def act_rsqrt(out, in_, scale, eps_ap, reads, wt):
            act(AF.Ln, out, in_, list(reads) + [epsT], [wt], bias=eps_ap, scale=scale)
            act(AF.Exp, out, out, [wt], [wt], scale=-0.5)

        def act_recip(out, in_, reads, wt):
            act(AF.Ln, out, in_, list(reads), [wt])
            act(AF.Exp, out, out, [wt], [wt], scale=-1.0)

        witems = []
        mi = ci = 0
        for kind in layer_kinds:
            if kind == "mla":
                w = dram["mwin%d" % mi]
                witems += [(w[:, 0:512], 512), (w[:, 512:1024], 512), (w[:, 1024:1536], 512), (w[:, 1536:1792], 256)]
                w = dram["mwout%d" % mi]
                witems += [(w[:, 0:512], 512), (w[:, 512:1024], 512)]
                mi += 1
            else:
                w = dram["cwin%d" % ci]
                witems += [(w[:, i * 512:(i + 1) * 512], 512) for i in range(6)]
                w = dram["cwout%d" % ci]
                witems += [(w[:, 0:512], 512), (w[:, 512:1024], 512)]
                ci += 1
        wstate = {"loaded": 0, "acq": 0, "rel": 0}

        def w_fill():
            while wstate["loaded"] < min(len(witems), wstate["rel"] + 2):
                i = wstate["loaded"]
                ap, ncols = witems[i]
                slot = i % 2
                sch.dma("pool", WR[:, slot, :, 0:ncols], ap.rearrange("(c p) n -> p c n", p=128),
                        writes=[WRt[slot]], key="wr%d" % slot)
                wstate["loaded"] += 1

        def w_acquire():
            i = wstate["acq"]
            assert i < wstate["loaded"], "weight item not loaded"
            wstate["acq"] += 1
            return i % 2

        def w_release():
            wstate["rel"] += 1
            w_fill()

        sch.dma("sp", vecs[:], dram["vecs"][:, :], writes=[vecT])
        first_pos = [True]
        posi0 = R1[:, 0:4096].bitcast(I32)
        pos0T = T("posi")
        sch.dma("sp", posi0, dram["pos"].partition_broadcast(128), writes=[pos0T])
        for t in range(NT):
            sch.dma("sp", X[:, :, t * TW:(t + 1) * TW],
                    dram["xT"][:, t * TW:(t + 1) * TW].rearrange("(c p) n -> p c n", p=128),
                    writes=[Xt[c][t] for c in range(8)])
        cT = [T("c_ident"), T("c_mask"), T("c_i4")]
        sch.dma("pool", ident[:], dram["ident"][:, :], writes=[cT[0]], key="cst0")
        sch.dma("pool", maskT[:], dram["maskT"][:, :], writes=[cT[1]], key="cst1")
        sch.dma("pool", i4[:], dram["i4"][:, :], writes=[cT[2]], key="cst2")
        onesT = T("ones")
        sch.op("dve", lambda e: e.memset(ones[:], 1.0), writes=[onesT])
        epsT = T("eps")
        sch.op("dve", lambda e: e.memset(epsr[:], RMS_EPS), writes=[epsT])
        sch.op("dve", lambda e: e.memset(epsl[:], LN_EPS), writes=[epsT])
        w_fill()

        def rmsnorm(gcol, dst_ap_fn, dst_t_fn, tiles=None, bank_=None):
            for t in (range(NT) if tiles is None else tiles):
                tsl = slice(t * TW, (t + 1) * TW)
                bank = (6 + (t % 2)) if bank_ is None else bank_
                for c in range(8):
                    b_ap, b_t = nextB()
                    act(AF.Square, b_ap[:], X[:, c, tsl], [Xt[c][t]], [b_t])
                    mm(PSb[bank][:], ones[:], b_ap[:], c == 0, c == 7, [b_t, onesT], [PSt[bank]], True)
                f_ap, f_t = nextF()
                act_rsqrt(f_ap[:], PSb[bank][:], 1.0 / D, epsr[:], [PSt[bank]], f_t)
                for c in range(8):
                    stt("dve", dst_ap_fn(c, tsl), X[:, c, tsl], vcol(gcol + c), f_ap[:], ALU.mult, ALU.mult,
                        [Xt[c][t], f_t, vecT], [dst_t_fn(c, t)])

        rope_state = [None]

        def rope_chain(posi, t1, ki, kf, tmpT, ROPE, ropeT):
            cp("dve", t1, posi, [tmpT[0]], [tmpT[1]])
            ts2("dve", t1, t1, vcol(V_INVF), vcol(V_PHASE), ALU.mult, ALU.add, [tmpT[1], vecT], [tmpT[1]])
            ts2("dve", kf, t1, 1.0 / TWO_PI, None, ALU.mult, None, [tmpT[1]], [tmpT[3]])
            cp("dve", ki, kf, [tmpT[3]], [tmpT[2]])
            cp("dve", kf, ki, [tmpT[2]], [tmpT[3]])
            stt("dve", t1, kf, -C1, t1, ALU.mult, ALU.add, [tmpT[3], tmpT[1]], [tmpT[1]])
            stt("dve", t1, kf, -C2, t1, ALU.mult, ALU.add, [tmpT[3], tmpT[1]], [tmpT[1]])
            ts2("dve", t1, t1, PI_LO, -PI_LO, ALU.min, ALU.max, [tmpT[1]], [tmpT[1]])
            act(AF.Sin, ROPE, t1, [tmpT[1]], [ropeT])

        H = R1[:, :].rearrange("p (c n) -> p c n", c=8)
        G = R2[:, :].rearrange("p (c n) -> p c n", c=8)

        def mla_layer(j, conv_next):
            vb = V_MLA + j * MLA_VW
            QN = R3[:, 0:6144].rearrange("p (c n) -> p c n", c=3)
            KVN = R3[:, 6144:10240].rearrange("p (c n) -> p c n", c=2)
            KPE = R3[:, 10240:12288]
            ROPE = R3[:, 12288:16384].bitcast(F32)
            LAT = [QN[:, 0], QN[:, 1], QN[:, 2], KVN[:, 0], KVN[:, 1]]
            wq_d = dram["mwqb%d" % j]
            wkv_d = dram["mwkvb%d" % j]

            def load_head_w(h):
                s_ = h % 2
                sch.dma("pool", WH[:, s_, 0:768].rearrange("p (c n) -> p c n", c=3),
                        wq_d[:, h * 256:(h + 1) * 256].rearrange("(c p) n -> p c n", p=128),
                        writes=[WHt[s_]], key="wh%d" % s_)
                sch.dma("pool", WH[:, s_, 768:1280].rearrange("p (c n) -> p c n", c=2),
                        wkv_d[:, h * 256:(h + 1) * 256].rearrange("(c p) n -> p c n", p=128),
                        writes=[WHt[s_]], key="wh%d" % s_)

            load_head_w(0)
            load_head_w(1)
            posi = R1[:, 0:4096].bitcast(I32)
            t1 = R1[:, 4096:8192].bitcast(F32)
            ki = R1[:, 8192:12288].bitcast(I32)
            kf = R1[:, 12288:16384].bitcast(F32)
            tmpT = [T("posi"), T("t1"), T("ki"), T("kf")]
            if first_pos[0]:
                tmpT[0] = pos0T
                region["R1"] = [pos0T]
            take("R1", tmpT[1:] if first_pos[0] else tmpT)
            if first_pos[0]:
                region["R1"] = list(tmpT)
            ropeT = T("rope")
            LATt = [[T("lat%d_%d" % (f, t)) for t in range(NT)] for f in range(5)]
            KPEt = [T("kpe%d" % t) for t in range(NT)]
            if rope_state[0] is None:
                take("R3", [ropeT] + [x for r in LATt for x in r] + KPEt)
            else:
                ropeT = rope_state[0]
                take("R3", [x for r in LATt for x in r] + KPEt)
            first_pos[0] = False
            if rope_state[0] is None:
                rope_chain(posi, t1, ki, kf, tmpT, ROPE, ropeT)
                rope_state[0] = ropeT

            Ht = [[T("h%d_%d" % (c, t)) for t in range(NT)] for c in range(8)]
            take("R1", [x for r in Ht for x in r])
            rmsnorm(vb, lambda c, tsl: H[:, c, tsl], lambda c, t: Ht[c][t])

            RAWv = R2[:, 0:10240].bitcast(F32)
            RAWt = [[T("raw%d_%d" % (s_, f)) for f in range(5)] for s_ in range(2)]
            take("R2", [x for r in RAWt for x in r])
            s0 = w_acquire()
            s1 = w_acquire()
            wsl = [(s0, 0), (s0, 128), (s0, 256), (s0, 384), (s1, 0), (s1, 128)]
            pcnt = [0]

            def proj_bank():
                b = pcnt[0] % 4
                pcnt[0] += 1
                return b

            def rope_apply(ps_ap, ps_t, out_ap, out_t, tsl):
                a_ap, a_t = nextF()
                tt("dve", a_ap[:], ps_ap, ROPE[:, tsl], ALU.mult, [ps_t, ropeT], [a_t])
                b_ap, b_t = nextF()
                cp("dve", b_ap[0:64, :], a_ap[64:128, :], [a_t], [b_t])
                tt("dve", out_ap, a_ap[0:64, :], b_ap[0:64, :], ALU.add, [a_t, b_t], [out_t])

            for t in range(NT):
                tsl = slice(t * TW, (t + 1) * TW)
                rs = t % 2
                for f in range(6):
                    slot, off = wsl[f]
                    b = proj_bank()
                    for kc in range(8):
                        mm(PSb[b][:], WR[:, slot, kc, off:off + 128], H[:, kc, tsl], kc == 0, kc == 7,
                           [WRt[slot], Ht[kc][t]], [PSt[b]], kc == 7)
                    if f < 5:
                        raw = RAWv[:, (rs * 5 + f) * TW:(rs * 5 + f + 1) * TW]
                        act(AF.Copy, raw, PSb[b][:], [PSt[b]], [RAWt[rs][f]])
                        q_ap, q_t = nextB()
                        act(AF.Square, q_ap[:], PSb[b][:], [PSt[b]], [q_t])
                        sbank = 6 if f < 3 else 7
                        first = f in (0, 3)
                        last = f in (2, 4)
                        mm(PSb[sbank][:], ones[:], q_ap[:], first, last, [q_t, onesT], [PSt[sbank]], True)
                    else:
                        rope_apply(PSb[b][:], PSt[b], KPE[0:64, tsl], KPEt[t], tsl)
                rq_ap, rq_t = nextF()
                act_rsqrt(rq_ap[:], PSb[6][:], 1.0 / 384, epsr[:], [PSt[6]], rq_t)
                rk_ap, rk_t = nextF()
                act_rsqrt(rk_ap[:], PSb[7][:], 1.0 / 256, epsr[:], [PSt[7]], rk_t)
                for f in range(5):
                    raw = RAWv[:, (rs * 5 + f) * TW:(rs * 5 + f + 1) * TW]
                    r_ap, r_t = (rq_ap, rq_t) if f < 3 else (rk_ap, rk_t)
                    stt("dve", LAT[f][:, tsl], raw, vcol(vb + 8 + f), r_ap[:], ALU.mult, ALU.mult,
                        [RAWt[rs][f], r_t, vecT], [LATt[f][t]])

            Gt = [[T("g%d_%d" % (c, t)) for t in range(NT)] for c in range(8)]
            take("R2", [x for r in Gt for x in r])
            cur = s1
            held = [s0, s1]
            for gi in range(8):
                if gi == 0:
                    slot, off = s1, 256
                elif gi == 1:
                    slot, off = s1, 384
                elif gi < 6:
                    if gi == 2:
                        w_release()
                        s2 = w_acquire()
                    slot, off = s2, (gi - 2) * 128
                else:
                    if gi == 6:
                        w_release()
                        s3 = w_acquire()
                    slot, off = s3, (gi - 6) * 128
                for t in range(NT):
                    tsl = slice(t * TW, (t + 1) * TW)
                    b = proj_bank()
                    for kc in range(8):
                        mm(PSb[b][:], WR[:, slot, kc, off:off + 128], H[:, kc, tsl], kc == 0, kc == 7,
                           [WRt[slot], Ht[kc][t]], [PSt[b]], kc == 7)
                    act(AF.Silu, G[:, gi, tsl], PSb[b][:], [PSt[b]], [Gt[gi][t]])
                if gi == 5:
                    w_release()
            w_release()

            HB = []
            for s_ in range(2):
                base = s_ * 8192
                HB.append(dict(
                    qn=R1[:, base:base + 2048], qp=R1[:, base + 2048:base + 4096],
                    kh=R1[:, base + 4096:base + 6144],
                    vh=R1[:, base + 6144:base + 8192].rearrange("p (b n) -> p b n", n=128),
                    qnt=[T("qn%d_%d" % (s_, t)) for t in range(NT)],
                    qpt=[T("qp%d_%d" % (s_, t)) for t in range(NT)],
                    kht=[T("kh%d_%d" % (s_, t)) for t in range(NT)],
                    vht=[T("vh%d_%d" % (s_, t)) for t in range(NT)],
                ))
            take("R1", [x for hb in HB for key in ("qnt", "qpt", "kht", "vht") for x in hb[key]])
            jb = [0]

            def jit_bank():
                return 7


            def proj_pieces(h):
                s_ = h % 2
                hb = HB[s_]
                wq = WH[:, s_, 0:768].rearrange("p (c n) -> p c n", c=3)
                wkv = WH[:, s_, 768:1280].rearrange("p (c n) -> p c n", c=2)
                pieces = []
                for t in range(NT):
                    tsl = slice(t * TW, (t + 1) * TW)

                    def g_qn(t=t, tsl=tsl):
                        b = jit_bank()
                        for kc in range(3):
                            mm(PSb[b][:], wq[:, kc, 0:128], QN[:, kc, tsl], kc == 0, kc == 2,
                               [WHt[s_], LATt[kc][t]], [PSt[b]], kc == 2)
                        cp("dve", hb["qn"][:, tsl], PSb[b][:], [PSt[b]], [hb["qnt"][t]])

                    def g_qp(t=t, tsl=tsl):
                        b = jit_bank()
                        for kc in range(3):
                            mm(PSb[b][:], wq[:, kc, 128:256], QN[:, kc, tsl], kc == 0, kc == 2,
                               [WHt[s_], LATt[kc][t]], [PSt[b]], kc == 2)
                        rope_apply(PSb[b][:], PSt[b], hb["qp"][0:64, tsl], hb["qpt"][t], tsl)

                    def g_k(t=t, tsl=tsl):
                        b = jit_bank()
                        for kc in range(2):
                            mm(PSb[b][:], wkv[:, kc, 0:128], KVN[:, kc, tsl], kc == 0, kc == 1,
                               [WHt[s_], LATt[3 + kc][t]], [PSt[b]], kc == 1)
                        cp("dve", hb["kh"][:, tsl], PSb[b][:], [PSt[b]], [hb["kht"][t]])

                    def g_v(t=t, tsl=tsl):
                        b = jit_bank()
                        for bb in range(4):
                            tb = 4 * t + bb
                            for kc in range(2):
                                mm(PSb[b][:, bb * 128:(bb + 1) * 128], KVN[:, kc, tb * 128:(tb + 1) * 128],
                                   wkv[:, kc, 128:256], kc == 0, kc == 1,
                                   [WHt[s_], LATt[3 + kc][t]], [PSt[b]], (bb == 3 and kc == 1))
                        cp("dve", hb["vh"][:, 4 * t:4 * t + 4, :],
                           PSb[b][:].rearrange("p (b n) -> p b n", n=128), [PSt[b]], [hb["vht"][t]])
                    pieces.append([g_qn, g_k, g_qp, g_v])
                return pieces

            def attention(h, extra, groups):
                s_ = h % 2
                hb = HB[s_]
                n_groups = max(len(groups), 1)
                blk_done = [0]
                for i in range(NT):
                    ob, sbk = 3 + 2 * (i % 2), 4 + 2 * (i % 2)
                    nblk = 4 * i + 4
                    tsl = slice(i * TW, (i + 1) * TW)

                    def c0_of(jk):
                        return max(jk - 4 * i, 0) * 128

                    def qk(jk):
                        bank = jk % 3
                        c0 = c0_of(jk)
                        diag = jk >= 4 * i
                        tq0 = i * TW + c0
                        tk = slice(jk * 128, (jk + 1) * 128)
                        mm(PSb[bank][:, c0:TW], hb["kh"][:, tk], hb["qn"][:, tq0:(i + 1) * TW], True, False,
                           [hb["kht"][jk // 4], hb["qnt"][i]], [PSt[bank]], False)
                        mm(PSb[bank][:, c0:TW], KPE[0:64, tk], hb["qp"][0:64, tq0:(i + 1) * TW], False, not diag,
                           [KPEt[jk // 4], hb["qpt"][i]], [PSt[bank]], not diag)
                        if diag:
                            mm(PSb[bank][:, c0:c0 + 128], ident[:], maskT[:], False, True,
                               cT, [PSt[bank]], True)

                    def pv(jk):
                        bank = jk % 3
                        c0 = c0_of(jk)
                        p_ap, p_t = nextB()
                        act(AF.Exp, p_ap[:, c0:TW], PSb[bank][:, c0:TW], [PSt[bank]], [p_t], scale=SCALE)
                        mm(PSb[ob][:, c0:TW], hb["vh"][:, jk, :], p_ap[:, c0:TW], jk == 0, jk == nblk - 1,
                           [hb["vht"][jk // 4], p_t], [PSt[ob]], False)
                        mm(PSb[sbk][:, c0:TW], ones[:], p_ap[:, c0:TW], jk == 0, jk == nblk - 1,
                           [onesT, p_t], [PSt[sbk]], True)

                    qk(0)
                    if nblk > 1:
                        qk(1)
                    for jk in range(nblk):
                        if jk + 2 < nblk:
                            qk(jk + 2)
                        pv(jk)
                        blk_done[0] += 1
                        while groups and blk_done[0] * n_groups >= (n_groups - len(groups) + 1) * 40:
                            groups.pop(0)()
                    r_ap, r_t = nextF()
                    act_recip(r_ap[:], PSb[sbk][:], [PSt[sbk]], r_t)
                    u_ap, u_t = nextF()
                    tt("dve", u_ap[:], PSb[ob][:], r_ap[:], ALU.mult, [PSt[ob], r_t], [u_t])
                    tt("dve", G[:, h, tsl], u_ap[:], G[:, h, tsl], ALU.mult, [u_t, Gt[h][i]], [Gt[h][i]])
                    for fn in extra[i]:
                        fn()
                while groups:
                    groups.pop(0)()

            for p in proj_pieces(0):
                for g_ in p:
                    g_()
            for h in range(NH):
                extra = [[] for _ in range(NT)]
                groups = []
                if h + 1 < NH:
                    groups = [g_ for p in proj_pieces(h + 1) for g_ in p]
                attention(h, extra, groups)
                if h + 2 < NH:
                    load_head_w(h + 2)

            for half in range(2):
                slot = w_acquire()
                for mm_ in range(4):
                    m = half * 4 + mm_
                    for t in range(NT):
                        tsl = slice(t * TW, (t + 1) * TW)
                        b = proj_bank()
                        for kc in range(8):
                            mm(PSb[b][:], WR[:, slot, kc, mm_ * 128:(mm_ + 1) * 128], G[:, kc, tsl], kc == 0, kc == 7,
                               [WRt[slot], Gt[kc][t]], [PSt[b]], kc == 7)
                        tt("dve", X[:, m, tsl], PSb[b][:], X[:, m, tsl], ALU.add, [PSt[b], Xt[m][t]], [Xt[m][t]])
                w_release()

        def conv_layer(j, after_tile=None):
            vb = V_CONV + j * CONV_VW
            c_bin, c_dwb, c_lng, c_lnb, c_bout = vb + 8, vb + 32, vb + 40, vb + 48, vb + 56
            Ht = [[T("h%d_%d" % (c, t)) for t in range(NT)] for c in range(8)]
            take("R1", [x for r in Ht for x in r])
            rmsnorm(vb, lambda c, tsl: H[:, c, tsl], lambda c, t: Ht[c][t])
            NUR = 3
            Ur = [R3[:, r_ * 2080:(r_ + 1) * 2080] for r_ in range(NUR)]
            Upad = [T("upad%d" % r_) for r_ in range(NUR)]
            Urt = [[T("ur%d_%d" % (r_, t)) for t in range(NT)] for r_ in range(NUR)]
            take("R3", Upad + [x for r in Urt for x in r])
            for r_ in range(NUR):
                sch.op("dve", lambda e, r_=r_: e.memset(Ur[r_][:, 0:32], 0.0), writes=[Upad[r_]])
            Gt = [[T("g%d_%d" % (c, t)) for t in range(NT)] for c in range(8)]
            take("R2", [x for r in Gt for x in r])
            udT = [T("ud%d" % p) for p in range(8)]
            pc = [0]
            for p in range(8):
                if p % 2 == 0:
                    slot = w_acquire()
                off = (p % 2) * 256
                for t in range(NT):
                    tsl = slice(t * TW, (t + 1) * TW)
                    ba = (2 * pc[0]) % 4
                    bb_ = ba + 1
                    pc[0] += 1
                    for kc in range(8):
                        mm(PSb[ba][:], WR[:, slot, kc, off:off + 128], H[:, kc, tsl], kc == 0, kc == 7,
                           [WRt[slot], Ht[kc][t]], [PSt[ba]], kc == 7)
                    for kc in range(8):
                        mm(PSb[bb_][:], WR[:, slot, kc, off + 128:off + 256], H[:, kc, tsl], kc == 0, kc == 7,
                           [WRt[slot], Ht[kc][t]], [PSt[bb_]], kc == 7)
                    s_ap, s_t = nextF()
                    act(AF.Sigmoid, s_ap[:], PSb[bb_][:], [PSt[bb_], vecT], [s_t], bias=vcol(c_bin + 2 * p + 1))
                    stt("dve", Ur[p % NUR][:, 32 + t * TW:32 + (t + 1) * TW], PSb[ba][:], vcol(c_bin + 2 * p), s_ap[:],
                        ALU.add, ALU.mult, [PSt[ba], s_t, vecT], [Urt[p % NUR][t]])
                sch.dma("sp", ud[j, p * 128:(p + 1) * 128, :], Ur[p % NUR][:, :],
                        reads=[Upad[p % NUR]] + Urt[p % NUR], writes=[udT[p]], key="ud%d" % p)
                if p % 2 == 1:
                    w_release()
            NUB = 5
            UB = [R3[:, s_ * 2176:(s_ + 1) * 2176] for s_ in range(NUB)]
            UBt = [[T("ub%d_%d" % (s_, jj)) for jj in range(4)] for s_ in range(NUB)]
            take("R3", [x for r in UBt for x in r])
            for s_ in range(NUB):
                sch.op("dve" if s_ % 2 == 0 else "pool", lambda e, s_=s_: e.memset(UB[s_][:, 0:2176], 0.0),
                       writes=UBt[s_])

            def ub_fetch(n):
                t, p = divmod(n, 8)
                s_ = n % NUB
                for jj in range(4):
                    wdt = 540 if jj < 3 else 539
                    c0 = t * TW + 2 + jj
                    sch.dma("sp",
                            UB[s_][32 * jj:32 * jj + 32, 0:2176].rearrange("p (g x) -> p g x", g=4)[:, :, 0:wdt],
                            ud[j, p * 128:(p + 1) * 128, c0:c0 + wdt].rearrange("(g c) x -> c g x", g=4),
                            reads=[udT[p]], writes=[UBt[s_][jj]], key="ub%d_%d" % (s_, jj))

            for n in range(NUB):
                ub_fetch(n)
            gc = [0]
            for gi in range(8):
                if gi % 4 == 0:
                    slot = w_acquire()
                off = (gi % 4) * 128
                for t in range(NT):
                    tsl = slice(t * TW, (t + 1) * TW)
                    b = 4 + (gc[0] % 2)
                    gc[0] += 1
                    for kc in range(8):
                        mm(PSb[b][:], WR[:, slot, kc, off:off + 128], H[:, kc, tsl], kc == 0, kc == 7,
                           [WRt[slot], Ht[kc][t]], [PSt[b]], kc == 7)
                    act(AF.Silu, G[:, gi, tsl], PSb[b][:], [PSt[b], vecT], [Gt[gi][t]], bias=vcol(c_bin + 16 + gi))
                if gi % 4 == 3:
                    w_release()
            Cv = R1[:, 0:8192].bitcast(F32)
            L = R1[:, 8192:16384].rearrange("p (g m c) -> p g m c", g=32, m=8)
            Ct = [T("c%d" % p) for p in range(8)]
            Lt = [T("L%d" % p) for p in range(8)]
            take("R1", Ct + Lt)
            w4c = vb + 64
            for p in range(8):
                sch.op("dve" if p % 2 == 0 else "pool",
                       lambda e, p=p: e.tensor_tensor(
                           out=L[:, 4 * p:4 * p + 4, :, :].rearrange("p g m c -> p (g m) c"),
                           in0=i4[:].unsqueeze(1).to_broadcast([128, 32, 32]),
                           in1=vecs[:, w4c + 32 * p:w4c + 32 * p + 32].unsqueeze(2).to_broadcast([128, 32, 32]),
                           op=ALU.mult),
                       cT + [vecT], [Lt[p]])
            ws0 = w_acquire()
            ws1 = w_acquire()
            wso = [ws0, ws1]
            oc = [0]
            yc = [0]
            mu_ap, mu_t = Fp[0], Ft[0]
            m2_ap, m2_t = Fp[1], Ft[1]
            pend = {}

            def conv_mm(n):
                t, p = divmod(n, 8)
                s_ = n % NUB
                cb = n % 2
                for m in range(8):
                    for g_ in range(4):
                        last = (m == 7 and g_ == 3)
                        out_ap = PSb[cb][32 * g_:32 * g_ + 32, :]
                        lhs = L[:, 4 * p + g_, m, :]
                        rhs = UB[s_][:, g_ * 544 + 4 * m:g_ * 544 + 4 * m + TW]
                        sch.op("pe", lambda e, out_ap=out_ap, lhs=lhs, rhs=rhs, m=m, g_=g_: e.matmul(
                            out_ap, lhsT=lhs, rhs=rhs, start=(m == 0), stop=(m == 7), tile_position=(0, 32 * g_)),
                            [Lt[p]] + UBt[s_], [PSt[cb]], last)
                if n + NUB < NT * 8:
                    ub_fetch(n + NUB)

            def evac(n):
                t, p = divmod(n, 8)
                cb = n % 2
                Cp = Cv[:, p * TW:(p + 1) * TW]
                act(AF.Identity, Cp, PSb[cb][:], [PSt[cb], vecT], [Ct[p]], bias=vcol(c_dwb + p))
                q_ap, q_t = nextB()
                act(AF.Square, q_ap[:], PSb[cb][:], [PSt[cb], vecT], [q_t], bias=vcol(c_dwb + p))
                l_ap, l_t = nextB()
                cp("dve", l_ap[:], Cp, [Ct[p]], [l_t])
                pend[n] = (l_ap, l_t, q_ap, q_t)

            def stats_mm(n):
                t, p = divmod(n, 8)
                l_ap, l_t, q_ap, q_t = pend.pop(n)
                mm(PSb[6][:], ones[:], l_ap[:], p == 0, p == 7, [onesT, l_t], [PSt[6]], True)
                mm(PSb[7][:], ones[:], q_ap[:], p == 0, p == 7, [onesT, q_t], [PSt[7]], True)

            def ln_stats():
                ts2("dve", mu_ap[:], PSb[6][:], 1.0 / D, None, ALU.mult, None, [PSt[6]], [mu_t])
                tt("dve", m2_ap[:], mu_ap[:], mu_ap[:], ALU.mult, [mu_t], [m2_t])
                stt("dve", m2_ap[:], PSb[7][:], 1.0 / D, m2_ap[:], ALU.mult, ALU.subtract, [PSt[7], m2_t], [m2_t])
                act_rsqrt(m2_ap[:], m2_ap[:], 1.0, epsl[:], [m2_t], m2_t)

            ypend = {}

            def dve_norm(n):
                t, p = divmod(n, 8)
                Cp = Cv[:, p * TW:(p + 1) * TW]
                y_ap, y_t = Fp[2 + yc[0] % 3], Ft[2 + yc[0] % 3]
                yc[0] += 1
                tt("dve", y_ap[:], Cp, mu_ap[:], ALU.subtract, [Ct[p], mu_t], [y_t])
                tt("dve", y_ap[:], y_ap[:], m2_ap[:], ALU.mult, [y_t, m2_t], [y_t])
                ypend[n] = (y_ap, y_t)

            def act_fin(n):
                if n not in ypend:
                    return
                t, p = divmod(n, 8)
                tsl = slice(t * TW, (t + 1) * TW)
                y_ap, y_t = ypend.pop(n)
                act(AF.Silu, y_ap[:], y_ap[:], [y_t, vecT], [y_t], bias=vcol(c_lnb + p), scale=vcol(c_lng + p))
                tt("dve", G[:, p, tsl], y_ap[:], G[:, p, tsl], ALU.mult, [y_t, Gt[p][t]], [Gt[p][t]])

            def outproj(t):
                tsl = slice(t * TW, (t + 1) * TW)
                for m in range(8):
                    b = 2 + (oc[0] % 4)
                    oc[0] += 1
                    slot = wso[m // 4]
                    for kc in range(8):
                        mm(PSb[b][:], WR[:, slot, kc, (m % 4) * 128:(m % 4 + 1) * 128], G[:, kc, tsl], kc == 0, kc == 7,
                           [WRt[slot], Gt[kc][t]], [PSt[b]], kc == 7)
                    stt("dve", X[:, m, tsl], PSb[b][:], vcol(c_bout + m), X[:, m, tsl], ALU.add, ALU.add,
                        [PSt[b], Xt[m][t], vecT], [Xt[m][t]])

            NCH = NT * 8
            fpool[0] = [5]
            for n in range(NCH):
                t, p = divmod(n, 8)
                conv_mm(n)
                if n >= 1:
                    stats_mm(n - 1)
                if p == 0 and t > 0:
                    ln_stats()
                if n >= 9:
                    act_fin(n - 9)
                if n >= 8:
                    dve_norm(n - 8)
                if p == 7 and t > 0:
                    act_fin(n - 8)
                    outproj(t - 1)
                    if after_tile is not None:
                        after_tile(t - 1, 2 + (oc[0] % 4))
                        oc[0] += 1
                evac(n)
            stats_mm(NCH - 1)
            ln_stats()
            for n in range(NCH - 8, NCH):
                dve_norm(n)
                if n > NCH - 8:
                    act_fin(n - 1)
            act_fin(NCH - 1)
            outproj(NT - 1)
            if after_tile is not None:
                after_tile(NT - 1, 2 + (oc[0] % 4))
            fpool[0] = list(range(NF))
            w_release()
            w_release()

        def finish_tile(t, bank_=None):
            if final_norm:
                rmsnorm(V_FINAL, lambda c, tsl: X[:, c, tsl], lambda c, t_: Xt[c][t_], tiles=[t], bank_=bank_)
            sch.dma("pool", yT[:, t * TW:(t + 1) * TW].rearrange("(c p) n -> p c n", p=128),
                    X[:, :, t * TW:(t + 1) * TW], reads=[Xt[c][t] for c in range(8)], key="out")

        mi = ci = 0
        for li, kind in enumerate(layer_kinds):
            if kind == "mla":
                conv_next = ci if (li + 1 < n_layers and layer_kinds[li + 1] == "conv") else None
                mla_layer(mi, conv_next)
                mi += 1
            else:
                conv_layer(ci, finish_tile if li == n_layers - 1 else None)
                ci += 1

        if layer_kinds[-1] != "conv":
            for t in range(NT):
                finish_tile(t)
        sch.wait_all("pool", "out")
        sch.emit()
    return nc


def _col8(v):
    v = np.asarray(v, np.float32)
    n = v.shape[0] // 128
    return np.ascontiguousarray(v.reshape(n, 128).T)


def _perm_half(w):
    return np.concatenate([w[..., 32:64], w[..., 0:32]], axis=-1)


def prep_shared(inp, layer_kinds):
    f32 = np.float32
    vecs = np.zeros((128, NV), f32)
    vecs[:, V_FINAL:V_FINAL + 8] = _col8(inp["final_norm_g"])
    inv = (np.float32(10000.0) ** (-(np.arange(0, 64, 2, dtype=np.float32)) / np.float32(64))).astype(f32)
    vecs[:, V_INVF] = np.tile(inv, 4)
    ph = np.zeros(128, f32)
    ph[0:64] = np.float32(math.pi / 2)
    ph[64:96] = np.float32(math.pi)
    vecs[:, V_PHASE] = ph
    out = {}
    mi = ci = 0
    for kind in layer_kinds:
        if kind == "mla":
            vb = V_MLA + mi * MLA_VW
            vecs[:, vb:vb + 8] = _col8(inp["mla_norm_g"][mi])
            vecs[:, vb + 8:vb + 11] = _col8(inp["mla_q_norm_g"][mi])
            vecs[:, vb + 11:vb + 13] = _col8(inp["mla_kv_norm_g"][mi])
            w = np.asarray(inp["mla_w_in"][mi], f32)
            kpe = w[:, 640:704]
            out["mwin%d" % mi] = np.ascontiguousarray(
                np.concatenate([w[:, 0:640], kpe, _perm_half(kpe), w[:, 704:1728]], axis=1))
            wq = np.asarray(inp["mla_w_qb"][mi], f32).reshape(384, 8, 192)
            out["mwqb%d" % mi] = np.ascontiguousarray(
                np.concatenate([wq, _perm_half(wq[:, :, 128:192])], axis=2).reshape(384, 2048))
            out["mwkvb%d" % mi] = np.ascontiguousarray(np.asarray(inp["mla_w_kvb"][mi], f32))
            out["mwout%d" % mi] = np.ascontiguousarray(np.asarray(inp["mla_w_out"][mi], f32))
            mi += 1
        else:
            vb = V_CONV + ci * CONV_VW
            vecs[:, vb:vb + 8] = _col8(inp["conv_norm_g"][ci])
            bin_ = np.asarray(inp["conv_b_in"][ci], f32)
            ba, bb, bg = _col8(bin_[0:1024]), _col8(bin_[1024:2048]), _col8(bin_[2048:3072])
            for p in range(8):
                vecs[:, vb + 8 + 2 * p] = ba[:, p]
                vecs[:, vb + 8 + 2 * p + 1] = bb[:, p]
            vecs[:, vb + 24:vb + 32] = bg
            vecs[:, vb + 32:vb + 40] = _col8(inp["conv_dw_b"][ci])
            vecs[:, vb + 40:vb + 48] = _col8(inp["conv_ln_g"][ci])
            vecs[:, vb + 48:vb + 56] = _col8(inp["conv_ln_b"][ci])
            vecs[:, vb + 56:vb + 64] = _col8(inp["conv_b_out"][ci])
            dw = np.asarray(inp["conv_dw_w"][ci], f32)
            dwp = np.concatenate([dw, np.zeros((1, 1024), f32)], axis=0)
            w4 = dwp.reshape(8, 4, 8, 4, 32).transpose(1, 4, 2, 3, 0).reshape(128, 256)
            vecs[:, vb + 64:vb + 64 + 256] = w4
            w = np.asarray(inp["conv_w_in"][ci], f32)
            cols = []
            for p in range(8):
                cols.append(w[:, p * 128:(p + 1) * 128])
                cols.append(w[:, 1024 + p * 128:1024 + (p + 1) * 128])
            cols.append(w[:, 2048:3072])
            out["cwin%d" % ci] = np.ascontiguousarray(np.concatenate(cols, axis=1))
            out["cwout%d" % ci] = np.ascontiguousarray(np.asarray(inp["conv_w_out"][ci], f32))
            ci += 1
    out["vecs"] = vecs
    out["ident"] = np.eye(128, dtype=f32)
    out["i4"] = np.tile(np.eye(32, dtype=f32), (4, 1))
    tk = np.arange(128)[:, None]
    tq = np.arange(128)[None, :]
    out["maskT"] = np.where(tk <= tq, 0.0, NEG).astype(f32)
    return out


LAYERS = ["mla", "conv", "mla", "conv"]
_CACHE = {}


def kernel(**inputs):
    x = np.asarray(inputs["x"], np.float32)
    pos = np.asarray(inputs["positions"], np.int32)
    B = x.shape[0]
    shared = prep_shared(inputs, LAYERS)
    if "nc" not in _CACHE:
        _CACHE["nc"] = build_program(LAYERS, final_norm=True)
    nc = _CACHE["nc"]
    in_maps = []
    for b in range(B):
        m = dict(shared)
        m["xT"] = np.ascontiguousarray(x[b].T)
        m["pos"] = np.ascontiguousarray(pos[b][None, :])
        in_maps.append(m)
    res = run_bass_kernel_spmd(nc, in_maps, core_ids=list(range(B)))
    out = np.stack([np.ascontiguousarray(res.results[b]["yT"].T) for b in range(B)], axis=0)
    return out.astype(np.float32)
```
